# Optimizing a Trainium2 kernel written in Bass

```python
import jax, jax.numpy as jnp
from jax import lax
import numpy as np

D_MODEL = 2048
BATCH = 4
SEQ = 2048
DEPTH = 2
DEC_BATCH = 32
DEC_SEQ = 4
PAST_LEN = 8192
PAGE_SIZE = 128

H_A = 8
DK_A = 128
DV_A = 128
QK_A = H_A * DK_A
V_A = H_A * DV_A
CONV_K = 4
CONV_CH = 2 * QK_A + V_A
GDN_CHUNK = 64
DIL_GROUPS = ((128, 1), (512, 4), (2048, 16))
N_GROUPS = len(DIL_GROUPS)
H_G = 4
HD_B = 128
QKV_B = N_GROUPS * H_G * HD_B
OUT_B = H_G * HD_B
Q_BLOCK = 128
POOL_WINDOWS = (2, 4, 8, 16)
C_POOL = 1024
CG = C_POOL // len(POOL_WINDOWS)
POOL_HIST = max(POOL_WINDOWS) - 1
D_FF = ((8 * D_MODEL + 3 * 256 - 1) // (3 * 256)) * 256
EPS = 1e-6
L2_EPS = 1e-6
IN_SPLITS = (QK_A, QK_A, V_A, V_A, H_A, H_A, QKV_B, QKV_B, QKV_B, C_POOL, D_MODEL, D_MODEL, D_MODEL)
N_IN = sum(IN_SPLITS)

kernel_name = 'hybrid_gdn_dilated_pool_decoder_step'


def _rms_norm(x, gain):
    xf = x.astype(jnp.float32)
    y = xf * lax.rsqrt(jnp.mean(xf * xf, axis=-1, keepdims=True) + EPS)
    return (y * gain.astype(jnp.float32)).astype(x.dtype)


def _split(x, sizes):
    cuts = [int(c) for c in np.cumsum(sizes)[:-1]]
    return jnp.split(x, cuts, axis=-1)


def _l2norm(x):
    return x * lax.rsqrt(jnp.sum(x * x, axis=-1, keepdims=True) + L2_EPS)


def _causal_conv(hist, u, w):
    T = u.shape[1]
    xp = jnp.concatenate([hist.astype(u.dtype), u], axis=1)
    out = xp[:, 0:T] * w[0]
    for j in range(1, CONV_K):
        out = out + xp[:, j:j + T] * w[j]
    return out, xp[:, -(CONV_K - 1):]


def _gdn_chunked(q, k, v, g, beta, s0, chunk):
    Bn, T, H, DK = q.shape
    DV = v.shape[-1]
    N = T // chunk

    def blocks(t):
        return t.reshape(Bn, N, chunk, H, t.shape[-1]).transpose(1, 0, 3, 2, 4)

    qc, kc, vc = blocks(q), blocks(k), blocks(v)
    gc = jnp.cumsum(g.reshape(Bn, N, chunk, H).transpose(1, 0, 3, 2), axis=-1)
    bc = beta.reshape(Bn, N, chunk, H).transpose(1, 0, 3, 2)
    incl = jnp.tril(jnp.ones((chunk, chunk), dtype=bool))
    strict = jnp.tril(jnp.ones((chunk, chunk), dtype=bool), k=-1)
    diff = gc[..., :, None] - gc[..., None, :]
    decay = jnp.where(incl, jnp.exp(jnp.where(incl, diff, 0.0)), 0.0)
    kb = kc * bc[..., None]
    lower = jnp.where(strict, jnp.einsum('nbhid,nbhjd->nbhij', kb, kc) * decay, 0.0)
    eye = jnp.eye(chunk, dtype=jnp.float32)
    tinv = lax.linalg.triangular_solve(eye + lower, jnp.broadcast_to(eye, lower.shape), left_side=True, lower=True)
    u = tinv @ (vc * bc[..., None])
    w = tinv @ (kb * jnp.exp(gc)[..., None])
    a_intra = jnp.where(incl, jnp.einsum('nbhid,nbhjd->nbhij', qc, kc) * decay, 0.0)

    def step(s, xs):
        qi, ki, ui, wi, gi, ai = xs
        v_new = ui - wi @ s
        o = (qi * jnp.exp(gi)[..., None]) @ s + ai @ v_new
        g_last = gi[..., -1]
        s = s * jnp.exp(g_last)[..., None, None] + jnp.einsum('bhcd,bhce->bhde', ki * jnp.exp(g_last[..., None] - gi)[..., None], v_new)
        return s, o

    s_final, o = lax.scan(step, s0, (qc, kc, u, w, gc, a_intra))
    o = o.transpose(1, 0, 3, 2, 4).reshape(Bn, T, H, DV)
    return o, s_final


def _dilated_group(q, k, v, q_idx, dil, n_back):
    dist = jnp.arange(n_back + 1) * dil
    idx = q_idx[:, None] - dist[None, :]
    valid = idx >= 0
    idx = jnp.maximum(idx, 0)
    kg = k[:, idx].astype(jnp.float32)
    vg = v[:, idx].astype(jnp.float32)
    s = jnp.einsum('bqhd,bqjhd->bqhj', q.astype(jnp.float32), kg) * (HD_B ** -0.5)
    s = jnp.where(valid[None, :, None, :], s, -jnp.inf)
    m = jnp.max(s, axis=-1)
    p = jnp.exp(s - m[..., None])
    return jnp.einsum('bqhj,bqjhd->bqhd', p, vg), m, jnp.sum(p, axis=-1)


def _dilated_mixture(qs, ks, vs, q_idxs):
    parts = [_dilated_group(q, k, v, qi, dil, win // dil)
             for q, k, v, qi, (win, dil) in zip(qs, ks, vs, q_idxs, DIL_GROUPS)]
    m_all = jnp.stack([pt[1] for pt in parts])
    e = jnp.exp(m_all - jnp.max(m_all, axis=0))
    num = sum(e[i][..., None] * parts[i][0] for i in range(N_GROUPS))
    den = sum(e[i] * parts[i][2] for i in range(N_GROUPS))
    return num / den[..., None]


def _dilated_prompt(q, k, v):
    Bn, S = q.shape[:2]
    nb = S // Q_BLOCK
    qb = q.reshape(Bn, nb, Q_BLOCK, N_GROUPS, H_G, HD_B).transpose(1, 0, 2, 3, 4, 5)
    k_g = [k[:, :, gi] for gi in range(N_GROUPS)]
    v_g = [v[:, :, gi] for gi in range(N_GROUPS)]

    def blk(args):
        i, qi = args
        q_idx = i * Q_BLOCK + jnp.arange(Q_BLOCK)
        return _dilated_mixture([qi[:, :, gi] for gi in range(N_GROUPS)], k_g, v_g, [q_idx] * N_GROUPS)

    o = lax.map(blk, (jnp.arange(nb), qb))
    return o.transpose(1, 0, 2, 3, 4).reshape(Bn, S, H_G, HD_B)


def _multiscale_pool(hist, u, pos):
    T = u.shape[1]
    xp = jnp.concatenate([hist.astype(u.dtype), u], axis=1)
    xf = xp.astype(jnp.float32)
    csum = jnp.concatenate([jnp.zeros_like(xf[:, :1]), jnp.cumsum(xf, axis=1)], axis=1)
    end = csum[:, POOL_HIST + 1:POOL_HIST + 1 + T]
    means = []
    for gi, win in enumerate(POOL_WINDOWS):
        ch = slice(gi * CG, (gi + 1) * CG)
        start = csum[:, POOL_HIST + 1 - win:POOL_HIST + 1 - win + T, ch]
        cnt = jnp.minimum(win, pos + 1).astype(jnp.float32)[None, :, None]
        means.append((end[..., ch] - start) / cnt)
    mixed = jnp.concatenate(means, axis=-1) - xf[:, POOL_HIST:]
    return mixed.astype(u.dtype), xp[:, -POOL_HIST:]


def _layer(x, lp, hist, prompt):
    (w_in, conv_w, a_log, dt_bias, gdn_gain, w_pool, pool_scale,
     w_br_a, w_br_b, w_br_c, w_out, w_gu, w_down,
     g_pre_mix, g_post_mix, g_pre_ffn, g_post_ffn) = lp
    conv_hist, s0, pool_hist, win_hists = hist
    Bn, T, _ = x.shape
    dt = x.dtype
    h = _rms_norm(x, g_pre_mix)
    (qa, ka, va, za, ba, aa, qb, kb, vb, uc, ga, gb, gc) = _split(h @ w_in, IN_SPLITS)

    qkv, conv_new = _causal_conv(conv_hist, jnp.concatenate([qa, ka, va], axis=-1), conv_w)
    qkv = jax.nn.silu(qkv).astype(jnp.float32)
    q, k, v = _split(qkv, (QK_A, QK_A, V_A))
    q = _l2norm(q.reshape(Bn, T, H_A, DK_A)) * (DK_A ** -0.5)
    k = _l2norm(k.reshape(Bn, T, H_A, DK_A))
    v = v.reshape(Bn, T, H_A, DV_A)
    beta = jax.nn.sigmoid(ba.astype(jnp.float32))
    g = -jnp.exp(a_log.astype(jnp.float32)) * jax.nn.softplus(aa.astype(jnp.float32) + dt_bias.astype(jnp.float32))
    chunk = GDN_CHUNK if prompt else T
    o, s_new = _gdn_chunked(q, k, v, g, beta, s0.astype(jnp.float32), chunk)
    o = o * lax.rsqrt(jnp.mean(o * o, axis=-1, keepdims=True) + EPS) * gdn_gain.astype(jnp.float32)
    o = o * jax.nn.silu(za.astype(jnp.float32).reshape(Bn, T, H_A, DV_A))
    out_a = o.reshape(Bn, T, V_A).astype(dt)

    qb = qb.reshape(Bn, T, N_GROUPS, H_G, HD_B)
    kb = kb.reshape(Bn, T, N_GROUPS, H_G, HD_B)
    vb = vb.reshape(Bn, T, N_GROUPS, H_G, HD_B)
    kv_new = [jnp.stack([kb[:, :, gi], vb[:, :, gi]], axis=2) for gi in range(N_GROUPS)]
    if prompt:
        ob = _dilated_prompt(qb, kb, vb)
        kv_all = kv_new
    else:
        kv_all = [jnp.concatenate([win_hists[gi].astype(dt), kv_new[gi]], axis=1) for gi in range(N_GROUPS)]
        q_idxs = [kv.shape[1] - T + jnp.arange(T) for kv in kv_all]
        ob = _dilated_mixture([qb[:, :, gi] for gi in range(N_GROUPS)],
                              [kv[:, :, 0] for kv in kv_all], [kv[:, :, 1] for kv in kv_all], q_idxs)
    win_new = [kv_all[gi][:, -win:] for gi, (win, _) in enumerate(DIL_GROUPS)]
    out_b = ob.reshape(Bn, T, OUT_B).astype(dt)

    pos = jnp.arange(T) + (0 if prompt else PAST_LEN)
    pooled, pool_new = _multiscale_pool(pool_hist, uc, pos)
    oc = jnp.einsum('btgc,gcd->btgd', pooled.reshape(Bn, T, len(POOL_WINDOWS), CG), w_pool).reshape(Bn, T, C_POOL) * pool_scale

    merged = (jax.nn.sigmoid(ga) * (out_a @ w_br_a)
              + jax.nn.sigmoid(gb) * (out_b @ w_br_b)
              + jax.nn.sigmoid(gc) * (oc @ w_br_c))
    x = x + _rms_norm(merged @ w_out, g_post_mix)

    gate, up = _split(_rms_norm(x, g_pre_ffn) @ w_gu, (D_FF, D_FF))
    x = x + _rms_norm((jax.nn.silu(gate) * up) @ w_down, g_post_ffn)
    return x, (win_new[0], win_new[1], win_new[2], s_new.astype(dt), conv_new, pool_new)


def setup_inputs(seed: int = 0) -> dict:
    key = jax.random.key(seed)
    ks = jax.random.split(key, 32)
    f32 = jnp.float32

    def nrm(k, shape, s=1.0):
        return jax.random.normal(k, shape, f32) * s

    wb = [min(win, PAST_LEN) for win, _ in DIL_GROUPS]
    return {
        'x_prompt': nrm(ks[0], (BATCH, SEQ, D_MODEL)),
        'x_sample': nrm(ks[1], (DEC_BATCH, DEC_SEQ, D_MODEL)),
        'cache_win1': nrm(ks[2], (DEPTH, DEC_BATCH, wb[0], 2, H_G, HD_B)),
        'cache_win2': nrm(ks[3], (DEPTH, DEC_BATCH, wb[1], 2, H_G, HD_B)),
        'cache_win3': nrm(ks[4], (DEPTH, DEC_BATCH, wb[2], 2, H_G, HD_B)),
        'state_gdn': nrm(ks[5], (DEPTH, DEC_BATCH, H_A, DK_A, DV_A), 0.1),
        'state_conv': nrm(ks[6], (DEPTH, DEC_BATCH, CONV_K - 1, CONV_CH)),
        'state_pool': nrm(ks[7], (DEPTH, DEC_BATCH, POOL_HIST, C_POOL)),
        'w_in': nrm(ks[8], (DEPTH, D_MODEL, N_IN), D_MODEL ** -0.5),
        'conv_w': nrm(ks[9], (DEPTH, CONV_K, CONV_CH), CONV_K ** -0.5),
        'a_log': jnp.log(jax.random.uniform(ks[10], (DEPTH, H_A), f32, 1.0, 16.0)),
        'dt_bias': jnp.log(jnp.expm1(jax.random.uniform(ks[11], (DEPTH, H_A), f32, 1e-3, 0.1))),
        'gdn_gain': 1.0 + nrm(ks[12], (DEPTH, DV_A), 0.02),
        'w_pool': nrm(ks[13], (DEPTH, len(POOL_WINDOWS), CG, CG), CG ** -0.5),
        'pool_scale': 1.0 + nrm(ks[14], (DEPTH, C_POOL), 0.02),
        'w_br_a': nrm(ks[15], (DEPTH, V_A, D_MODEL), V_A ** -0.5),
        'w_br_b': nrm(ks[16], (DEPTH, OUT_B, D_MODEL), OUT_B ** -0.5),
        'w_br_c': nrm(ks[17], (DEPTH, C_POOL, D_MODEL), C_POOL ** -0.5),
        'w_out': nrm(ks[18], (DEPTH, D_MODEL, D_MODEL), D_MODEL ** -0.5),
        'w_gu': nrm(ks[19], (DEPTH, D_MODEL, 2 * D_FF), D_MODEL ** -0.5),
        'w_down': nrm(ks[20], (DEPTH, D_FF, D_MODEL), D_FF ** -0.5),
        'g_pre_mix': 1.0 + nrm(ks[21], (DEPTH, D_MODEL), 0.02),
        'g_post_mix': 1.0 + nrm(ks[22], (DEPTH, D_MODEL), 0.02),
        'g_pre_ffn': 1.0 + nrm(ks[23], (DEPTH, D_MODEL), 0.02),
        'g_post_ffn': 1.0 + nrm(ks[24], (DEPTH, D_MODEL), 0.02),
    }


def reference(x_prompt, x_sample, cache_win1, cache_win2, cache_win3, state_gdn, state_conv, state_pool,
              w_in, conv_w, a_log, dt_bias, gdn_gain, w_pool, pool_scale, w_br_a, w_br_b, w_br_c,
              w_out, w_gu, w_down, g_pre_mix, g_post_mix, g_pre_ffn, g_post_ffn):
    yp, ys = x_prompt, x_sample
    bp = x_prompt.shape[0]
    dt = x_prompt.dtype
    new_p, new_s = [], []
    for l in range(DEPTH):
        lp = (w_in[l], conv_w[l], a_log[l], dt_bias[l], gdn_gain[l], w_pool[l], pool_scale[l],
              w_br_a[l], w_br_b[l], w_br_c[l], w_out[l], w_gu[l], w_down[l],
              g_pre_mix[l], g_post_mix[l], g_pre_ffn[l], g_post_ffn[l])
        hist_p = (jnp.zeros((bp, CONV_K - 1, CONV_CH), dt),
                  jnp.zeros((bp, H_A, DK_A, DV_A), jnp.float32),
                  jnp.zeros((bp, POOL_HIST, C_POOL), dt),
                  None)
        yp, st_p = _layer(yp, lp, hist_p, True)
        hist_s = (state_conv[l], state_gdn[l], state_pool[l], (cache_win1[l], cache_win2[l], cache_win3[l]))
        ys, st_s = _layer(ys, lp, hist_s, False)
        new_p.append(st_p)
        new_s.append(st_s)
    pw1, pw2, pw3, pgdn, pconv, ppool = [jnp.stack([st[i] for st in new_p], axis=0) for i in range(6)]
    sw1, sw2, sw3, sgdn, sconv, spool = [jnp.stack([st[i] for st in new_s], axis=0) for i in range(6)]
    return (yp, ys, pw1, pw2, pw3, pgdn, pconv, ppool, sw1, sw2, sw3, sgdn, sconv, spool)
```

```python
import contextlib
import numpy as np
import concourse.bass as bass
import concourse.mybir as mybir
from concourse.bass_utils import run_bass_kernel_spmd

F32 = mybir.dt.float32
BF16 = mybir.dt.bfloat16
AF = mybir.ActivationFunctionType
ALU = mybir.AluOpType

D = 2048
KC = 16
NIN = 15888
DFF = 5632
FT = DFF // 128
EPS = 1e-6
NEG = -30000.0
O_QA, O_KA, O_VA, O_ZA, O_BA, O_AA = 0, 1024, 2048, 3072, 4096, 4104
O_QB, O_KB, O_VB, O_UC, O_GA, O_GB, O_GC = 4112, 5648, 7184, 8720, 9744, 11792, 13840
WINS = (128, 512, 2048)
DILS = (1, 4, 16)
NDS = 24
ISQ = 128 ** -0.5
DBG_STOP = None
DBG_DUMP = False


class _Stop(Exception):
    pass


def _chk(k):
    if DBG_STOP is not None and DBG_STOP == k:
        raise _Stop()


class Trk:
    def __init__(self, name="t"):
        self.name = name
        self.w = {}
        self.r = {}


class Tile(Trk):
    def __init__(self, name, t):
        super().__init__(name)
        self.t = t

    def __getitem__(self, idx):
        return self.t[idx]


class Sched:
    def __init__(self, nc, es):
        self.nc = nc
        self.es = es
        self.eng = {"pe": nc.tensor, "act": nc.scalar, "dve": nc.vector, "pool": nc.gpsimd, "sp": nc.sync}
        self.sem = {e: es.enter_context(nc.semaphore("s_" + e)) for e in ("pe", "act", "dve", "pool")}
        self.cnt = {e: 0 for e in self.sem}
        self.dsem = [es.enter_context(nc.semaphore("d%d" % i)) for i in range(NDS)]
        self.dcnt = [0] * NDS
        self.dnext = 0
        self.waited = {e: {} for e in self.eng}
        self.released = {}
        self.uid = 0
        self.banks = []
        self.ninst = 0

    def tile(self, stack, shape, dtype, name=None):
        self.uid += 1
        nm = "%s_%d" % (name or "t", self.uid)
        t = stack.enter_context(self.nc.sbuf_tensor(nm, list(shape), dtype))
        tl = Tile(nm, t)
        tl.r = dict(self.released)
        return tl

    def release(self, tiles):
        for tl in tiles:
            for k, v in list(tl.w.items()) + list(tl.r.items()):
                if self.released.get(k, 0) < v:
                    self.released[k] = v

    @contextlib.contextmanager
    def scope(self):
        st = contextlib.ExitStack()
        tiles = []

        def mk(shape, dtype, name=None):
            tl = self.tile(st, shape, dtype, name)
            tiles.append(tl)
            return tl

        try:
            yield mk
        finally:
            self.release(tiles)
            st.close()

    def init_psum(self):
        for i in range(8):
            t = self.es.enter_context(self.nc.psum_tensor("psb%d" % i, [128, 512], F32))
            self.banks.append(Tile("psb%d" % i, t))
            self.banks[-1].psum = True

    def bank(self):
        b = self.banks.pop(0)
        self.banks.append(b)
        return b

    def bank_reserve(self):
        return self.banks.pop(0)

    def bank_release(self, b):
        self.banks.append(b)

    def _semof(self, key):
        if isinstance(key, str):
            return self.sem[key]
        return self.dsem[key[1]]

    def _wait(self, e, key, val):
        if self.waited[e].get(key, 0) >= val:
            return
        self.waited[e][key] = val
        self.eng[e].wait_ge(self._semof(key), val)
        self.ninst += 1

    def _deps(self, e, r, w, append=False):
        deps = {}

        def add(ev, same_ok):
            if ev is None:
                return
            k, v = ev
            if k == e and not same_ok:
                return
            if deps.get(k, 0) < v:
                deps[k] = v

        for b in r:
            for ev in b.w.items():
                add(ev, e != "pe")
            if getattr(b, "psum", False):
                for ev in b.r.items():
                    add(ev, False)
        for b in w:
            if not append:
                for ev in b.w.items():
                    add(ev, False)
            for ev in b.r.items():
                add(ev, False)
        for k, v in deps.items():
            self._wait(e, k, v)

    def _record(self, ev, r, w, append=False):
        k, v = ev
        for b in r:
            if b.r.get(k, 0) < v:
                b.r[k] = v
        for b in w:
            if append:
                b.w[k] = v
            else:
                b.w = {k: v}
            b.r = {}

    def op(self, e, fn, r=(), w=(), sig=True):
        self._deps(e, r, w)
        ins = fn()
        self.ninst += 1
        if sig:
            self.cnt[e] += 1
            ins.then_inc(self.sem[e], 1)
            ev = (e, self.cnt[e])
        else:
            ev = (e, self.cnt[e] + 1)
        self._record(ev, r, w)
        return ins

    def dma(self, q, out, in_, r=(), w=(), append=False):
        self._deps(q, r, w, append)
        j = self.dnext
        self.dnext = (j + 1) % NDS
        if self.dcnt[j] > 0:
            self._wait(q, ("d", j), 16 * self.dcnt[j])
        self.dcnt[j] += 1
        self.eng[q].dma_start(out=out, in_=in_).then_inc(self.dsem[j], 16)
        self.ninst += 1
        self._record((("d", j), 16 * self.dcnt[j]), r, w, append)

    def finish(self):
        for j in range(NDS):
            if self.dcnt[j] > 0:
                self._wait("sp", ("d", j), 16 * self.dcnt[j])
        for e in ("pe", "act", "dve", "pool"):
            if self.cnt[e] > 0:
                self._wait("sp", e, self.cnt[e])

    def mm(self, out, lhsT, rhs, start=True, stop=True, r=(), w=(), sig=None):
        if sig is None:
            sig = stop
        return self.op("pe", lambda: self.nc.tensor.matmul(out, lhsT=lhsT, rhs=rhs, start=start, stop=stop), r, w, sig)

    def tr(self, out, in_, ident, r=(), w=(), sig=True):
        return self.op("pe", lambda: self.nc.tensor.transpose(out=out, in_=in_, identity=ident), r, w, sig)

    def act(self, out, in_, func, r=(), w=(), **kw):
        return self.op("act", lambda: self.nc.scalar.activation(out=out, in_=in_, func=func, **kw), r, w)

    def v(self, name, r=(), w=(), **kw):
        return self.op("dve", lambda: getattr(self.nc.vector, name)(**kw), r, w)

    def g(self, name, r=(), w=(), **kw):
        return self.op("pool", lambda: getattr(self.nc.gpsimd, name)(**kw), r, w)


def build(T, NS, L=2):
    NB = T // 512
    NSQ = NS * 4
    nc = bass.Bass("TRN2", target_bir_lowering=False)

    def din(name, shape, dt=F32):
        return nc.dram_tensor(name, list(shape), dt, kind="ExternalInput").ap()

    def dout(name, shape, dt=F32):
        return nc.dram_tensor(name, list(shape), dt, kind="ExternalOutput").ap()

    def dscr(name, shape, dt=F32):
        return nc.dram_tensor(name, list(shape), dt, kind="Internal").ap()

    I = {}
    I["xp"] = din("xp", [T, D])
    I["xs"] = din("xs", [NSQ, D])
    I["cw1"] = din("cw1", [L, NS, 128, 1024])
    I["cw2"] = din("cw2", [L, NS, 512, 1024])
    I["cw3"] = din("cw3", [L, NS, 2048, 1024])
    I["sgdn"] = din("sgdn", [L, NS, 8, 128, 128])
    I["sconv"] = din("sconv", [L, NS * 3, 3072])
    I["spool"] = din("spool", [L, NS * 15, 1024])
    I["w_in"] = din("w_in", [L, D, NIN])
    I["convw"] = din("convw", [L, 128, 24, 4])
    I["alog"] = din("alog", [L, 128, 4, 8])
    I["dtb"] = din("dtb", [L, 128, 4, 8])
    I["ggain"] = din("ggain", [L, 128, 1])
    I["w_pool"] = din("w_pool", [L, 4, 256, 256])
    I["pscale"] = din("pscale", [L, 128, 8])
    I["w_br_a"] = din("w_br_a", [L, 1024, D])
    I["w_br_b"] = din("w_br_b", [L, 512, D])
    I["w_br_c"] = din("w_br_c", [L, 1024, D])
    I["w_out"] = din("w_out", [L, D, D])
    I["w_gu"] = din("w_gu", [L, D, 2 * DFF])
    I["w_down"] = din("w_down", [L, DFF, D])
    for nm in ("gpre", "gpost", "gpre2", "gpost2"):
        I[nm] = din(nm, [L, 128, 16])
    I["ident"] = din("ident", [128, 128])
    I["umat"] = din("umat", [128, 128])
    I["maskS4"] = din("maskS4", [128, 4, 128])
    I["maskI4"] = din("maskI4", [128, 4, 128])
    I["ident4"] = din("ident4", [128, 4, 128])
    I["amask"] = din("amask", [128, 7, 128])
    I["pcorr"] = din("pcorr", [128, 4, 16])
    I["smask"] = din("smask", [128, 12, 4])

    O = {}
    O["yp"] = dout("yp", [T, D])
    O["ys"] = dout("ys", [NSQ, D])
    PW = [min(w, T) for w in WINS]
    O["w1p"] = dout("w1p", [L, PW[0], 1024])
    O["w2p"] = dout("w2p", [L, PW[1], 1024])
    O["w3p"] = dout("w3p", [L, PW[2], 1024])
    O["gdnp"] = dout("gdnp", [L, 8, 128, 128])
    O["convp"] = dout("convp", [L, 3, 3072])
    O["poolp"] = dout("poolp", [L, 15, 1024])
    O["w1s"] = dout("w1s", [L, NS, 128, 1024])
    O["w2s"] = dout("w2s", [L, NS, 512, 1024])
    O["w3s"] = dout("w3s", [L, NS, 2048, 1024])
    O["gdns"] = dout("gdns", [L, NS, 8, 128, 128])
    O["convs"] = dout("convs", [L, NS, 3, 3072])
    O["pools"] = dout("pools", [L, NS, 15, 1024])
    if DBG_DUMP:
        O["dbg"] = dout("dbg", [40, 128, 512])
    WP = [O["w1p"], O["w2p"], O["w3p"]]
    WS = [O["w1s"], O["w2s"], O["w3s"]]
    CW = [I["cw1"], I["cw2"], I["cw3"]]

    x1T = dscr("x1T", [KC, 128, T])
    xs1T = dscr("xs1T", [KC, 128, NSQ])
    kTs = dscr("kTs", [3, 4, 128, T], BF16)
    Vs = dscr("Vs", [3, 4, T, 128], BF16)
    d_x1T, d_xs1T, d_kTs, d_Vs = Trk("x1T"), Trk("xs1T"), Trk("kTs"), Trk("Vs")

    es = contextlib.ExitStack()
    with es:
        S = Sched(nc, es)
        S.init_psum()
        mm, act, V_, G_, dma, tr = S.mm, S.act, S.v, S.g, S.dma, S.tr

        def gt(shape, dt, name):
            return S.tile(es, shape, dt, name)

        ident = gt([128, 128], F32, "ident")
        umat = gt([128, 128], F32, "umat")
        ones_f = gt([128, 128], F32, "ones_f")
        ones_b = gt([128, 128], BF16, "ones_b")
        one_c = gt([128, 1], F32, "one_c")
        maskS4 = gt([128, 4, 128], F32, "maskS4")
        maskI4 = gt([128, 4, 128], F32, "maskI4")
        ident4 = gt([128, 4, 128], F32, "ident4")
        amask = gt([128, 7, 128], BF16, "amask")
        pcorr = gt([128, 4, 16], F32, "pcorr")
        smask = gt([128, 12, 4], F32, "smask")
        for tl, nm in ((ident, "ident"), (umat, "umat"), (maskS4, "maskS4"), (maskI4, "maskI4"),
                       (ident4, "ident4"), (pcorr, "pcorr"), (smask, "smask")):
            dma("sp", tl[:], I[nm], w=[tl])
        dma("pool", amask[:], I["amask"], w=[amask])
        G_("memset", w=[ones_f], ap=ones_f[:], constant=1.0)
        G_("memset", w=[ones_b], ap=ones_b[:], constant=1.0)
        G_("memset", w=[one_c], ap=one_c[:], constant=1.0)

        d_ws = Trk("ws")
        for l in range(L):
            for g in range(3):
                w = WINS[g]
                for s in range(NS):
                    nch_ = max(1, (w - 4) // 512)
                    rows = [(i * (w - 4)) // nch_ for i in range(nch_ + 1)]
                    for i in range(nch_):
                        dma("act", WS[g][l, s, rows[i]:rows[i + 1], :], CW[g][l, s, 4 + rows[i]:4 + rows[i + 1], :], w=[d_ws])
            for s in range(NS):
                dma("act", O["pools"][l, s, 0:11, :], I["spool"][l, s * 15 + 4:s * 15 + 15, :], w=[d_ws])

        convw = gt([128, 24, 4], F32, "convw")
        alog = gt([128, 4, 8], F32, "alog")
        dtb = gt([128, 4, 8], F32, "dtb")
        negA = gt([128, 4, 8], F32, "negA")
        ggain = gt([128, 1], F32, "ggain")
        pscale = gt([128, 8], F32, "pscale")
        gpre, gpost, gpre2, gpost2 = (gt([128, 16], F32, n) for n in ("gpre", "gpost", "gpre2", "gpost2"))
        wpool = gt([128, 4, 2, 256], BF16, "wpool")
        wba = gt([128, KC, 16], BF16, "wba")

        WB = [gt([128, 8192], BF16, "wb%d" % i) for i in range(2)]
        wstate = {"i": 0}

        wscr_chunks = [dscr("wscr%d" % i, [32, 128, 8192], BF16) for i in range(7)]

        def wscr_at(slot):
            return wscr_chunks[slot // 32][slot % 32]
        wslots = {}

        def wload(parts, key):
            wb = WB[wstate["i"] % 2]
            wstate["i"] += 1
            off = 0
            views = []
            hit = key in wslots
            for ap_ in parts:
                shp = list(ap_.shape)[1:]
                n = int(np.prod(shp))
                flat = wb.t[:, off:off + n]
                if len(shp) == 2:
                    vw = flat.rearrange("p (a b) -> p a b", a=shp[0])
                elif len(shp) == 3:
                    vw = flat.rearrange("p (a b c) -> p a b c", a=shp[0], b=shp[1])
                else:
                    raise ValueError(shp)
                if not hit:
                    if len(shp) == 3:
                        for bi in range(shp[1]):
                            dma("pool", vw[:, :, bi, :], ap_[:, :, bi, :], w=[wb], append=(len(views) > 0 or bi > 0))
                    else:
                        dma("pool", vw, ap_, w=[wb], append=(len(views) > 0))
                views.append(vw)
                off += n
            assert off <= 8192, off
            if hit:
                slot, stk = wslots[key]
                dma("pool", wb.t[:, :off], wscr_at(slot)[:, :off], r=[stk], w=[wb])
            elif key is not None:
                slot = len(wslots)
                assert slot < 7 * 32
                stk = Trk("ws%d" % slot)
                wslots[key] = (slot, stk)
                dma("sp", wscr_at(slot)[:, :off], wb.t[:, :off], r=[wb], w=[stk])
            return wb, views

        def win_cols(l, c0, n):
            return I["w_in"][l][:, c0:c0 + n].rearrange("(kc p) n -> p kc n", p=128)

        def rows_cols(w2d, c0, n):
            return w2d[:, c0:c0 + n].rearrange("(kc p) n -> p kc n", p=128)

        ctail = gt([128, 24, 3], F32, "ctail")
        ptail = gt([128, 8, 15], F32, "ptail")
        sstate = gt([128, 8, 128], F32, "sstate")
        xT = gt([128, KC, 512], F32, "xT")
        hT = gt([128, KC, 512], BF16, "hT")

        def dump(slot, ap_, trk, n=512):
            if not DBG_DUMP:
                return
            with S.scope() as mkd:
                tmp = mkd([128, 512], F32, "dbgt")
                V_("tensor_copy", r=[trk], w=[tmp], out=tmp[:, :n], in_=ap_)
                dma("sp", O["dbg"][slot, :, :n], tmp[:, :n], r=[tmp], w=[])

        def rstd_from(mk, srcs, trks, ntok, div):
            ssb = S.bank_reserve()
            sqs = [mk([128, 512], F32, "sq") for _ in range(2)]
            n = len(srcs)
            for i in range(n):
                sq = sqs[i % 2]
                act(sq[:, :ntok], srcs[i], AF.Square, r=[trks[i]], w=[sq])
                mm(ssb[:, :ntok], ones_f[:], sq[:, :ntok], start=(i == 0), stop=(i == n - 1), r=[ones_f, sq], w=[ssb], sig=True)
            rstd = mk([128, 512], F32, "rstd")
            V_("tensor_scalar", r=[ssb], w=[rstd], out=rstd[:, :ntok], in0=ssb[:, :ntok], scalar1=1.0 / div,
               scalar2=EPS, op0=ALU.mult, op1=ALU.add)
            act(rstd[:, :ntok], rstd[:, :ntok], AF.Ln, r=[rstd], w=[rstd])
            act(rstd[:, :ntok], rstd[:, :ntok], AF.Exp, r=[rstd], w=[rstd], scale=-0.5)
            S.bank_release(ssb)
            return rstd

        def norm_to_hT(gain, ntok):
            with S.scope() as mk:
                rstd = rstd_from(mk, [xT[:, kc, :ntok] for kc in range(KC)], [xT] * KC, ntok, float(D))
                for kc in range(KC):
                    V_("scalar_tensor_tensor", r=[xT, gain, rstd], w=[hT], out=hT[:, kc, :ntok], in0=xT[:, kc, :ntok],
                       scalar=gain[:, kc:kc + 1], in1=rstd[:, :ntok], op0=ALU.mult, op1=ALU.mult)

        def proj_fm(lhs_of_kc, ntok, r_w):
            pb = S.bank()
            for kc in range(KC):
                mm(pb[:, :ntok], lhs_of_kc(kc), hT[:, kc, :ntok], start=(kc == 0), stop=(kc == KC - 1), r=[r_w, hT], w=[pb])
            return pb

        def load_layer_params(l):
            for tl, nm in ((convw, "convw"), (alog, "alog"), (dtb, "dtb"), (ggain, "ggain"), (pscale, "pscale"),
                           (gpre, "gpre"), (gpost, "gpost"), (gpre2, "gpre2"), (gpost2, "gpost2")):
                dma("sp", tl[:], I[nm][l], w=[tl])
            dma("pool", wpool[:], I["w_pool"][l].rearrange("g (kt p) n -> p g kt n", p=128), w=[wpool])
            dma("pool", wba[:], win_cols(l, O_BA, 16), w=[wba])
            act(negA[:], alog[:], AF.Exp, r=[alog], w=[negA])
            V_("tensor_scalar", r=[negA], w=[negA], out=negA[:], in0=negA[:], scalar1=-1.0, scalar2=None, op0=ALU.mult)

        def gdn(l, b, prompt, brT):
            nseq = 1 if prompt else NS
            Tq = 512 if prompt else 4
            C = 128 if prompt else 4
            nch = 4 if prompt else NS
            ntok = nch * C
            nlev = 6 if prompt else 1
            last = prompt and (b == NB - 1)
            with S.scope() as mk:
                chist = sst = None
                if not prompt:
                    chist = mk([128, 24, NS, 3], F32, "chist")
                    with S.scope() as mk3:
                        craw = mk3([NS * 3, 3072], F32, "craw")
                        dma("sp", craw[:NS * 3, :], I["sconv"][l], w=[craw])
                        pb = S.bank()
                        for t_ in range(24):
                            tr(pb[:, t_ * NS * 3:(t_ + 1) * NS * 3], craw[:NS * 3, t_ * 128:(t_ + 1) * 128], ident[:NS * 3, :NS * 3],
                               r=[craw, ident], w=[pb], sig=(t_ == 23))
                        act(chist[:].rearrange("p t s j -> p (t s j)"), pb[:, :24 * NS * 3], AF.Copy, r=[pb], w=[chist])
                    sst = mk([128, NS, 8, 128], F32, "sst")
                    for s in range(NS):
                        dma("sp", sst[:, s, :, :], I["sgdn"][l, s].rearrange("h k v -> k h v"), w=[sst], append=(s > 0))
                gp = S.bank()
                for c in range(nch):
                    for kc in range(KC):
                        mm(gp[:C, c * 16:(c + 1) * 16], hT[:, kc, c * C:(c + 1) * C], wba[:, kc, :], start=(kc == 0),
                           stop=(kc == KC - 1), r=[hT, wba], w=[gp])
                gview = gp[:C, :nch * 16].rearrange("p (c n) -> p c n", n=16)
                sm = lambda nm: mk([128, 4, 8], F32, nm)
                beta, nbeta, spx, gg, gc, ngc, egc, bg = (sm(n) for n in ("beta", "nbeta", "spx", "gg", "gc", "ngc", "egc", "bg"))
                act(beta[:C, :nch, :], gview[:, :, 0:8], AF.Sigmoid, r=[gp], w=[beta])
                V_("tensor_tensor", r=[gp, dtb], w=[spx], out=spx[:C, :nch, :], in0=gview[:, :, 8:16], in1=dtb[:C, :nch, :], op=ALU.add)
                act(spx[:C, :nch, :], spx[:C, :nch, :], AF.Exp, r=[spx], w=[spx])
                act(spx[:C, :nch, :], spx[:C, :nch, :], AF.Ln, r=[spx, one_c], w=[spx], bias=one_c[:C, :])
                V_("tensor_tensor", r=[spx, negA], w=[gg], out=gg[:C, :nch, :], in0=spx[:C, :nch, :], in1=negA[:C, :nch, :], op=ALU.mult)
                gcb = S.bank()
                for c in range(nch):
                    mm(gcb[:C, c * 8:(c + 1) * 8], umat[:C, :C], gg[:C, c, :], r=[umat, gg], w=[gcb])
                act(gc[:C, :nch, :], gcb[:C, :nch * 8].rearrange("p (c n) -> p c n", n=8), AF.Copy, r=[gcb], w=[gc])
                V_("tensor_scalar", r=[gc], w=[ngc], out=ngc[:C, :nch, :], in0=gc[:C, :nch, :], scalar1=-1.0, scalar2=None, op0=ALU.mult)
                V_("tensor_scalar", r=[beta], w=[nbeta], out=nbeta[:C, :nch, :], in0=beta[:C, :nch, :], scalar1=-1.0, scalar2=None, op0=ALU.mult)
                act(egc[:C, :nch, :], gc[:C, :nch, :], AF.Exp, r=[gc], w=[egc])
                V_("tensor_tensor", r=[beta, egc], w=[bg], out=bg[:C, :nch, :], in0=beta[:C, :nch, :], in1=egc[:C, :nch, :], op=ALU.mult)

                big = lambda nm: mk([128, 4, 128], F32, nm)
                pre = mk([128, nseq, 3 + Tq], F32, "pre")
                cv = mk([128, nseq, Tq], F32, "cv")
                xs_ = mk([128, ntok], F32, "xs_")
                qT, kT, vT, zs, qgT, oT = (mk([128, ntok], F32, n) for n in ("qT", "kT", "vT", "zs", "qgT", "oT"))
                Gb, Dm, DT, Eg, W_a, W_b, N_a, N_b, RT, AT, kbg, kd, vb, u_, wT = (
                    big(n) for n in ("Gb", "Dm", "DT", "Eg", "W_a", "W_b", "N_a", "N_b", "RT", "AT", "kbg", "kd", "vb", "u_", "wT"))
                vns = [mk([128, 128], F32, "vn%d" % i) for i in range(2)]
                wq4 = I["w_in"][l][:, 0:4096].rearrange("(kc p) (j h c) -> p kc j h c", p=128, j=4, h=8)
                for hd in range(8):
                    wb, (wv,) = wload([wq4[:, :, :, hd, :]], ("gdn", l, hd))
                    for j, dst in ((0, qT), (1, kT), (2, vT)):
                        pb = proj_fm(lambda kc: wv[:, kc, j, :], ntok, wb)
                        ctile = j * 8 + hd
                        act(pre[:, :, 3:3 + Tq], pb[:, :ntok].rearrange("p (s t) -> p s t", s=nseq), AF.Copy, r=[pb], w=[pre])
                        if prompt:
                            V_("tensor_copy", r=[ctail], w=[pre], out=pre[:, 0, 0:3], in_=ctail[:, ctile, :])
                        else:
                            V_("tensor_copy", r=[chist], w=[pre], out=pre[:, :, 0:3], in_=chist[:, ctile, :, :])
                        V_("tensor_scalar", r=[pre, convw], w=[cv], out=cv[:], in0=pre[:, :, 0:Tq], scalar1=convw[:, ctile, 0:1],
                           scalar2=None, op0=ALU.mult)
                        for jj in range(1, 4):
                            V_("scalar_tensor_tensor", r=[pre, convw, cv], w=[cv], out=cv[:], in0=pre[:, :, jj:jj + Tq],
                               scalar=convw[:, ctile, jj:jj + 1], in1=cv[:], op0=ALU.mult, op1=ALU.add)
                        if prompt:
                            V_("tensor_copy", r=[pre], w=[ctail], out=ctail[:, ctile, :], in_=pre[:, 0, Tq:Tq + 3])
                        cvf = cv[:].rearrange("p s t -> p (s t)")
                        if j == 2:
                            act(vT[:, :ntok], cvf, AF.Silu, r=[cv], w=[vT])
                        else:
                            act(xs_[:, :ntok], cvf, AF.Silu, r=[cv], w=[xs_])
                            with S.scope() as mk2:
                                rs = rstd_from(mk2, [xs_[:, :ntok]], [xs_], ntok, 1.0)
                                V_("scalar_tensor_tensor", r=[xs_, rs], w=[dst], out=dst[:, :ntok], in0=xs_[:, :ntok],
                                   scalar=(ISQ if j == 0 else 1.0), in1=rs[:, :ntok], op0=ALU.mult, op1=ALU.mult)
                    pb = proj_fm(lambda kc: wv[:, kc, 3, :], ntok, wb)
                    act(zs[:, :ntok], pb[:, :ntok], AF.Silu, r=[pb], w=[zs])
                    for c in range(nch):
                        V_("tensor_scalar", r=[ones_f, gg], w=[Gb], out=Gb[:C, c, :], in0=ones_f[:C, :], scalar1=gg[:C, c, hd:hd + 1],
                           scalar2=None, op0=ALU.mult)
                    rb = S.bank()
                    for c in range(nch):
                        mm(rb[:, c * 128:c * 128 + C], Gb[:C, c, :], umat[:C, :C], r=[Gb, umat], w=[rb])
                    rb3 = rb[:].rearrange("p (c n) -> p c n", c=4)
                    V_("scalar_tensor_tensor", r=[rb, maskS4], w=[Dm], out=Dm[:C, :nch, :C], in0=rb3[:C, :nch, :C], scalar=-1.0,
                       in1=maskS4[:C, :nch, :C], op0=ALU.mult, op1=ALU.add)
                    V_("tensor_tensor", r=[rb, maskI4], w=[DT], out=DT[:C, :nch, :C], in0=rb3[:C, :nch, :C], in1=maskI4[:C, :nch, :C], op=ALU.add)
                    act(Eg[:, :nch, :C], rb3[:, :nch, :C], AF.Exp, r=[rb], w=[Eg])
                    for c in range(nch):
                        act(Dm[:C, c, :C], Dm[:C, c, :C], AF.Exp, r=[Dm, gc], w=[Dm], bias=gc[:C, c, hd:hd + 1])
                        act(DT[:C, c, :C], DT[:C, c, :C], AF.Exp, r=[DT, ngc], w=[DT], bias=ngc[:C, c, hd:hd + 1])
                    V_("tensor_tensor", r=[qT, Eg], w=[qgT], out=qgT[:, :ntok].rearrange("p (c t) -> p c t", c=nch),
                       in0=qT[:, :ntok].rearrange("p (c t) -> p c t", c=nch), in1=Eg[:, :nch, :C], op=ALU.mult)
                    kkb = S.bank()
                    qkb = S.bank()
                    for c in range(nch):
                        cs = slice(c * C, (c + 1) * C)
                        mm(kkb[:C, c * 128:c * 128 + C], kT[:, cs], kT[:, cs], r=[kT], w=[kkb])
                        mm(qkb[:C, c * 128:c * 128 + C], kT[:, cs], qT[:, cs], r=[kT, qT], w=[qkb])
                    kk3 = kkb[:].rearrange("p (c n) -> p c n", c=4)
                    qk3 = qkb[:].rearrange("p (c n) -> p c n", c=4)
                    for c in range(nch):
                        V_("scalar_tensor_tensor", r=[kkb, nbeta, Dm], w=[W_a], out=W_a[:C, c, :C], in0=kk3[:C, c, :C],
                           scalar=nbeta[:C, c, hd:hd + 1], in1=Dm[:C, c, :C], op0=ALU.mult, op1=ALU.mult)
                    V_("tensor_tensor", r=[qkb, DT], w=[AT], out=AT[:C, :nch, :C], in0=qk3[:C, :nch, :C], in1=DT[:C, :nch, :C], op=ALU.mult)
                    tb = S.bank()
                    for c in range(nch):
                        tr(tb[:C, c * 128:c * 128 + C], W_a[:C, c, :C], ident[:C, :C], r=[W_a, ident], w=[tb], sig=(c == nch - 1))
                    tb3 = tb[:].rearrange("p (c n) -> p c n", c=4)
                    act(N_a[:C, :nch, :C], tb3[:C, :nch, :C], AF.Copy, r=[tb], w=[N_a])
                    V_("tensor_tensor", r=[tb, ident4], w=[RT], out=RT[:C, :nch, :C], in0=tb3[:C, :nch, :C], in1=ident4[:C, :nch, :C], op=ALU.add)
                    Wc, Nc, Wn, Nn = W_a, N_a, W_b, N_b
                    for k in range(1, nlev + 1):
                        wb_ = S.bank()
                        for c in range(nch):
                            mm(wb_[:C, c * 128:c * 128 + C], Nc[:C, c, :C], Wc[:C, c, :C], r=[Nc, Wc], w=[wb_])
                        act(Wn[:C, :nch, :C], wb_[:].rearrange("p (c n) -> p c n", c=4)[:C, :nch, :C], AF.Copy, r=[wb_], w=[Wn])
                        if k < nlev:
                            nb_ = S.bank()
                            for c in range(nch):
                                mm(nb_[:C, c * 128:c * 128 + C], Wc[:C, c, :C], Nc[:C, c, :C], r=[Nc, Wc], w=[nb_])
                            V_("tensor_copy", r=[nb_], w=[Nn], out=Nn[:C, :nch, :C], in_=nb_[:].rearrange("p (c n) -> p c n", c=4)[:C, :nch, :C])
                        pb_ = S.bank()
                        for c in range(nch):
                            mm(pb_[:C, c * 128:c * 128 + C], Wn[:C, c, :C], RT[:C, c, :C], r=[Wn, RT], w=[pb_])
                        V_("tensor_tensor", r=[pb_, RT], w=[RT], out=RT[:C, :nch, :C], in0=RT[:C, :nch, :C],
                           in1=pb_[:].rearrange("p (c n) -> p c n", c=4)[:C, :nch, :C], op=ALU.add)
                        Wc, Wn = Wn, Wc
                        Nc, Nn = Nn, Nc
                    ktb = S.bank()
                    vtb = S.bank()
                    for c in range(nch):
                        cs = slice(c * C, (c + 1) * C)
                        tr(ktb[:C, c * 128:(c + 1) * 128], kT[:, cs], ident[:], r=[kT, ident], w=[ktb], sig=(c == nch - 1))
                    for c in range(nch):
                        cs = slice(c * C, (c + 1) * C)
                        tr(vtb[:C, c * 128:(c + 1) * 128], vT[:, cs], ident[:], r=[vT, ident], w=[vtb], sig=(c == nch - 1))
                    for c in range(nch):
                        V_("tensor_scalar", r=[ktb, bg], w=[kbg], out=kbg[:C, c, :], in0=ktb[:C, c * 128:(c + 1) * 128],
                           scalar1=bg[:C, c, hd:hd + 1], scalar2=None, op0=ALU.mult)
                        V_("tensor_scalar", r=[ktb, DT], w=[kd], out=kd[:C, c, :], in0=ktb[:C, c * 128:(c + 1) * 128],
                           scalar1=DT[:C, c, C - 1:C], scalar2=None, op0=ALU.mult)
                        V_("tensor_scalar", r=[vtb, beta], w=[vb], out=vb[:C, c, :], in0=vtb[:C, c * 128:(c + 1) * 128],
                           scalar1=beta[:C, c, hd:hd + 1], scalar2=None, op0=ALU.mult)
                    ub = S.bank()
                    wtb = S.bank()
                    for c in range(nch):
                        mm(ub[:C, c * 128:(c + 1) * 128], RT[:C, c, :C], vb[:C, c, :], r=[RT, vb], w=[ub])
                    for c in range(nch):
                        mm(wtb[:, c * 128:c * 128 + C], kbg[:C, c, :], RT[:C, c, :C], r=[RT, kbg], w=[wtb])
                    act(u_[:C, :nch, :], ub[:].rearrange("p (c n) -> p c n", c=4)[:C, :nch, :], AF.Copy, r=[ub], w=[u_])
                    V_("tensor_copy", r=[wtb], w=[wT], out=wT[:, :nch, :C], in_=wtb[:].rearrange("p (c n) -> p c n", c=4)[:, :nch, :C])
                    ob = S.bank_reserve()
                    for c in range(nch):
                        cs = slice(c * C, (c + 1) * C)
                        if prompt:
                            s_ap, s_tk = sstate[:, hd, :], sstate
                        else:
                            s_ap, s_tk = sst[:, c, hd, :], sst
                        vn = vns[c % 2]
                        wsb = S.bank()
                        mm(wsb[:C, :128], wT[:, c, :C], s_ap, r=[wT, s_tk], w=[wsb])
                        V_("tensor_tensor", r=[u_, wsb], w=[vn], out=vn[:C, :], in0=u_[:C, c, :], in1=wsb[:C, :128], op=ALU.subtract)
                        mm(ob[:, cs], s_ap, qgT[:, cs], start=True, stop=False, r=[s_tk, qgT], w=[ob])
                        mm(ob[:, cs], vn[:C, :], AT[:C, c, :C], start=False, stop=True, r=[vn, AT], w=[ob])
                        sb_ = S.bank()
                        mm(sb_[:, :128], kd[:C, c, :], vn[:C, :], r=[kd, vn], w=[sb_])
                        V_("scalar_tensor_tensor", r=[s_tk, Eg, sb_], w=[s_tk], out=s_ap, in0=s_ap, scalar=Eg[:, c, C - 1:C],
                           in1=sb_[:, :128], op0=ALU.mult, op1=ALU.add)
                    act(oT[:, :ntok], ob[:, :ntok], AF.Copy, r=[ob], w=[oT])
                    S.bank_release(ob)
                    with S.scope() as mk2:
                        rs = rstd_from(mk2, [oT[:, :ntok]], [oT], ntok, 128.0)
                        V_("tensor_tensor", r=[oT, rs], w=[oT], out=oT[:, :ntok], in0=oT[:, :ntok], in1=rs[:, :ntok], op=ALU.mult)
                    V_("scalar_tensor_tensor", r=[oT, ggain, zs], w=[brT], out=brT[:, hd, :ntok], in0=oT[:, :ntok], scalar=ggain[:, 0:1],
                       in1=zs[:, :ntok], op0=ALU.mult, op1=ALU.mult)
                    if last:
                        dma("sp", O["gdnp"][l, hd], sstate[:, hd, :], r=[sstate], w=[])
                if not prompt:
                    for s in range(NS):
                        dma("sp", O["gdns"][l, s].rearrange("h k v -> k h v"), sst[:, s, :, :], r=[sst], w=[])

        def attn_prompt(l, b, brT):
            last = (b == NB - 1)
            wq_ = I["w_in"][l][:, O_QB:O_QB + 1536].rearrange("(kc p) (g h c) -> p kc g h c", p=128, g=3, h=4)
            wk_ = I["w_in"][l][:, O_KB:O_KB + 1536].rearrange("(kc p) (g h c) -> p kc g h c", p=128, g=3, h=4)
            wv_ = I["w_in"][l][:, O_VB:O_VB + 1536].rearrange("(kc p) (g h c) -> p kc g h c", p=128, g=3, h=4)
            need = [(b + 1) * 512 > T - PW[g] for g in range(3)]
            t0 = [max(0, 4 * b - 1), max(0, 4 * b - 4), 0]
            nh = [4 * b - t0[g] for g in range(3)]
            hoff = [0, nh[0], nh[0] + nh[1]]
            nhT = sum(nh)
            with S.scope() as mk:
                qT = mk([128, 3, 512], BF16, "aq")
                kTc = mk([128, 3, 512], BF16, "ak")
                Vc = mk([128, 4, 3, 128], BF16, "av")
                kTh = mk([128, max(nhT, 1) * 128], BF16, "akh")
                Vh = mk([128, max(nhT, 1), 128], BF16, "avh")
                kst = [mk([128, 3, 128], F32, "kst%d" % i) for i in range(2)]
                Pts = [mk([128, 512], BF16, "Pt%d" % i) for i in range(3)]
                rec = mk([128, 512], F32, "rec")
                pcount = 0
                for hs in range(4):
                    wbq, (qv,) = wload([wq_[:, :, :, hs, :]], ("aq", l, hs))
                    for g in range(3):
                        pb = proj_fm(lambda kc: qv[:, kc, g, :], 512, wbq)
                        act(qT[:, g, :], pb[:], AF.Copy, r=[pb], w=[qT])
                    _chk(1.2)
                    wbk, (kv,) = wload([wk_[:, :, :, hs, :]], ("ak", l, hs))
                    for g in range(3):
                        pb = proj_fm(lambda kc: kv[:, kc, g, :], 512, wbk)
                        act(kTc[:, g, :], pb[:], AF.Copy, r=[pb], w=[kTc])
                        if not last:
                            dma("sp", kTs[g, hs, :, b * 512:(b + 1) * 512], kTc[:, g, :], r=[kTc], w=[d_kTs], append=True)
                    if any(need):
                        for tt in range(4):
                            pb = S.bank()
                            for kc in range(KC):
                                mm(pb[:, :384], hT[:, kc, tt * 128:(tt + 1) * 128], kv[:, kc].rearrange("p g c -> p (g c)"),
                                   start=(kc == 0), stop=(kc == KC - 1), r=[hT, wbk], w=[pb])
                            st = kst[tt % 2]
                            act(st[:].rearrange("p g c -> p (g c)"), pb[:, :384], AF.Copy, r=[pb], w=[st])
                            tok0 = b * 512 + tt * 128
                            for g in range(3):
                                r0 = tok0 - (T - PW[g])
                                if r0 >= 0:
                                    dma("sp", WP[g][l, r0:r0 + 128, hs * 128:(hs + 1) * 128], st[:, g, :], r=[st], w=[])
                    _chk(1.4)
                    wbv, (vv,) = wload([wv_[:, :, :, hs, :]], ("av", l, hs))
                    for tt in range(4):
                        pb = S.bank()
                        for kc in range(KC):
                            mm(pb[:, :384], hT[:, kc, tt * 128:(tt + 1) * 128], vv[:, kc].rearrange("p g c -> p (g c)"),
                               start=(kc == 0), stop=(kc == KC - 1), r=[hT, wbv], w=[pb])
                        V_("tensor_copy", r=[pb], w=[Vc], out=Vc[:, tt, :, :].rearrange("p g c -> p (g c)"), in_=pb[:, :384])
                        tok0 = b * 512 + tt * 128
                        if any(tok0 - (T - PW[g]) >= 0 for g in range(3)):
                            st = kst[tt % 2]
                            act(st[:].rearrange("p g c -> p (g c)"), pb[:, :384], AF.Copy, r=[pb], w=[st])
                            for g in range(3):
                                r0 = tok0 - (T - PW[g])
                                if r0 >= 0:
                                    dma("sp", WP[g][l, r0:r0 + 128, 512 + hs * 128:512 + (hs + 1) * 128], st[:, g, :], r=[st], w=[])
                    if not last:
                        for g in range(3):
                            dma("sp", Vs[g, hs, b * 512:(b + 1) * 512, :].rearrange("(n p) d -> p n d", p=128), Vc[:, :, g, :], r=[Vc], w=[d_Vs], append=True)
                    for g in range(3):
                        if nh[g] > 0:
                            dma("sp", kTh[:, hoff[g] * 128:(hoff[g] + nh[g]) * 128], kTs[g, hs, :, t0[g] * 128:4 * b * 128], r=[d_kTs], w=[kTh], append=(g > 0 and nh[0] + (nh[1] if g > 1 else 0) > 0))
                            dma("sp", Vh[:, hoff[g]:hoff[g] + nh[g], :], Vs[g, hs, t0[g] * 128:4 * b * 128, :].rearrange("(n p) d -> p n d", p=128),
                                r=[d_Vs], w=[Vh], append=(g > 0 and nh[0] + (nh[1] if g > 1 else 0) > 0))
                    _chk(1.6)
                    ob = S.bank_reserve()
                    db = S.bank_reserve()
                    for qt in range(4):
                        qa = 4 * b + qt
                        pairs = []
                        for tt_ in (qa - 1, qa):
                            if tt_ >= 0:
                                pairs.append((0, tt_, 1 if tt_ == qa - 1 else 0))
                        for tt_ in range(qa - 4, qa + 1):
                            if tt_ >= 0:
                                dl = qa - tt_
                                pairs.append((1, tt_, 2 if dl == 0 else (4 if dl == 4 else 3)))
                        for tt_ in range(0, qa + 1):
                            pairs.append((2, tt_, 5 if tt_ == qa else 6))
                        npair = len(pairs)
                        done = 0
                        for i0 in range(0, npair, 4):
                            grp = pairs[i0:i0 + 4]
                            sb = S.bank()
                            for i, (g, tt_, m) in enumerate(grp):
                                if tt_ >= 4 * b:
                                    klhs, ktk = kTc[:, g, (tt_ - 4 * b) * 128:(tt_ - 4 * b + 1) * 128], kTc
                                else:
                                    hh = hoff[g] + tt_ - t0[g]
                                    klhs, ktk = kTh[:, hh * 128:(hh + 1) * 128], kTh
                                mm(sb[:, i * 128:(i + 1) * 128], klhs, qT[:, g, qt * 128:(qt + 1) * 128], r=[ktk, qT], w=[sb])
                            Pt = Pts[pcount % 3]
                            pcount += 1
                            n = len(grp) * 128
                            act(Pt[:, :n], sb[:, :n], AF.Exp, r=[sb], w=[Pt], scale=ISQ)
                            for i, (g, tt_, m) in enumerate(grp):
                                V_("tensor_tensor", r=[Pt, amask], w=[Pt], out=Pt[:, i * 128:(i + 1) * 128], in0=Pt[:, i * 128:(i + 1) * 128],
                                   in1=amask[:, m, :], op=ALU.mult)
                            for i, (g, tt_, m) in enumerate(grp):
                                if tt_ >= 4 * b:
                                    vl, vtk = Vc[:, tt_ - 4 * b, g, :], Vc
                                else:
                                    vl, vtk = Vh[:, hoff[g] + tt_ - t0[g], :], Vh
                                first = (done == 0)
                                lastp = (done == npair - 1)
                                mm(ob[:, qt * 128:(qt + 1) * 128], vl, Pt[:, i * 128:(i + 1) * 128], start=first, stop=lastp, r=[vtk, Pt], w=[ob])
                                mm(db[:, qt * 128:(qt + 1) * 128], ones_b[:], Pt[:, i * 128:(i + 1) * 128], start=first, stop=lastp,
                                   r=[ones_b, Pt], w=[db])
                                done += 1
                    _chk(1.8)
                    V_("reciprocal", r=[db], w=[rec], out=rec[:], in_=db[:])
                    V_("tensor_tensor", r=[ob, rec], w=[brT], out=brT[:, 8 + hs, :], in0=ob[:], in1=rec[:], op=ALU.mult)
                    S.bank_release(ob)
                    S.bank_release(db)

        def attn_sample(l, brT):
            wq_ = I["w_in"][l][:, O_QB:O_QB + 1536].rearrange("(kc p) (g h c) -> p kc g h c", p=128, g=3, h=4)
            wk_ = I["w_in"][l][:, O_KB:O_KB + 1536].rearrange("(kc p) (g h c) -> p kc g h c", p=128, g=3, h=4)
            wv_ = I["w_in"][l][:, O_VB:O_VB + 1536].rearrange("(kc p) (g h c) -> p kc g h c", p=128, g=3, h=4)
            with S.scope() as mk:
                qT = mk([128, 3, NSQ], F32, "sq_")
                kTn = mk([128, 3, NSQ], F32, "sk_")
                Kn = mk([128, NS, 3, 128], F32, "sKn")
                Vn = mk([128, NS, 3, 128], F32, "sVn")
                cks = [mk([128, 9, 2, 128], F32, "ck%d" % i) for i in range(2)]
                kTc = mk([128, 9, 128], F32, "skT")
                Pc = mk([128, 48], F32, "sPc")
                Pn = mk([128, 12], F32, "sPn")
                rec = mk([128, NSQ], F32, "srec")
                ci = 0
                for hs in range(4):
                    wbq, (qv,) = wload([wq_[:, :, :, hs, :]], ("aq", l, hs))
                    for g in range(3):
                        pb = proj_fm(lambda kc: qv[:, kc, g, :], NSQ, wbq)
                        act(qT[:, g, :], pb[:, :NSQ], AF.Copy, r=[pb], w=[qT])
                    wbk, (kv,) = wload([wk_[:, :, :, hs, :]], ("ak", l, hs))
                    for g in range(3):
                        pb = proj_fm(lambda kc: kv[:, kc, g, :], NSQ, wbk)
                        act(kTn[:, g, :], pb[:, :NSQ], AF.Copy, r=[pb], w=[kTn])
                    for s in range(NS):
                        pb = S.bank()
                        for kc in range(KC):
                            mm(pb[:4, :384], hT[:, kc, 4 * s:4 * s + 4], kv[:, kc].rearrange("p g c -> p (g c)"), start=(kc == 0),
                               stop=(kc == KC - 1), r=[hT, wbk], w=[pb])
                        act(Kn[:4, s, :, :].rearrange("p g c -> p (g c)"), pb[:4, :384], AF.Copy, r=[pb], w=[Kn])
                    wbv, (vv,) = wload([wv_[:, :, :, hs, :]], ("av", l, hs))
                    for s in range(NS):
                        pb = S.bank()
                        for kc in range(KC):
                            mm(pb[:4, :384], hT[:, kc, 4 * s:4 * s + 4], vv[:, kc].rearrange("p g c -> p (g c)"), start=(kc == 0),
                               stop=(kc == KC - 1), r=[hT, wbv], w=[pb])
                        act(Vn[:4, s, :, :].rearrange("p g c -> p (g c)"), pb[:4, :384], AF.Copy, r=[pb], w=[Vn])
                    for s in range(NS):
                        for g in range(3):
                            w = WINS[g]
                            dma("sp", WS[g][l, s, w - 4:w, hs * 128:(hs + 1) * 128], Kn[:4, s, g, :], r=[Kn], w=[])
                            dma("sp", WS[g][l, s, w - 4:w, 512 + hs * 128:512 + (hs + 1) * 128], Vn[:4, s, g, :], r=[Vn], w=[])
                    ob = S.bank_reserve()
                    db = S.bank_reserve()
                    for s in range(NS):
                        ck = cks[ci % 2]
                        ci += 1
                        c4 = lambda g: CW[g][l, s].rearrange("r (kv h d) -> r kv h d", kv=2, h=4)[:, :, hs, :]
                        dma("sp", ck[:, 0, :, :], c4(0), w=[ck])
                        for r_ in range(4):
                            dma("sp", ck[:, 1 + r_, :, :], c4(1).rearrange("(m q) kv d -> m q kv d", q=4)[:, r_, :, :], w=[ck], append=True)
                            dma("sp", ck[:, 5 + r_, :, :], c4(2).rearrange("(m q) kv d -> m q kv d", q=16)[:, r_, :, :], w=[ck], append=True)
                        for i0 in (0, 4, 8):
                            n = min(4, 9 - i0)
                            pb = S.bank()
                            for i in range(n):
                                tr(pb[:, i * 128:(i + 1) * 128], ck[:, i0 + i, 0, :], ident[:], r=[ck, ident], w=[pb], sig=(i == n - 1))
                            act(kTc[:, i0:i0 + n, :].rearrange("p a d -> p (a d)"), pb[:, :n * 128], AF.Copy, r=[pb], w=[kTc])
                        sb = S.bank()
                        for i in range(9):
                            g = 0 if i == 0 else (1 if i < 5 else 2)
                            mm(sb[:, i * 4:(i + 1) * 4], kTc[:, i, :], qT[:, g, 4 * s:4 * s + 4], r=[kTc, qT], w=[sb])
                        sb2 = S.bank()
                        for g in range(3):
                            mm(sb2[:4, g * 4:(g + 1) * 4], kTn[:, g, 4 * s:4 * s + 4], qT[:, g, 4 * s:4 * s + 4], r=[kTn, qT], w=[sb2])
                        act(Pc[:, :36], sb[:, :36], AF.Exp, r=[sb], w=[Pc], scale=ISQ)
                        act(Pn[:4, :12], sb2[:4, :12], AF.Exp, r=[sb2], w=[Pn], scale=ISQ)
                        V_("tensor_tensor", r=[Pc, smask], w=[Pc], out=Pc[:, :36], in0=Pc[:, :36], in1=smask[:, 0:9, :].rearrange("p a q -> p (a q)"), op=ALU.mult)
                        V_("tensor_tensor", r=[Pn, smask], w=[Pn], out=Pn[:4, :12], in0=Pn[:4, :12], in1=smask[:4, 9:12, :].rearrange("p a q -> p (a q)"), op=ALU.mult)
                        oc = slice(4 * s, 4 * s + 4)
                        for i in range(9):
                            mm(ob[:, oc], ck[:, i, 1, :], Pc[:, i * 4:(i + 1) * 4], start=(i == 0), stop=False, r=[ck, Pc], w=[ob])
                        for g in range(3):
                            mm(ob[:, oc], Vn[:4, s, g, :], Pn[:4, g * 4:(g + 1) * 4], start=False, stop=(g == 2), r=[Vn, Pn], w=[ob])
                        for i in range(9):
                            mm(db[:, oc], ones_f[:], Pc[:, i * 4:(i + 1) * 4], start=(i == 0), stop=False, r=[ones_f, Pc], w=[db])
                        for g in range(3):
                            mm(db[:, oc], ones_f[:4, :], Pn[:4, g * 4:(g + 1) * 4], start=False, stop=(g == 2), r=[ones_f, Pn], w=[db])
                    V_("reciprocal", r=[db], w=[rec], out=rec[:, :NSQ], in_=db[:, :NSQ])
                    V_("tensor_tensor", r=[ob, rec], w=[brT], out=brT[:, 8 + hs, :NSQ], in0=ob[:, :NSQ], in1=rec[:, :NSQ], op=ALU.mult)
                    S.bank_release(ob)
                    S.bank_release(db)

        def pool_branch(l, b, prompt, brT):
            nseq = 1 if prompt else NS
            Tq = 512 if prompt else 4
            ntok = nseq * Tq
            Lx = 15 + Tq
            with S.scope() as mk:
                phist = None
                if not prompt:
                    praw = mk([128, 1024], F32, "praw")
                    dma("sp", praw[:NS * 15, :], I["spool"][l], w=[praw])
                    phist = mk([128, 8, NS, 15], F32, "phist")
                    pb = S.bank()
                    for t_ in range(8):
                        tr(pb[:, t_ * NS * 15:(t_ + 1) * NS * 15], praw[:NS * 15, t_ * 128:(t_ + 1) * 128], ident[:NS * 15, :NS * 15],
                           r=[praw, ident], w=[pb], sig=(t_ == 7))
                    act(phist[:].rearrange("p t s j -> p (t s j)"), pb[:, :8 * NS * 15], AF.Copy, r=[pb], w=[phist])
                pbuf = mk([128, nseq, Lx], F32, "pbuf")
                Pa = mk([128, nseq, Lx], F32, "Pa")
                Pb = mk([128, nseq, Lx], F32, "Pb")
                pooledT = mk([128, 2, ntok], BF16, "pooledT")
                for half in range(2):
                    wb, (wv,) = wload([win_cols(l, O_UC + 512 * half, 512)], ("uc", l, half))
                    for j in range(4):
                        ct = half * 4 + j
                        gi = ct // 2
                        win = 2 << gi
                        pb = proj_fm(lambda kc: wv[:, kc, j * 128:(j + 1) * 128], ntok, wb)
                        act(pbuf[:, :, 15:Lx], pb[:, :ntok].rearrange("p (s t) -> p s t", s=nseq), AF.Copy, r=[pb], w=[pbuf])
                        if prompt:
                            V_("tensor_copy", r=[ptail], w=[pbuf], out=pbuf[:, 0, 0:15], in_=ptail[:, ct, :])
                        else:
                            V_("tensor_copy", r=[phist], w=[pbuf], out=pbuf[:, :, 0:15], in_=phist[:, ct, :, :])
                        srcb, sh = pbuf, 1
                        res = None
                        for lv in range(gi + 1):
                            dst = Pa if lv % 2 == 0 else Pb
                            lo = 2 * sh - 1
                            V_("tensor_tensor", r=[srcb], w=[dst], out=dst[:, :, lo:Lx], in0=srcb[:, :, lo:Lx], in1=srcb[:, :, lo - sh:Lx - sh], op=ALU.add)
                            srcb, sh, res = dst, sh * 2, dst
                        if prompt and b == 0:
                            V_("tensor_tensor", r=[res, pcorr], w=[res], out=res[:, 0, 15:31], in0=res[:, 0, 15:31], in1=pcorr[:, gi, :], op=ALU.mult)
                        V_("scalar_tensor_tensor", r=[res, pbuf], w=[pooledT], out=pooledT[:, ct % 2, :ntok].rearrange("p (s t) -> p s t", s=nseq),
                           in0=res[:, :, 15:Lx], scalar=1.0 / win, in1=pbuf[:, :, 15:Lx], op0=ALU.mult, op1=ALU.subtract)
                        if prompt:
                            V_("tensor_copy", r=[pbuf], w=[ptail], out=ptail[:, ct, :], in_=pbuf[:, 0, Tq:Tq + 15])
                        if ct % 2 == 1:
                            for ot in range(2):
                                pb2 = S.bank()
                                for kt in range(2):
                                    mm(pb2[:, :ntok], wpool[:, gi, kt, ot * 128:(ot + 1) * 128], pooledT[:, kt, :ntok], start=(kt == 0), stop=(kt == 1),
                                       r=[wpool, pooledT], w=[pb2])
                                V_("tensor_scalar", r=[pb2, pscale], w=[brT], out=brT[:, 12 + 2 * gi + ot, :ntok], in0=pb2[:, :ntok],
                                   scalar1=pscale[:, 2 * gi + ot:2 * gi + ot + 1], scalar2=None, op0=ALU.mult)

        def tails(l, prompt):
            t0 = 496 if prompt else 0
            with S.scope() as mk:
                sts = [mk([16, 512], F32, "tst%d" % i) for i in range(2)]
                for ci_ in range(8):
                    c0 = ci_ * 512 if ci_ < 6 else O_UC + (ci_ - 6) * 512
                    wb, (wv,) = wload([win_cols(l, c0, 512)], ("tail", l, ci_))
                    pb = S.bank()
                    for kc in range(KC):
                        mm(pb[:16, :], hT[:, kc, t0:t0 + 16], wv[:, kc, :], start=(kc == 0), stop=(kc == KC - 1), r=[hT, wb], w=[pb])
                    st = sts[ci_ % 2]
                    act(st[:], pb[:16, :], AF.Copy, r=[pb], w=[st])
                    if prompt:
                        if ci_ < 6:
                            dma("sp", O["convp"][l, :, ci_ * 512:(ci_ + 1) * 512], st[13:16, :], r=[st], w=[])
                        else:
                            dma("sp", O["poolp"][l, :, (ci_ - 6) * 512:(ci_ - 5) * 512], st[1:16, :], r=[st], w=[])
                    else:
                        for s in range(NS):
                            if ci_ < 6:
                                dma("sp", O["convs"][l, s, :, ci_ * 512:(ci_ + 1) * 512], st[4 * s + 1:4 * s + 4, :], r=[st], w=[])
                            else:
                                dma("sp", O["pools"][l, s, 11:15, (ci_ - 6) * 512:(ci_ - 5) * 512], st[4 * s:4 * s + 4, :], r=[st], w=[])

        def emit_block(l, b, prompt):
            ntok = 512 if prompt else NSQ
            nt = 4 if prompt else 1
            rows = 128 if prompt else NSQ
            last = prompt and (b == NB - 1)
            if l == 0:
                src = I["xp"] if prompt else I["xs"]
                with S.scope() as mk:
                    xins = [mk([128, D], F32, "xin%d" % i) for i in range(2)]
                    for tt in range(nt):
                        xin = xins[tt % 2]
                        r0 = b * 512 + tt * 128 if prompt else 0
                        dma("sp", xin[:rows, :], src[r0:r0 + rows, :], w=[xin])
                        for k4 in range(4):
                            pb = S.bank()
                            for j in range(4):
                                kc = k4 * 4 + j
                                tr(pb[:, j * 128:j * 128 + rows], xin[:rows, kc * 128:(kc + 1) * 128], ident[:rows, :rows],
                                   r=[xin, ident], w=[pb], sig=(j == 3))
                            act(xT[:, k4 * 4:(k4 + 1) * 4, tt * 128:tt * 128 + rows], pb[:].rearrange("p (j t) -> p j t", j=4)[:, :, :rows],
                                AF.Copy, r=[pb], w=[xT])
            else:
                if prompt:
                    dma("sp", xT[:], x1T[:, :, b * 512:(b + 1) * 512].rearrange("kc p t -> p kc t"), r=[d_x1T], w=[xT])
                else:
                    dma("sp", xT[:, :, :ntok], xs1T.rearrange("kc p t -> p kc t"), r=[d_xs1T], w=[xT])
            _chk(0.7)
            norm_to_hT(gpre, ntok)
            _chk(1)

            with S.scope() as mkA2:
                brT = mkA2([128, 20, ntok], BF16, "brT")
                mergedT = mkA2([128, KC, ntok], BF16, "mergedT")
                if prompt:
                    attn_prompt(l, b, brT)
                else:
                    attn_sample(l, brT)
                _chk(2)
                gdn(l, b, prompt, brT)
                _chk(3)
                pool_branch(l, b, prompt, brT)
                _chk(4)
                if last or not prompt:
                    tails(l, prompt)
                _chk(5)
                with S.scope() as mk:
                    sg = [mk([128, 512], F32, "sg%d" % i) for i in range(3)]
                    acc = mk([128, 512], F32, "acc")
                    t1 = mk([128, 512], F32, "t1")
                    for dg in range(16):
                        wb, gv = wload([win_cols(l, O_GA + 2048 * i + 128 * dg, 128) for i in range(3)], ("gate", l, dg))
                        wb2, bv = wload([rows_cols(I["w_br_a"][l], dg * 128, 128), rows_cols(I["w_br_b"][l], dg * 128, 128),
                                         rows_cols(I["w_br_c"][l], dg * 128, 128)], ("br", l, dg))
                        for j in range(1):
                            dt_ = dg
                            gps = [proj_fm(lambda kc, i=i: gv[i][:, kc, j * 128:(j + 1) * 128], ntok, wb) for i in range(3)]
                            bps = []
                            for i, (nk, k0) in enumerate(((8, 0), (4, 8), (8, 12))):
                                pb = S.bank()
                                for kk in range(nk):
                                    mm(pb[:, :ntok], bv[i][:, kk, j * 128:(j + 1) * 128], brT[:, k0 + kk, :ntok], start=(kk == 0), stop=(kk == nk - 1),
                                       r=[wb2, brT], w=[pb])
                                bps.append(pb)
                            for i in range(3):
                                act(sg[i][:, :ntok], gps[i][:, :ntok], AF.Sigmoid, r=[gps[i]], w=[sg[i]])
                            V_("tensor_tensor", r=[sg[0], bps[0]], w=[acc], out=acc[:, :ntok], in0=sg[0][:, :ntok], in1=bps[0][:, :ntok], op=ALU.mult)
                            V_("tensor_tensor", r=[sg[1], bps[1]], w=[t1], out=t1[:, :ntok], in0=sg[1][:, :ntok], in1=bps[1][:, :ntok], op=ALU.mult)
                            V_("tensor_tensor", r=[acc, t1], w=[acc], out=acc[:, :ntok], in0=acc[:, :ntok], in1=t1[:, :ntok], op=ALU.add)
                            V_("tensor_tensor", r=[sg[2], bps[2]], w=[t1], out=t1[:, :ntok], in0=sg[2][:, :ntok], in1=bps[2][:, :ntok], op=ALU.mult)
                            V_("tensor_tensor", r=[acc, t1], w=[mergedT], out=mergedT[:, dt_, :ntok], in0=acc[:, :ntok], in1=t1[:, :ntok], op=ALU.add)
                if l == 0 and b == 0 and prompt:
                    for i_ in range(20):
                        dump(i_, brT[:, i_, :], brT)
                    for i_ in range(4):
                        dump(20 + i_, mergedT[:, i_, :], mergedT)
                _chk(6)
                with S.scope() as mk:
                    ytmp = mk([128, KC, ntok], F32, "ytmp")
                    sqs = [mk([128, 512], F32, "sqo%d" % i) for i in range(2)]
                    ssb = S.bank_reserve()
                    for cg in range(4):
                        wb, (wv,) = wload([rows_cols(I["w_out"][l], cg * 512, 512)], ("out", l, cg))
                        for j in range(4):
                            dt_ = cg * 4 + j
                            pb = S.bank()
                            for kc in range(KC):
                                mm(pb[:, :ntok], wv[:, kc, j * 128:(j + 1) * 128], mergedT[:, kc, :ntok], start=(kc == 0), stop=(kc == KC - 1),
                                   r=[wb, mergedT], w=[pb])
                            V_("tensor_copy", r=[pb], w=[ytmp], out=ytmp[:, dt_, :ntok], in_=pb[:, :ntok])
                            sq = sqs[dt_ % 2]
                            act(sq[:, :ntok], pb[:, :ntok], AF.Square, r=[pb], w=[sq])
                            mm(ssb[:, :ntok], ones_f[:], sq[:, :ntok], start=(dt_ == 0), stop=(dt_ == KC - 1), r=[ones_f, sq], w=[ssb], sig=True)
                    rstd = mk([128, 512], F32, "rstdo")
                    V_("tensor_scalar", r=[ssb], w=[rstd], out=rstd[:, :ntok], in0=ssb[:, :ntok], scalar1=1.0 / D, scalar2=EPS, op0=ALU.mult, op1=ALU.add)
                    act(rstd[:, :ntok], rstd[:, :ntok], AF.Ln, r=[rstd], w=[rstd])
                    act(rstd[:, :ntok], rstd[:, :ntok], AF.Exp, r=[rstd], w=[rstd], scale=-0.5)
                    S.bank_release(ssb)
                    for dt_ in range(KC):
                        V_("scalar_tensor_tensor", r=[ytmp, gpost, rstd], w=[ytmp], out=ytmp[:, dt_, :ntok], in0=ytmp[:, dt_, :ntok],
                           scalar=gpost[:, dt_:dt_ + 1], in1=rstd[:, :ntok], op0=ALU.mult, op1=ALU.mult)
                        V_("tensor_tensor", r=[ytmp, xT], w=[xT], out=xT[:, dt_, :ntok], in0=xT[:, dt_, :ntok], in1=ytmp[:, dt_, :ntok], op=ALU.add)

            if l == 0 and b == 0 and prompt:
                for i_ in range(4):
                    dump(24 + i_, xT[:, i_, :], xT)
            _chk(7)
            norm_to_hT(gpre2, ntok)
            with S.scope() as mk:
                actT = mk([128, FT, ntok], BF16, "actT")
                ytmp = mk([128, KC, ntok], F32, "ytmp2")
                sgs = [mk([128, 512], F32, "sgf%d" % i) for i in range(2)]
                sqs = [mk([128, 512], F32, "sqf%d" % i) for i in range(2)]
                wgu = I["w_gu"][l].rearrange("(kc p) (u n) -> p kc u n", p=128, u=2)
                for fg in range(FT // 2):
                    wb, (wv,) = wload([wgu[:, :, :, fg * 256:(fg + 1) * 256]], ("gu", l, fg))
                    for j in range(2):
                        ft = fg * 2 + j
                        gpb = proj_fm(lambda kc: wv[:, kc, 0, j * 128:(j + 1) * 128], ntok, wb)
                        upb = proj_fm(lambda kc: wv[:, kc, 1, j * 128:(j + 1) * 128], ntok, wb)
                        sg_ = sgs[ft % 2]
                        act(sg_[:, :ntok], gpb[:, :ntok], AF.Silu, r=[gpb], w=[sg_])
                        V_("tensor_tensor", r=[sg_, upb], w=[actT], out=actT[:, ft, :ntok], in0=sg_[:, :ntok], in1=upb[:, :ntok], op=ALU.mult)
                ssb = S.bank_reserve()
                wdn = I["w_down"][l].rearrange("(kc p) n -> p kc n", p=128)
                for dg in range(8):
                    pbs = [S.bank_reserve() for _ in range(2)]
                    for kh in range(2):
                        wb, (wv,) = wload([wdn[:, kh * 22:(kh + 1) * 22, dg * 256:(dg + 1) * 256]], ("dn", l, dg, kh))
                        for j in range(2):
                            for kk in range(22):
                                fk = kh * 22 + kk
                                mm(pbs[j][:, :ntok], wv[:, kk, j * 128:(j + 1) * 128], actT[:, fk, :ntok], start=(fk == 0), stop=(fk == FT - 1),
                                   r=[wb, actT], w=[pbs[j]])
                    for j in range(2):
                        dt_ = dg * 2 + j
                        pb = pbs[j]
                        V_("tensor_copy", r=[pb], w=[ytmp], out=ytmp[:, dt_, :ntok], in_=pb[:, :ntok])
                        sq = sqs[dt_ % 2]
                        act(sq[:, :ntok], pb[:, :ntok], AF.Square, r=[pb], w=[sq])
                        mm(ssb[:, :ntok], ones_f[:], sq[:, :ntok], start=(dt_ == 0), stop=(dt_ == KC - 1), r=[ones_f, sq], w=[ssb], sig=True)
                        S.bank_release(pb)
                rstd = mk([128, 512], F32, "rstdf")
                V_("tensor_scalar", r=[ssb], w=[rstd], out=rstd[:, :ntok], in0=ssb[:, :ntok], scalar1=1.0 / D, scalar2=EPS, op0=ALU.mult, op1=ALU.add)
                act(rstd[:, :ntok], rstd[:, :ntok], AF.Ln, r=[rstd], w=[rstd])
                act(rstd[:, :ntok], rstd[:, :ntok], AF.Exp, r=[rstd], w=[rstd], scale=-0.5)
                S.bank_release(ssb)
                for dt_ in range(KC):
                    V_("scalar_tensor_tensor", r=[ytmp, gpost2, rstd], w=[ytmp], out=ytmp[:, dt_, :ntok], in0=ytmp[:, dt_, :ntok],
                       scalar=gpost2[:, dt_:dt_ + 1], in1=rstd[:, :ntok], op0=ALU.mult, op1=ALU.mult)
                    V_("tensor_tensor", r=[ytmp, xT], w=[xT], out=xT[:, dt_, :ntok], in0=xT[:, dt_, :ntok], in1=ytmp[:, dt_, :ntok], op=ALU.add)

            if l == 0 and b == 0 and prompt:
                for i_ in range(4):
                    dump(28 + i_, xT[:, i_, :], xT)
            _chk(8)
            if l < L - 1:
                if prompt:
                    dma("sp", x1T[:, :, b * 512:(b + 1) * 512].rearrange("kc p t -> p kc t"), xT[:], r=[xT], w=[d_x1T])
                else:
                    dma("sp", xs1T.rearrange("kc p t -> p kc t"), xT[:, :, :ntok], r=[xT], w=[d_xs1T])
            else:
                dst = O["yp"] if prompt else O["ys"]
                with S.scope() as mk:
                    ysts = [mk([128, D], F32, "yst%d" % i) for i in range(2)]
                    for tt in range(nt):
                        yst = ysts[tt % 2]
                        for k4 in range(4):
                            pb = S.bank()
                            for j in range(4):
                                kc = k4 * 4 + j
                                tr(pb[:rows, j * 128:(j + 1) * 128], xT[:, kc, tt * 128:tt * 128 + rows], ident[:], r=[xT, ident], w=[pb], sig=(j == 3))
                            act(yst[:rows, k4 * 512:(k4 + 1) * 512], pb[:rows, :], AF.Copy, r=[pb], w=[yst])
                        r0 = b * 512 + tt * 128 if prompt else 0
                        dma("sp", dst[r0:r0 + rows, :], yst[:rows, :], r=[yst], w=[])

        try:
            _chk(0)
            for l in range(L):
                load_layer_params(l)
                _chk(0.3)
                for tl in (ctail, ptail, sstate):
                    G_("memset", w=[tl], ap=tl[:], constant=0.0)
                _chk(0.5)
                for b in range(NB):
                    emit_block(l, b, True)
                emit_block(l, 0, False)
        except _Stop:
            pass
        S.finish()
        build.ninst = S.ninst
    return nc


def _consts():
    i = np.arange(128)
    ident = np.eye(128, dtype=np.float32)
    umat = (i[:, None] <= i[None, :]).astype(np.float32)
    mS = np.where(i[:, None] > i[None, :], 0.0, NEG).astype(np.float32)
    mI = np.where(i[None, :] >= i[:, None], 0.0, NEG).astype(np.float32)
    rep4 = lambda a: np.ascontiguousarray(np.broadcast_to(a[:, None, :], (128, 4, 128)))
    k = i[:, None]
    q = i[None, :]
    am = np.zeros((128, 7, 128), np.float32)
    am[:, 0] = (q >= k)
    am[:, 1] = (k >= q)
    am[:, 2] = (q >= k) & ((q - k) % 4 == 0)
    am[:, 3] = ((q - k) % 4 == 0)
    am[:, 4] = (k >= q) & ((q - k) % 4 == 0)
    am[:, 5] = (q >= k) & ((q - k) % 16 == 0)
    am[:, 6] = ((q - k) % 16 == 0)
    pc = np.ones((128, 4, 16), np.float32)
    for gi, win in enumerate((2, 4, 8, 16)):
        t = np.arange(16)
        pc[:, gi, :] = (win / np.minimum(win, t + 1))[None, :]
    sm = np.zeros((128, 12, 4), np.float32)
    t = np.arange(4)[None, :]
    sm[:, 0, :] = (i[:, None] >= t)
    for r in range(4):
        sm[:, 1 + r, :] = (t == r)
        sm[:, 5 + r, :] = (t == r)
    rr = np.arange(4)[:, None]
    sm[:4, 9, :] = (rr <= t)
    sm[:4, 10, :] = (rr == t)
    sm[:4, 11, :] = (rr == t)
    return dict(ident=ident, umat=umat, maskS4=rep4(mS), maskI4=rep4(mI), ident4=rep4(ident), amask=am, pcorr=pc, smask=sm)


_CACHE = {}


def run(inp, T, NS, ncores, prompt_of_core, sample_of_core):
    L = inp["w_in"].shape[0]
    key = (T, NS, L)
    if key not in _CACHE:
        _CACHE[key] = build(T, NS, L)
    nc = _CACHE[key]
    f = lambda a: np.ascontiguousarray(np.asarray(a, dtype=np.float32))
    shared = {k: f(inp[k]) for k in ("w_in", "w_pool", "w_br_a", "w_br_b", "w_br_c", "w_out", "w_gu", "w_down")}
    shared["convw"] = f(np.asarray(inp["conv_w"]).reshape(L, 4, 24, 128).transpose(0, 3, 2, 1))
    shared["alog"] = f(np.broadcast_to(np.asarray(inp["a_log"])[:, None, None, :], (L, 128, 4, 8)))
    shared["dtb"] = f(np.broadcast_to(np.asarray(inp["dt_bias"])[:, None, None, :], (L, 128, 4, 8)))
    shared["ggain"] = f(np.asarray(inp["gdn_gain"]).reshape(L, 128, 1))
    shared["pscale"] = f(np.asarray(inp["pool_scale"]).reshape(L, 8, 128).transpose(0, 2, 1))
    for nm, src in (("gpre", "g_pre_mix"), ("gpost", "g_post_mix"), ("gpre2", "g_pre_ffn"), ("gpost2", "g_post_ffn")):
        shared[nm] = f(np.asarray(inp[src]).reshape(L, 16, 128).transpose(0, 2, 1))
    shared.update(_consts())
    in_maps = []
    for c in range(ncores):
        pb = prompt_of_core[c]
        s0 = sample_of_core[c]
        m = dict(shared)
        m["xp"] = f(inp["x_prompt"][pb])
        m["xs"] = f(np.asarray(inp["x_sample"])[s0:s0 + NS].reshape(NS * 4, D))
        m["cw1"] = f(np.asarray(inp["cache_win1"])[:, s0:s0 + NS].reshape(L, NS, 128, 1024))
        m["cw2"] = f(np.asarray(inp["cache_win2"])[:, s0:s0 + NS].reshape(L, NS, 512, 1024))
        m["cw3"] = f(np.asarray(inp["cache_win3"])[:, s0:s0 + NS].reshape(L, NS, 2048, 1024))
        m["sgdn"] = f(np.asarray(inp["state_gdn"])[:, s0:s0 + NS])
        m["sconv"] = f(np.asarray(inp["state_conv"])[:, s0:s0 + NS].reshape(L, NS * 3, 3072))
        m["spool"] = f(np.asarray(inp["state_pool"])[:, s0:s0 + NS].reshape(L, NS * 15, 1024))
        in_maps.append(m)
    res = run_bass_kernel_spmd(nc, in_maps, core_ids=list(range(ncores)))
    return res.results


def kernel(**inputs):
    T, NS, NCORE = 2048, 4, 8
    L = 2
    res = run(inputs, T, NS, NCORE, [c % 4 for c in range(NCORE)], [4 * c for c in range(NCORE)])
    B = 4
    yp = np.stack([res[b]["yp"] for b in range(B)], 0)
    ys = np.concatenate([res[c]["ys"].reshape(NS, 4, D) for c in range(NCORE)], 0)

    def pst(name, shp):
        return np.stack([res[b][name].reshape(shp) for b in range(B)], 1)

    def sst(name, shp):
        return np.concatenate([res[c][name].reshape((L, NS) + shp) for c in range(NCORE)], 1)

    outs = (yp, ys,
            pst("w1p", (L, 128, 2, 4, 128)), pst("w2p", (L, 512, 2, 4, 128)), pst("w3p", (L, 2048, 2, 4, 128)),
            pst("gdnp", (L, 8, 128, 128)), pst("convp", (L, 3, 3072)), pst("poolp", (L, 15, 1024)),
            sst("w1s", (128, 2, 4, 128)), sst("w2s", (512, 2, 4, 128)), sst("w3s", (2048, 2, 4, 128)),
            sst("gdns", (8, 128, 128)), sst("convs", (3, 3072)), sst("pools", (15, 1024)))
    return tuple(np.ascontiguousarray(o, dtype=np.float32) for o in outs)
```

```python
import contextlib
import numpy as np
import concourse.bass as bass
import concourse.mybir as mybir
from concourse.bass_utils import run_bass_kernel_spmd

F32 = mybir.dt.float32
BF16 = mybir.dt.bfloat16
AF = mybir.ActivationFunctionType
ALU = mybir.AluOpType

D = 2048
KC = 16
NIN = 15888
DFF = 5632
FT = DFF // 128
EPS = 1e-6
NEG = -30000.0
O_QA, O_KA, O_VA, O_ZA, O_BA, O_AA = 0, 1024, 2048, 3072, 4096, 4104
O_QB, O_KB, O_VB, O_UC, O_GA, O_GB, O_GC = 4112, 5648, 7184, 8720, 9744, 11792, 13840
WINS = (128, 512, 2048)
DILS = (1, 4, 16)
NDS = 24
ISQ = 128 ** -0.5
DBG_STOP = None
DBG_DUMP = False


class _Stop(Exception):
    pass


def _chk(k):
    if DBG_STOP is not None and DBG_STOP == k:
        raise _Stop()


class Trk:
    def __init__(self, name="t"):
        self.name = name
        self.w = {}
        self.r = {}


class Tile(Trk):
    def __init__(self, name, t):
        super().__init__(name)
        self.t = t

    def __getitem__(self, idx):
        return self.t[idx]


class Sched:
    def __init__(self, nc, es):
        self.nc = nc
        self.es = es
        self.eng = {"pe": nc.tensor, "act": nc.scalar, "dve": nc.vector, "pool": nc.gpsimd, "sp": nc.sync}
        self.sem = {e: es.enter_context(nc.semaphore("s_" + e)) for e in ("pe", "act", "dve", "pool")}
        self.cnt = {e: 0 for e in self.sem}
        self.dsem = [es.enter_context(nc.semaphore("d%d" % i)) for i in range(NDS)]
        self.dcnt = [0] * NDS
        self.dnext = 0
        self.waited = {e: {} for e in self.eng}
        self.released = {}
        self.uid = 0
        self.banks = []
        self.ninst = 0

    def tile(self, stack, shape, dtype, name=None):
        self.uid += 1
        nm = "%s_%d" % (name or "t", self.uid)
        t = stack.enter_context(self.nc.sbuf_tensor(nm, list(shape), dtype))
        tl = Tile(nm, t)
        tl.r = dict(self.released)
        return tl

    def release(self, tiles):
        for tl in tiles:
            for k, v in list(tl.w.items()) + list(tl.r.items()):
                if self.released.get(k, 0) < v:
                    self.released[k] = v

    @contextlib.contextmanager
    def scope(self):
        st = contextlib.ExitStack()
        tiles = []

        def mk(shape, dtype, name=None):
            tl = self.tile(st, shape, dtype, name)
            tiles.append(tl)
            return tl

        try:
            yield mk
        finally:
            self.release(tiles)
            st.close()

    def init_psum(self):
        for i in range(8):
            t = self.es.enter_context(self.nc.psum_tensor("psb%d" % i, [128, 512], F32))
            self.banks.append(Tile("psb%d" % i, t))
            self.banks[-1].psum = True

    def bank(self):
        b = self.banks.pop(0)
        self.banks.append(b)
        return b

    def bank_reserve(self):
        return self.banks.pop(0)

    def bank_release(self, b):
        self.banks.append(b)

    def _semof(self, key):
        if isinstance(key, str):
            return self.sem[key]
        return self.dsem[key[1]]

    def _wait(self, e, key, val):
        if self.waited[e].get(key, 0) >= val:
            return
        self.waited[e][key] = val
        self.eng[e].wait_ge(self._semof(key), val)
        self.ninst += 1

    def _deps(self, e, r, w, append=False):
        deps = {}

        def add(ev, same_ok):
            if ev is None:
                return
            k, v = ev
            if k == e and not same_ok:
                return
            if deps.get(k, 0) < v:
                deps[k] = v

        for b in r:
            for ev in b.w.items():
                add(ev, e != "pe")
            if getattr(b, "psum", False):
                for ev in b.r.items():
                    add(ev, False)
        for b in w:
            if not append:
                for ev in b.w.items():
                    add(ev, False)
            for ev in b.r.items():
                add(ev, False)
        for k, v in deps.items():
            self._wait(e, k, v)

    def _record(self, ev, r, w, append=False):
        k, v = ev
        for b in r:
            if b.r.get(k, 0) < v:
                b.r[k] = v
        for b in w:
            if append:
                b.w[k] = v
            else:
                b.w = {k: v}
            b.r = {}

    def op(self, e, fn, r=(), w=(), sig=True):
        self._deps(e, r, w)
        ins = fn()
        self.ninst += 1
        if sig:
            self.cnt[e] += 1
            ins.then_inc(self.sem[e], 1)
            ev = (e, self.cnt[e])
        else:
            ev = (e, self.cnt[e] + 1)
        self._record(ev, r, w)
        return ins

    def dma(self, q, out, in_, r=(), w=(), append=False):
        self._deps(q, r, w, append)
        j = self.dnext
        self.dnext = (j + 1) % NDS
        if self.dcnt[j] > 0:
            self._wait(q, ("d", j), 16 * self.dcnt[j])
        self.dcnt[j] += 1
        self.eng[q].dma_start(out=out, in_=in_).then_inc(self.dsem[j], 16)
        self.ninst += 1
        self._record((("d", j), 16 * self.dcnt[j]), r, w, append)

    def finish(self):
        for j in range(NDS):
            if self.dcnt[j] > 0:
                self._wait("sp", ("d", j), 16 * self.dcnt[j])
        for e in ("pe", "act", "dve", "pool"):
            if self.cnt[e] > 0:
                self._wait("sp", e, self.cnt[e])

    def mm(self, out, lhsT, rhs, start=True, stop=True, r=(), w=(), sig=None):
        if sig is None:
            sig = stop
        return self.op("pe", lambda: self.nc.tensor.matmul(out, lhsT=lhsT, rhs=rhs, start=start, stop=stop), r, w, sig)

    def tr(self, out, in_, ident, r=(), w=(), sig=True):
        return self.op("pe", lambda: self.nc.tensor.transpose(out=out, in_=in_, identity=ident), r, w, sig)

    def act(self, out, in_, func, r=(), w=(), **kw):
        return self.op("act", lambda: self.nc.scalar.activation(out=out, in_=in_, func=func, **kw), r, w)

    def v(self, name, r=(), w=(), **kw):
        return self.op("dve", lambda: getattr(self.nc.vector, name)(**kw), r, w)

    def g(self, name, r=(), w=(), **kw):
        return self.op("pool", lambda: getattr(self.nc.gpsimd, name)(**kw), r, w)


def build(T, NS, L=2):
    NB = T // 512
    NSQ = NS * 4
    nc = bass.Bass("TRN2", target_bir_lowering=False)

    def din(name, shape, dt=F32):
        return nc.dram_tensor(name, list(shape), dt, kind="ExternalInput").ap()

    def dout(name, shape, dt=F32):
        return nc.dram_tensor(name, list(shape), dt, kind="ExternalOutput").ap()

    def dscr(name, shape, dt=F32):
        return nc.dram_tensor(name, list(shape), dt, kind="Internal").ap()

    I = {}
    I["xp"] = din("xp", [T, D])
    I["xs"] = din("xs", [NSQ, D])
    I["cw1"] = din("cw1", [L, NS, 128, 1024])
    I["cw2"] = din("cw2", [L, NS, 512, 1024])
    I["cw3"] = din("cw3", [L, NS, 2048, 1024])
    I["sgdn"] = din("sgdn", [L, NS, 8, 128, 128])
    I["sconv"] = din("sconv", [L, NS * 3, 3072])
    I["spool"] = din("spool", [L, NS * 15, 1024])
    I["w_in"] = din("w_in", [L, D, NIN])
    I["convw"] = din("convw", [L, 128, 24, 4])
    I["alog"] = din("alog", [L, 128, 4, 8])
    I["dtb"] = din("dtb", [L, 128, 4, 8])
    I["ggain"] = din("ggain", [L, 128, 1])
    I["w_pool"] = din("w_pool", [L, 4, 256, 256])
    I["pscale"] = din("pscale", [L, 128, 8])
    I["w_br_a"] = din("w_br_a", [L, 1024, D])
    I["w_br_b"] = din("w_br_b", [L, 512, D])
    I["w_br_c"] = din("w_br_c", [L, 1024, D])
    I["w_out"] = din("w_out", [L, D, D])
    I["w_gu"] = din("w_gu", [L, D, 2 * DFF])
    I["w_down"] = din("w_down", [L, DFF, D])
    for nm in ("gpre", "gpost", "gpre2", "gpost2"):
        I[nm] = din(nm, [L, 128, 16])
    I["ident"] = din("ident", [128, 128])
    I["umat"] = din("umat", [128, 128])
    I["maskS4"] = din("maskS4", [128, 4, 128])
    I["maskI4"] = din("maskI4", [128, 4, 128])
    I["ident4"] = din("ident4", [128, 4, 128])
    I["amask"] = din("amask", [128, 7, 128])
    I["pcorr"] = din("pcorr", [128, 4, 16])
    I["smask"] = din("smask", [128, 12, 4])

    O = {}
    O["yp"] = dout("yp", [T, D])
    O["ys"] = dout("ys", [NSQ, D])
    PW = [min(w, T) for w in WINS]
    O["w1p"] = dout("w1p", [L, PW[0], 1024])
    O["w2p"] = dout("w2p", [L, PW[1], 1024])
    O["w3p"] = dout("w3p", [L, PW[2], 1024])
    O["gdnp"] = dout("gdnp", [L, 8, 128, 128])
    O["convp"] = dout("convp", [L, 3, 3072])
    O["poolp"] = dout("poolp", [L, 15, 1024])
    O["w1s"] = dout("w1s", [L, NS, 128, 1024])
    O["w2s"] = dout("w2s", [L, NS, 512, 1024])
    O["w3s"] = dout("w3s", [L, NS, 2048, 1024])
    O["gdns"] = dout("gdns", [L, NS, 8, 128, 128])
    O["convs"] = dout("convs", [L, NS, 3, 3072])
    O["pools"] = dout("pools", [L, NS, 15, 1024])
    if DBG_DUMP:
        O["dbg"] = dout("dbg", [40, 128, 512])
    WP = [O["w1p"], O["w2p"], O["w3p"]]
    WS = [O["w1s"], O["w2s"], O["w3s"]]
    CW = [I["cw1"], I["cw2"], I["cw3"]]

    x1T = dscr("x1T", [KC, 128, T])
    xs1T = dscr("xs1T", [KC, 128, NSQ])
    kTs = dscr("kTs", [3, 4, 128, T], BF16)
    Vs = dscr("Vs", [3, 4, T, 128], BF16)
    d_x1T, d_xs1T, d_kTs, d_Vs = Trk("x1T"), Trk("xs1T"), Trk("kTs"), Trk("Vs")

    es = contextlib.ExitStack()
    with es:
        S = Sched(nc, es)
        S.init_psum()
        mm, act, V_, G_, dma, tr = S.mm, S.act, S.v, S.g, S.dma, S.tr

        def gt(shape, dt, name):
            return S.tile(es, shape, dt, name)

        ident = gt([128, 128], F32, "ident")
        umat = gt([128, 128], F32, "umat")
        ones_f = gt([128, 128], F32, "ones_f")
        ones_b = gt([128, 128], BF16, "ones_b")
        one_c = gt([128, 1], F32, "one_c")
        maskS4 = gt([128, 4, 128], F32, "maskS4")
        maskI4 = gt([128, 4, 128], F32, "maskI4")
        ident4 = gt([128, 4, 128], F32, "ident4")
        amask = gt([128, 7, 128], BF16, "amask")
        pcorr = gt([128, 4, 16], F32, "pcorr")
        smask = gt([128, 12, 4], F32, "smask")
        for tl, nm in ((ident, "ident"), (umat, "umat"), (maskS4, "maskS4"), (maskI4, "maskI4"),
                       (ident4, "ident4"), (pcorr, "pcorr"), (smask, "smask")):
            dma("sp", tl[:], I[nm], w=[tl])
        dma("pool", amask[:], I["amask"], w=[amask])
        G_("memset", w=[ones_f], ap=ones_f[:], constant=1.0)
        G_("memset", w=[ones_b], ap=ones_b[:], constant=1.0)
        G_("memset", w=[one_c], ap=one_c[:], constant=1.0)

        d_ws = Trk("ws")
        for l in range(L):
            for g in range(3):
                w = WINS[g]
                for s in range(NS):
                    nch_ = max(1, (w - 4) // 512)
                    rows = [(i * (w - 4)) // nch_ for i in range(nch_ + 1)]
                    for i in range(nch_):
                        dma("act", WS[g][l, s, rows[i]:rows[i + 1], :], CW[g][l, s, 4 + rows[i]:4 + rows[i + 1], :], w=[d_ws])
            for s in range(NS):
                dma("act", O["pools"][l, s, 0:11, :], I["spool"][l, s * 15 + 4:s * 15 + 15, :], w=[d_ws])

        convw = gt([128, 24, 4], F32, "convw")
        alog = gt([128, 4, 8], F32, "alog")
        dtb = gt([128, 4, 8], F32, "dtb")
        negA = gt([128, 4, 8], F32, "negA")
        ggain = gt([128, 1], F32, "ggain")
        pscale = gt([128, 8], F32, "pscale")
        gpre, gpost, gpre2, gpost2 = (gt([128, 16], F32, n) for n in ("gpre", "gpost", "gpre2", "gpost2"))
        wpool = gt([128, 4, 2, 256], BF16, "wpool")
        wba = gt([128, KC, 16], BF16, "wba")

        WB = [gt([128, 8192], BF16, "wb%d" % i) for i in range(2)]
        wstate = {"i": 0}

        wscr_chunks = [dscr("wscr%d" % i, [32, 128, 8192], BF16) for i in range(7)]

        def wscr_at(slot):
            return wscr_chunks[slot // 32][slot % 32]
        wslots = {}

        def wload(parts, key):
            wb = WB[wstate["i"] % 2]
            wstate["i"] += 1
            off = 0
            views = []
            hit = key in wslots
            for ap_ in parts:
                shp = list(ap_.shape)[1:]
                n = int(np.prod(shp))
                flat = wb.t[:, off:off + n]
                if len(shp) == 2:
                    vw = flat.rearrange("p (a b) -> p a b", a=shp[0])
                elif len(shp) == 3:
                    vw = flat.rearrange("p (a b c) -> p a b c", a=shp[0], b=shp[1])
                else:
                    raise ValueError(shp)
                if not hit:
                    if len(shp) == 3:
                        for bi in range(shp[1]):
                            dma("pool", vw[:, :, bi, :], ap_[:, :, bi, :], w=[wb], append=(len(views) > 0 or bi > 0))
                    else:
                        dma("pool", vw, ap_, w=[wb], append=(len(views) > 0))
                views.append(vw)
                off += n
            assert off <= 8192, off
            if hit:
                slot, stk = wslots[key]
                dma("pool", wb.t[:, :off], wscr_at(slot)[:, :off], r=[stk], w=[wb])
            elif key is not None:
                slot = len(wslots)
                assert slot < 7 * 32
                stk = Trk("ws%d" % slot)
                wslots[key] = (slot, stk)
                dma("sp", wscr_at(slot)[:, :off], wb.t[:, :off], r=[wb], w=[stk])
            return wb, views

        def win_cols(l, c0, n):
            return I["w_in"][l][:, c0:c0 + n].rearrange("(kc p) n -> p kc n", p=128)

        def rows_cols(w2d, c0, n):
            return w2d[:, c0:c0 + n].rearrange("(kc p) n -> p kc n", p=128)

        ctail = gt([128, 24, 3], F32, "ctail")
        ptail = gt([128, 8, 15], F32, "ptail")
        sstate = gt([128, 8, 128], F32, "sstate")
        sstate_t = [Trk("sstate%d" % i) for i in range(8)]
        xT = gt([128, KC, 512], F32, "xT")
        hT = gt([128, KC, 512], BF16, "hT")

        def dump(slot, ap_, trk, n=512):
            if not DBG_DUMP:
                return
            with S.scope() as mkd:
                tmp = mkd([128, 512], F32, "dbgt")
                V_("tensor_copy", r=[trk], w=[tmp], out=tmp[:, :n], in_=ap_)
                dma("sp", O["dbg"][slot, :, :n], tmp[:, :n], r=[tmp], w=[])

        def rstd_from(mk, srcs, trks, ntok, div):
            ssb = S.bank_reserve()
            sqs = [mk([128, 512], F32, "sq") for _ in range(2)]
            n = len(srcs)
            for i in range(n):
                sq = sqs[i % 2]
                act(sq[:, :ntok], srcs[i], AF.Square, r=[trks[i]], w=[sq])
                mm(ssb[:, :ntok], ones_f[:], sq[:, :ntok], start=(i == 0), stop=(i == n - 1), r=[ones_f, sq], w=[ssb], sig=True)
            rstd = mk([128, 512], F32, "rstd")
            V_("tensor_scalar", r=[ssb], w=[rstd], out=rstd[:, :ntok], in0=ssb[:, :ntok], scalar1=1.0 / div,
               scalar2=EPS, op0=ALU.mult, op1=ALU.add)
            act(rstd[:, :ntok], rstd[:, :ntok], AF.Ln, r=[rstd], w=[rstd])
            act(rstd[:, :ntok], rstd[:, :ntok], AF.Exp, r=[rstd], w=[rstd], scale=-0.5)
            S.bank_release(ssb)
            return rstd

        def norm_to_hT(gain, ntok):
            with S.scope() as mk:
                rstd = rstd_from(mk, [xT[:, kc, :ntok] for kc in range(KC)], [xT] * KC, ntok, float(D))
                for kc in range(KC):
                    V_("scalar_tensor_tensor", r=[xT, gain, rstd], w=[hT], out=hT[:, kc, :ntok], in0=xT[:, kc, :ntok],
                       scalar=gain[:, kc:kc + 1], in1=rstd[:, :ntok], op0=ALU.mult, op1=ALU.mult)

        def proj_fm(lhs_of_kc, ntok, r_w):
            pb = S.bank()
            for kc in range(KC):
                mm(pb[:, :ntok], lhs_of_kc(kc), hT[:, kc, :ntok], start=(kc == 0), stop=(kc == KC - 1), r=[r_w, hT], w=[pb])
            return pb

        def load_layer_params(l):
            for tl, nm in ((convw, "convw"), (alog, "alog"), (dtb, "dtb"), (ggain, "ggain"), (pscale, "pscale"),
                           (gpre, "gpre"), (gpost, "gpost"), (gpre2, "gpre2"), (gpost2, "gpost2")):
                dma("sp", tl[:], I[nm][l], w=[tl])
            dma("pool", wpool[:], I["w_pool"][l].rearrange("g (kt p) n -> p g kt n", p=128), w=[wpool])
            dma("pool", wba[:], win_cols(l, O_BA, 16), w=[wba])
            act(negA[:], alog[:], AF.Exp, r=[alog], w=[negA])
            V_("tensor_scalar", r=[negA], w=[negA], out=negA[:], in0=negA[:], scalar1=-1.0, scalar2=None, op0=ALU.mult)

        def gdn(l, b, prompt, brT):
            nseq = 1 if prompt else NS
            Tq = 512 if prompt else 4
            C = 128 if prompt else 4
            nch = 4 if prompt else NS
            ntok = nch * C
            nlev = 6 if prompt else 1
            last = prompt and (b == NB - 1)
            with S.scope() as mk:
                chist = sst = None
                if not prompt:
                    chist = mk([128, 24, NS, 3], F32, "chist")
                    with S.scope() as mk3:
                        craw = mk3([NS * 3, 3072], F32, "craw")
                        dma("sp", craw[:NS * 3, :], I["sconv"][l], w=[craw])
                        pb = S.bank()
                        for t_ in range(24):
                            tr(pb[:, t_ * NS * 3:(t_ + 1) * NS * 3], craw[:NS * 3, t_ * 128:(t_ + 1) * 128], ident[:NS * 3, :NS * 3],
                               r=[craw, ident], w=[pb], sig=(t_ == 23))
                        act(chist[:].rearrange("p t s j -> p (t s j)"), pb[:, :24 * NS * 3], AF.Copy, r=[pb], w=[chist])
                    sst = mk([128, NS, 8, 128], F32, "sst")
                    for s in range(NS):
                        dma("sp", sst[:, s, :, :], I["sgdn"][l, s].rearrange("h k v -> k h v"), w=[sst], append=(s > 0))
                gp = S.bank()
                for c in range(nch):
                    for kc in range(KC):
                        mm(gp[:C, c * 16:(c + 1) * 16], hT[:, kc, c * C:(c + 1) * C], wba[:, kc, :], start=(kc == 0),
                           stop=(kc == KC - 1), r=[hT, wba], w=[gp])
                gview = gp[:C, :nch * 16].rearrange("p (c n) -> p c n", n=16)
                sm = lambda nm: mk([128, 4, 8], F32, nm)
                beta, nbeta, spx, gg, gc, ngc, egc, bg = (sm(n) for n in ("beta", "nbeta", "spx", "gg", "gc", "ngc", "egc", "bg"))
                act(beta[:C, :nch, :], gview[:, :, 0:8], AF.Sigmoid, r=[gp], w=[beta])
                V_("tensor_tensor", r=[gp, dtb], w=[spx], out=spx[:C, :nch, :], in0=gview[:, :, 8:16], in1=dtb[:C, :nch, :], op=ALU.add)
                act(spx[:C, :nch, :], spx[:C, :nch, :], AF.Exp, r=[spx], w=[spx])
                act(spx[:C, :nch, :], spx[:C, :nch, :], AF.Ln, r=[spx, one_c], w=[spx], bias=one_c[:C, :])
                V_("tensor_tensor", r=[spx, negA], w=[gg], out=gg[:C, :nch, :], in0=spx[:C, :nch, :], in1=negA[:C, :nch, :], op=ALU.mult)
                gcb = S.bank()
                for c in range(nch):
                    mm(gcb[:C, c * 8:(c + 1) * 8], umat[:C, :C], gg[:C, c, :], r=[umat, gg], w=[gcb])
                act(gc[:C, :nch, :], gcb[:C, :nch * 8].rearrange("p (c n) -> p c n", n=8), AF.Copy, r=[gcb], w=[gc])
                V_("tensor_scalar", r=[gc], w=[ngc], out=ngc[:C, :nch, :], in0=gc[:C, :nch, :], scalar1=-1.0, scalar2=None, op0=ALU.mult)
                V_("tensor_scalar", r=[beta], w=[nbeta], out=nbeta[:C, :nch, :], in0=beta[:C, :nch, :], scalar1=-1.0, scalar2=None, op0=ALU.mult)
                act(egc[:C, :nch, :], gc[:C, :nch, :], AF.Exp, r=[gc], w=[egc])
                V_("tensor_tensor", r=[beta, egc], w=[bg], out=bg[:C, :nch, :], in0=beta[:C, :nch, :], in1=egc[:C, :nch, :], op=ALU.mult)

                wq4 = I["w_in"][l][:, 0:4096].rearrange("(kc p) (j h c) -> p kc j h c", p=128, j=4, h=8)

                def mkset(i):
                    return ([mk([128, 520], F32, "gs%d_%d" % (i, k)) for k in range(13)]
                            + [mk([128, 128], F32, "vn%d_%d" % (i, k)) for k in range(2)] + [mk([128, 4], F32, "egl%d" % i)])

                sets = [mkset(0), mkset(1)]

                def head(hd, st):
                    (s1, s2, s3, s4, s5, s6, s7, s8, s9, s10, s11, s12, s13) = st[:13]
                    vns = st[13:15]
                    egl = st[15]
                    f4 = lambda tl: tl.t[:, :512].rearrange("p (c n) -> p c n", c=4)
                    fl = lambda tl: tl.t[:, :ntok]
                    pre = s1.t[:, :nseq * (3 + Tq)].rearrange("p (s t) -> p s t", s=nseq)
                    cv = s2.t[:, :nseq * Tq].rearrange("p (s t) -> p s t", s=nseq)
                    wb, (wv,) = wload([wq4[:, :, :, hd, :]], ("gdn", l, hd))
                    for j, dst in ((0, s4), (1, s5), (2, s6)):
                        pb = proj_fm(lambda kc: wv[:, kc, j, :], ntok, wb)
                        ctile = j * 8 + hd
                        act(pre[:, :, 3:3 + Tq], pb[:, :ntok].rearrange("p (s t) -> p s t", s=nseq), AF.Copy, r=[pb], w=[s1])
                        if prompt:
                            V_("tensor_copy", r=[ctail], w=[s1], out=pre[:, 0, 0:3], in_=ctail[:, ctile, :])
                        else:
                            V_("tensor_copy", r=[chist], w=[s1], out=pre[:, :, 0:3], in_=chist[:, ctile, :, :])
                        V_("tensor_scalar", r=[s1, convw], w=[s2], out=cv, in0=pre[:, :, 0:Tq], scalar1=convw[:, ctile, 0:1],
                           scalar2=None, op0=ALU.mult)
                        for jj in range(1, 4):
                            V_("scalar_tensor_tensor", r=[s1, convw, s2], w=[s2], out=cv, in0=pre[:, :, jj:jj + Tq],
                               scalar=convw[:, ctile, jj:jj + 1], in1=cv, op0=ALU.mult, op1=ALU.add)
                        if prompt:
                            V_("tensor_copy", r=[s1], w=[ctail], out=ctail[:, ctile, :], in_=pre[:, 0, Tq:Tq + 3])
                        cvf = s2.t[:, :ntok]
                        if j == 2:
                            act(fl(s6), cvf, AF.Silu, r=[s2], w=[s6])
                        else:
                            act(fl(s3), cvf, AF.Silu, r=[s2], w=[s3])
                            with S.scope() as mk2:
                                rs = rstd_from(mk2, [fl(s3)], [s3], ntok, 1.0)
                                V_("scalar_tensor_tensor", r=[s3, rs], w=[dst], out=fl(dst), in0=fl(s3),
                                   scalar=(ISQ if j == 0 else 1.0), in1=rs[:, :ntok], op0=ALU.mult, op1=ALU.mult)
                        yield
                    pb = proj_fm(lambda kc: wv[:, kc, 3, :], ntok, wb)
                    act(fl(s7), pb[:, :ntok], AF.Silu, r=[pb], w=[s7])
                    Gb, Dm, Eg, DT = f4(s1), f4(s2), f4(s3), f4(s9)
                    for c in range(nch):
                        V_("tensor_scalar", r=[ones_f, gg], w=[s1], out=Gb[:C, c, :], in0=ones_f[:C, :], scalar1=gg[:C, c, hd:hd + 1],
                           scalar2=None, op0=ALU.mult)
                    rb = S.bank()
                    for c in range(nch):
                        mm(rb[:, c * 128:c * 128 + C], Gb[:C, c, :], umat[:C, :C], r=[s1, umat], w=[rb])
                    rb3 = rb[:].rearrange("p (c n) -> p c n", c=4)
                    V_("scalar_tensor_tensor", r=[rb, maskS4], w=[s2], out=Dm[:C, :nch, :C], in0=rb3[:C, :nch, :C], scalar=-1.0,
                       in1=maskS4[:C, :nch, :C], op0=ALU.mult, op1=ALU.add)
                    V_("tensor_tensor", r=[rb, maskI4], w=[s9], out=DT[:C, :nch, :C], in0=rb3[:C, :nch, :C], in1=maskI4[:C, :nch, :C], op=ALU.add)
                    act(Eg[:, :nch, :C], rb3[:, :nch, :C], AF.Exp, r=[rb], w=[s3])
                    for c in range(nch):
                        act(Dm[:C, c, :C], Dm[:C, c, :C], AF.Exp, r=[s2, gc], w=[s2], bias=gc[:C, c, hd:hd + 1])
                        act(DT[:C, c, :C], DT[:C, c, :C], AF.Exp, r=[s9, ngc], w=[s9], bias=ngc[:C, c, hd:hd + 1])
                    V_("tensor_copy", r=[s3], w=[egl], out=egl[:, :nch], in_=Eg[:, :nch, C - 1])
                    V_("tensor_tensor", r=[s4, s3], w=[s8], out=fl(s8).rearrange("p (c t) -> p c t", c=nch),
                       in0=fl(s4).rearrange("p (c t) -> p c t", c=nch), in1=Eg[:, :nch, :C], op=ALU.mult)
                    yield
                    W_a, AT = f4(s10), f4(s11)
                    kkb = S.bank()
                    qkb = S.bank()
                    for c in range(nch):
                        cs = slice(c * C, (c + 1) * C)
                        mm(kkb[:C, c * 128:c * 128 + C], s5.t[:, cs], s5.t[:, cs], r=[s5], w=[kkb])
                        mm(qkb[:C, c * 128:c * 128 + C], s5.t[:, cs], s4.t[:, cs], r=[s5, s4], w=[qkb])
                    kk3 = kkb[:].rearrange("p (c n) -> p c n", c=4)
                    qk3 = qkb[:].rearrange("p (c n) -> p c n", c=4)
                    for c in range(nch):
                        V_("scalar_tensor_tensor", r=[kkb, nbeta, s2], w=[s10], out=W_a[:C, c, :C], in0=kk3[:C, c, :C],
                           scalar=nbeta[:C, c, hd:hd + 1], in1=Dm[:C, c, :C], op0=ALU.mult, op1=ALU.mult)
                    V_("tensor_tensor", r=[qkb, s9], w=[s11], out=AT[:C, :nch, :C], in0=qk3[:C, :nch, :C], in1=DT[:C, :nch, :C], op=ALU.mult)
                    yield
                    N_a, RT = f4(s1), f4(s12)
                    tb = S.bank()
                    for c in range(nch):
                        tr(tb[:C, c * 128:c * 128 + C], W_a[:C, c, :C], ident[:C, :C], r=[s10, ident], w=[tb], sig=(c == nch - 1))
                    tb3 = tb[:].rearrange("p (c n) -> p c n", c=4)
                    act(N_a[:C, :nch, :C], tb3[:C, :nch, :C], AF.Copy, r=[tb], w=[s1])
                    V_("tensor_tensor", r=[tb, ident4], w=[s12], out=RT[:C, :nch, :C], in0=tb3[:C, :nch, :C], in1=ident4[:C, :nch, :C], op=ALU.add)
                    yield
                    Wc, Nc, Wn, Nn = s10, s1, s3, s2
                    for k in range(1, nlev + 1):
                        wb_ = S.bank()
                        for c in range(nch):
                            mm(wb_[:C, c * 128:c * 128 + C], f4(Nc)[:C, c, :C], f4(Wc)[:C, c, :C], r=[Nc, Wc], w=[wb_])
                        act(f4(Wn)[:C, :nch, :C], wb_[:].rearrange("p (c n) -> p c n", c=4)[:C, :nch, :C], AF.Copy, r=[wb_], w=[Wn])
                        if k < nlev:
                            nb_ = S.bank()
                            for c in range(nch):
                                mm(nb_[:C, c * 128:c * 128 + C], f4(Wc)[:C, c, :C], f4(Nc)[:C, c, :C], r=[Nc, Wc], w=[nb_])
                            V_("tensor_copy", r=[nb_], w=[Nn], out=f4(Nn)[:C, :nch, :C], in_=nb_[:].rearrange("p (c n) -> p c n", c=4)[:C, :nch, :C])
                        yield
                        pb_ = S.bank()
                        for c in range(nch):
                            mm(pb_[:C, c * 128:c * 128 + C], f4(Wn)[:C, c, :C], RT[:C, c, :C], r=[Wn, s12], w=[pb_])
                        V_("tensor_tensor", r=[pb_, s12], w=[s12], out=RT[:C, :nch, :C], in0=RT[:C, :nch, :C],
                           in1=pb_[:].rearrange("p (c n) -> p c n", c=4)[:C, :nch, :C], op=ALU.add)
                        Wc, Wn = Wn, Wc
                        Nc, Nn = Nn, Nc
                        yield
                    kbg, kd, vb = f4(s4), f4(s10), f4(s13)
                    ktb = S.bank()
                    vtb = S.bank()
                    for c in range(nch):
                        cs = slice(c * C, (c + 1) * C)
                        tr(ktb[:C, c * 128:(c + 1) * 128], s5.t[:, cs], ident[:], r=[s5, ident], w=[ktb], sig=(c == nch - 1))
                    for c in range(nch):
                        cs = slice(c * C, (c + 1) * C)
                        tr(vtb[:C, c * 128:(c + 1) * 128], s6.t[:, cs], ident[:], r=[s6, ident], w=[vtb], sig=(c == nch - 1))
                    for c in range(nch):
                        V_("tensor_scalar", r=[ktb, bg], w=[s4], out=kbg[:C, c, :], in0=ktb[:C, c * 128:(c + 1) * 128],
                           scalar1=bg[:C, c, hd:hd + 1], scalar2=None, op0=ALU.mult)
                        V_("tensor_scalar", r=[ktb, s9], w=[s10], out=kd[:C, c, :], in0=ktb[:C, c * 128:(c + 1) * 128],
                           scalar1=DT[:C, c, C - 1:C], scalar2=None, op0=ALU.mult)
                        V_("tensor_scalar", r=[vtb, beta], w=[s13], out=vb[:C, c, :], in0=vtb[:C, c * 128:(c + 1) * 128],
                           scalar1=beta[:C, c, hd:hd + 1], scalar2=None, op0=ALU.mult)
                    yield
                    u_, wT = f4(s1), f4(s2)
                    ub = S.bank()
                    wtb = S.bank()
                    for c in range(nch):
                        mm(ub[:C, c * 128:(c + 1) * 128], RT[:C, c, :C], vb[:C, c, :], r=[s12, s13], w=[ub])
                    for c in range(nch):
                        mm(wtb[:, c * 128:c * 128 + C], kbg[:C, c, :], RT[:C, c, :C], r=[s12, s4], w=[wtb])
                    act(u_[:C, :nch, :], ub[:].rearrange("p (c n) -> p c n", c=4)[:C, :nch, :], AF.Copy, r=[ub], w=[s1])
                    V_("tensor_copy", r=[wtb], w=[s2], out=wT[:, :nch, :C], in_=wtb[:].rearrange("p (c n) -> p c n", c=4)[:, :nch, :C])
                    yield
                    ob = S.bank_reserve()
                    for c in range(nch):
                        cs = slice(c * C, (c + 1) * C)
                        if prompt:
                            s_ap, s_tk = sstate[:, hd, :], sstate_t[hd]
                        else:
                            s_ap, s_tk = sst[:, c, hd, :], sst
                        vn = vns[c % 2]
                        wsb = S.bank()
                        mm(wsb[:C, :128], wT[:, c, :C], s_ap, r=[s2, s_tk], w=[wsb])
                        V_("tensor_tensor", r=[s1, wsb], w=[vn], out=vn[:C, :], in0=u_[:C, c, :], in1=wsb[:C, :128], op=ALU.subtract)
                        mm(ob[:, cs], s_ap, s8.t[:, cs], start=True, stop=False, r=[s_tk, s8], w=[ob])
                        mm(ob[:, cs], vn[:C, :], AT[:C, c, :C], start=False, stop=True, r=[vn, s11], w=[ob])
                        sb_ = S.bank()
                        mm(sb_[:, :128], kd[:C, c, :], vn[:C, :], r=[s10, vn], w=[sb_])
                        V_("scalar_tensor_tensor", r=[s_tk, egl, sb_], w=[s_tk], out=s_ap, in0=s_ap, scalar=egl[:, c:c + 1],
                           in1=sb_[:, :128], op0=ALU.mult, op1=ALU.add)
                        yield
                    act(fl(s3), ob[:, :ntok], AF.Copy, r=[ob], w=[s3])
                    S.bank_release(ob)
                    with S.scope() as mk2:
                        rs = rstd_from(mk2, [fl(s3)], [s3], ntok, 128.0)
                        V_("tensor_tensor", r=[s3, rs], w=[s3], out=fl(s3), in0=fl(s3), in1=rs[:, :ntok], op=ALU.mult)
                    V_("scalar_tensor_tensor", r=[s3, ggain, s7], w=[brT], out=brT[:, hd, :ntok], in0=fl(s3), scalar=ggain[:, 0:1],
                       in1=fl(s7), op0=ALU.mult, op1=ALU.mult)
                    if last:
                        dma("sp", O["gdnp"][l, hd], sstate[:, hd, :], r=[sstate_t[hd]], w=[])

                for pair in range(4):
                    alive = [head(2 * pair, sets[0]), head(2 * pair + 1, sets[1])]
                    while alive:
                        for g_ in list(alive):
                            try:
                                next(g_)
                            except StopIteration:
                                alive.remove(g_)
                if not prompt:
                    for s in range(NS):
                        dma("sp", O["gdns"][l, s].rearrange("h k v -> k h v"), sst[:, s, :, :], r=[sst], w=[])

        def attn_prompt(l, b, brT):
            last = (b == NB - 1)
            wq_ = I["w_in"][l][:, O_QB:O_QB + 1536].rearrange("(kc p) (g h c) -> p kc g h c", p=128, g=3, h=4)
            wk_ = I["w_in"][l][:, O_KB:O_KB + 1536].rearrange("(kc p) (g h c) -> p kc g h c", p=128, g=3, h=4)
            wv_ = I["w_in"][l][:, O_VB:O_VB + 1536].rearrange("(kc p) (g h c) -> p kc g h c", p=128, g=3, h=4)
            need = [(b + 1) * 512 > T - PW[g] for g in range(3)]
            t0 = [max(0, 4 * b - 1), max(0, 4 * b - 4), 0]
            nh = [4 * b - t0[g] for g in range(3)]
            hoff = [0, nh[0], nh[0] + nh[1]]
            nhT = sum(nh)
            with S.scope() as mk:
                qT = mk([128, 3, 512], BF16, "aq")
                kTc = mk([128, 3, 512], BF16, "ak")
                Vc = mk([128, 4, 3, 128], BF16, "av")
                kTh = mk([128, max(nhT, 1) * 128], BF16, "akh")
                Vh = mk([128, max(nhT, 1), 128], BF16, "avh")
                kst = [mk([128, 3, 128], F32, "kst%d" % i) for i in range(2)]
                Pts = [mk([128, 512], BF16, "Pt%d" % i) for i in range(3)]
                rec = mk([128, 512], F32, "rec")
                pcount = 0
                for hs in range(4):
                    wbq, (qv,) = wload([wq_[:, :, :, hs, :]], ("aq", l, hs))
                    for g in range(3):
                        pb = proj_fm(lambda kc: qv[:, kc, g, :], 512, wbq)
                        act(qT[:, g, :], pb[:], AF.Copy, r=[pb], w=[qT])
                    _chk(1.2)
                    wbk, (kv,) = wload([wk_[:, :, :, hs, :]], ("ak", l, hs))
                    for g in range(3):
                        pb = proj_fm(lambda kc: kv[:, kc, g, :], 512, wbk)
                        act(kTc[:, g, :], pb[:], AF.Copy, r=[pb], w=[kTc])
                        if not last:
                            dma("sp", kTs[g, hs, :, b * 512:(b + 1) * 512], kTc[:, g, :], r=[kTc], w=[d_kTs], append=True)
                    if any(need):
                        for tt in range(4):
                            pb = S.bank()
                            for kc in range(KC):
                                mm(pb[:, :384], hT[:, kc, tt * 128:(tt + 1) * 128], kv[:, kc].rearrange("p g c -> p (g c)"),
                                   start=(kc == 0), stop=(kc == KC - 1), r=[hT, wbk], w=[pb])
                            st = kst[tt % 2]
                            act(st[:].rearrange("p g c -> p (g c)"), pb[:, :384], AF.Copy, r=[pb], w=[st])
                            tok0 = b * 512 + tt * 128
                            for g in range(3):
                                r0 = tok0 - (T - PW[g])
                                if r0 >= 0:
                                    dma("sp", WP[g][l, r0:r0 + 128, hs * 128:(hs + 1) * 128], st[:, g, :], r=[st], w=[])
                    _chk(1.4)
                    wbv, (vv,) = wload([wv_[:, :, :, hs, :]], ("av", l, hs))
                    for tt in range(4):
                        pb = S.bank()
                        for kc in range(KC):
                            mm(pb[:, :384], hT[:, kc, tt * 128:(tt + 1) * 128], vv[:, kc].rearrange("p g c -> p (g c)"),
                               start=(kc == 0), stop=(kc == KC - 1), r=[hT, wbv], w=[pb])
                        V_("tensor_copy", r=[pb], w=[Vc], out=Vc[:, tt, :, :].rearrange("p g c -> p (g c)"), in_=pb[:, :384])
                        tok0 = b * 512 + tt * 128
                        if any(tok0 - (T - PW[g]) >= 0 for g in range(3)):
                            st = kst[tt % 2]
                            act(st[:].rearrange("p g c -> p (g c)"), pb[:, :384], AF.Copy, r=[pb], w=[st])
                            for g in range(3):
                                r0 = tok0 - (T - PW[g])
                                if r0 >= 0:
                                    dma("sp", WP[g][l, r0:r0 + 128, 512 + hs * 128:512 + (hs + 1) * 128], st[:, g, :], r=[st], w=[])
                    if not last:
                        for g in range(3):
                            dma("sp", Vs[g, hs, b * 512:(b + 1) * 512, :].rearrange("(n p) d -> p n d", p=128), Vc[:, :, g, :], r=[Vc], w=[d_Vs], append=True)
                    for g in range(3):
                        if nh[g] > 0:
                            dma("sp", kTh[:, hoff[g] * 128:(hoff[g] + nh[g]) * 128], kTs[g, hs, :, t0[g] * 128:4 * b * 128], r=[d_kTs], w=[kTh], append=(g > 0 and nh[0] + (nh[1] if g > 1 else 0) > 0))
                            dma("sp", Vh[:, hoff[g]:hoff[g] + nh[g], :], Vs[g, hs, t0[g] * 128:4 * b * 128, :].rearrange("(n p) d -> p n d", p=128),
                                r=[d_Vs], w=[Vh], append=(g > 0 and nh[0] + (nh[1] if g > 1 else 0) > 0))
                    _chk(1.6)
                    ob = S.bank_reserve()
                    db = S.bank_reserve()
                    for qt in range(4):
                        qa = 4 * b + qt
                        pairs = []
                        for tt_ in (qa - 1, qa):
                            if tt_ >= 0:
                                pairs.append((0, tt_, 1 if tt_ == qa - 1 else 0))
                        for tt_ in range(qa - 4, qa + 1):
                            if tt_ >= 0:
                                dl = qa - tt_
                                pairs.append((1, tt_, 2 if dl == 0 else (4 if dl == 4 else 3)))
                        for tt_ in range(0, qa + 1):
                            pairs.append((2, tt_, 5 if tt_ == qa else 6))
                        npair = len(pairs)
                        done = 0
                        for i0 in range(0, npair, 4):
                            grp = pairs[i0:i0 + 4]
                            sb = S.bank()
                            for i, (g, tt_, m) in enumerate(grp):
                                if tt_ >= 4 * b:
                                    klhs, ktk = kTc[:, g, (tt_ - 4 * b) * 128:(tt_ - 4 * b + 1) * 128], kTc
                                else:
                                    hh = hoff[g] + tt_ - t0[g]
                                    klhs, ktk = kTh[:, hh * 128:(hh + 1) * 128], kTh
                                mm(sb[:, i * 128:(i + 1) * 128], klhs, qT[:, g, qt * 128:(qt + 1) * 128], r=[ktk, qT], w=[sb])
                            Pt = Pts[pcount % 3]
                            pcount += 1
                            n = len(grp) * 128
                            act(Pt[:, :n], sb[:, :n], AF.Exp, r=[sb], w=[Pt], scale=ISQ)
                            for i, (g, tt_, m) in enumerate(grp):
                                V_("tensor_tensor", r=[Pt, amask], w=[Pt], out=Pt[:, i * 128:(i + 1) * 128], in0=Pt[:, i * 128:(i + 1) * 128],
                                   in1=amask[:, m, :], op=ALU.mult)
                            for i, (g, tt_, m) in enumerate(grp):
                                if tt_ >= 4 * b:
                                    vl, vtk = Vc[:, tt_ - 4 * b, g, :], Vc
                                else:
                                    vl, vtk = Vh[:, hoff[g] + tt_ - t0[g], :], Vh
                                first = (done == 0)
                                lastp = (done == npair - 1)
                                mm(ob[:, qt * 128:(qt + 1) * 128], vl, Pt[:, i * 128:(i + 1) * 128], start=first, stop=lastp, r=[vtk, Pt], w=[ob])
                                mm(db[:, qt * 128:(qt + 1) * 128], ones_b[:], Pt[:, i * 128:(i + 1) * 128], start=first, stop=lastp,
                                   r=[ones_b, Pt], w=[db])
                                done += 1
                    _chk(1.8)
                    V_("reciprocal", r=[db], w=[rec], out=rec[:], in_=db[:])
                    V_("tensor_tensor", r=[ob, rec], w=[brT], out=brT[:, 8 + hs, :], in0=ob[:], in1=rec[:], op=ALU.mult)
                    S.bank_release(ob)
                    S.bank_release(db)

        def attn_sample(l, brT):
            wq_ = I["w_in"][l][:, O_QB:O_QB + 1536].rearrange("(kc p) (g h c) -> p kc g h c", p=128, g=3, h=4)
            wk_ = I["w_in"][l][:, O_KB:O_KB + 1536].rearrange("(kc p) (g h c) -> p kc g h c", p=128, g=3, h=4)
            wv_ = I["w_in"][l][:, O_VB:O_VB + 1536].rearrange("(kc p) (g h c) -> p kc g h c", p=128, g=3, h=4)
            with S.scope() as mk:
                qT = mk([128, 3, NSQ], F32, "sq_")
                kTn = mk([128, 3, NSQ], F32, "sk_")
                Kn = mk([128, NS, 3, 128], F32, "sKn")
                Vn = mk([128, NS, 3, 128], F32, "sVn")
                cks = [mk([128, 9, 2, 128], F32, "ck%d" % i) for i in range(2)]
                kTc = mk([128, 9, 128], F32, "skT")
                Pc = mk([128, 48], F32, "sPc")
                Pn = mk([128, 12], F32, "sPn")
                rec = mk([128, NSQ], F32, "srec")
                ci = 0
                for hs in range(4):
                    wbq, (qv,) = wload([wq_[:, :, :, hs, :]], ("aq", l, hs))
                    for g in range(3):
                        pb = proj_fm(lambda kc: qv[:, kc, g, :], NSQ, wbq)
                        act(qT[:, g, :], pb[:, :NSQ], AF.Copy, r=[pb], w=[qT])
                    wbk, (kv,) = wload([wk_[:, :, :, hs, :]], ("ak", l, hs))
                    for g in range(3):
                        pb = proj_fm(lambda kc: kv[:, kc, g, :], NSQ, wbk)
                        act(kTn[:, g, :], pb[:, :NSQ], AF.Copy, r=[pb], w=[kTn])
                    for s in range(NS):
                        pb = S.bank()
                        for kc in range(KC):
                            mm(pb[:4, :384], hT[:, kc, 4 * s:4 * s + 4], kv[:, kc].rearrange("p g c -> p (g c)"), start=(kc == 0),
                               stop=(kc == KC - 1), r=[hT, wbk], w=[pb])
                        act(Kn[:4, s, :, :].rearrange("p g c -> p (g c)"), pb[:4, :384], AF.Copy, r=[pb], w=[Kn])
                    wbv, (vv,) = wload([wv_[:, :, :, hs, :]], ("av", l, hs))
                    for s in range(NS):
                        pb = S.bank()
                        for kc in range(KC):
                            mm(pb[:4, :384], hT[:, kc, 4 * s:4 * s + 4], vv[:, kc].rearrange("p g c -> p (g c)"), start=(kc == 0),
                               stop=(kc == KC - 1), r=[hT, wbv], w=[pb])
                        act(Vn[:4, s, :, :].rearrange("p g c -> p (g c)"), pb[:4, :384], AF.Copy, r=[pb], w=[Vn])
                    for s in range(NS):
                        for g in range(3):
                            w = WINS[g]
                            dma("sp", WS[g][l, s, w - 4:w, hs * 128:(hs + 1) * 128], Kn[:4, s, g, :], r=[Kn], w=[])
                            dma("sp", WS[g][l, s, w - 4:w, 512 + hs * 128:512 + (hs + 1) * 128], Vn[:4, s, g, :], r=[Vn], w=[])
                    ob = S.bank_reserve()
                    db = S.bank_reserve()
                    for s in range(NS):
                        ck = cks[ci % 2]
                        ci += 1
                        c4 = lambda g: CW[g][l, s].rearrange("r (kv h d) -> r kv h d", kv=2, h=4)[:, :, hs, :]
                        dma("sp", ck[:, 0, :, :], c4(0), w=[ck])
                        for r_ in range(4):
                            dma("sp", ck[:, 1 + r_, :, :], c4(1).rearrange("(m q) kv d -> m q kv d", q=4)[:, r_, :, :], w=[ck], append=True)
                            dma("sp", ck[:, 5 + r_, :, :], c4(2).rearrange("(m q) kv d -> m q kv d", q=16)[:, r_, :, :], w=[ck], append=True)
                        for i0 in (0, 4, 8):
                            n = min(4, 9 - i0)
                            pb = S.bank()
                            for i in range(n):
                                tr(pb[:, i * 128:(i + 1) * 128], ck[:, i0 + i, 0, :], ident[:], r=[ck, ident], w=[pb], sig=(i == n - 1))
                            act(kTc[:, i0:i0 + n, :].rearrange("p a d -> p (a d)"), pb[:, :n * 128], AF.Copy, r=[pb], w=[kTc])
                        sb = S.bank()
                        for i in range(9):
                            g = 0 if i == 0 else (1 if i < 5 else 2)
                            mm(sb[:, i * 4:(i + 1) * 4], kTc[:, i, :], qT[:, g, 4 * s:4 * s + 4], r=[kTc, qT], w=[sb])
                        sb2 = S.bank()
                        for g in range(3):
                            mm(sb2[:4, g * 4:(g + 1) * 4], kTn[:, g, 4 * s:4 * s + 4], qT[:, g, 4 * s:4 * s + 4], r=[kTn, qT], w=[sb2])
                        act(Pc[:, :36], sb[:, :36], AF.Exp, r=[sb], w=[Pc], scale=ISQ)
                        act(Pn[:4, :12], sb2[:4, :12], AF.Exp, r=[sb2], w=[Pn], scale=ISQ)
                        V_("tensor_tensor", r=[Pc, smask], w=[Pc], out=Pc[:, :36], in0=Pc[:, :36], in1=smask[:, 0:9, :].rearrange("p a q -> p (a q)"), op=ALU.mult)
                        V_("tensor_tensor", r=[Pn, smask], w=[Pn], out=Pn[:4, :12], in0=Pn[:4, :12], in1=smask[:4, 9:12, :].rearrange("p a q -> p (a q)"), op=ALU.mult)
                        oc = slice(4 * s, 4 * s + 4)
                        for i in range(9):
                            mm(ob[:, oc], ck[:, i, 1, :], Pc[:, i * 4:(i + 1) * 4], start=(i == 0), stop=False, r=[ck, Pc], w=[ob])
                        for g in range(3):
                            mm(ob[:, oc], Vn[:4, s, g, :], Pn[:4, g * 4:(g + 1) * 4], start=False, stop=(g == 2), r=[Vn, Pn], w=[ob])
                        for i in range(9):
                            mm(db[:, oc], ones_f[:], Pc[:, i * 4:(i + 1) * 4], start=(i == 0), stop=False, r=[ones_f, Pc], w=[db])
                        for g in range(3):
                            mm(db[:, oc], ones_f[:4, :], Pn[:4, g * 4:(g + 1) * 4], start=False, stop=(g == 2), r=[ones_f, Pn], w=[db])
                    V_("reciprocal", r=[db], w=[rec], out=rec[:, :NSQ], in_=db[:, :NSQ])
                    V_("tensor_tensor", r=[ob, rec], w=[brT], out=brT[:, 8 + hs, :NSQ], in0=ob[:, :NSQ], in1=rec[:, :NSQ], op=ALU.mult)
                    S.bank_release(ob)
                    S.bank_release(db)

        def pool_branch(l, b, prompt, brT):
            nseq = 1 if prompt else NS
            Tq = 512 if prompt else 4
            ntok = nseq * Tq
            Lx = 15 + Tq
            with S.scope() as mk:
                phist = None
                if not prompt:
                    praw = mk([128, 1024], F32, "praw")
                    dma("sp", praw[:NS * 15, :], I["spool"][l], w=[praw])
                    phist = mk([128, 8, NS, 15], F32, "phist")
                    pb = S.bank()
                    for t_ in range(8):
                        tr(pb[:, t_ * NS * 15:(t_ + 1) * NS * 15], praw[:NS * 15, t_ * 128:(t_ + 1) * 128], ident[:NS * 15, :NS * 15],
                           r=[praw, ident], w=[pb], sig=(t_ == 7))
                    act(phist[:].rearrange("p t s j -> p (t s j)"), pb[:, :8 * NS * 15], AF.Copy, r=[pb], w=[phist])
                pbuf = mk([128, nseq, Lx], F32, "pbuf")
                Pa = mk([128, nseq, Lx], F32, "Pa")
                Pb = mk([128, nseq, Lx], F32, "Pb")
                pooledT = mk([128, 2, ntok], BF16, "pooledT")
                for half in range(2):
                    wb, (wv,) = wload([win_cols(l, O_UC + 512 * half, 512)], ("uc", l, half))
                    for j in range(4):
                        ct = half * 4 + j
                        gi = ct // 2
                        win = 2 << gi
                        pb = proj_fm(lambda kc: wv[:, kc, j * 128:(j + 1) * 128], ntok, wb)
                        act(pbuf[:, :, 15:Lx], pb[:, :ntok].rearrange("p (s t) -> p s t", s=nseq), AF.Copy, r=[pb], w=[pbuf])
                        if prompt:
                            V_("tensor_copy", r=[ptail], w=[pbuf], out=pbuf[:, 0, 0:15], in_=ptail[:, ct, :])
                        else:
                            V_("tensor_copy", r=[phist], w=[pbuf], out=pbuf[:, :, 0:15], in_=phist[:, ct, :, :])
                        srcb, sh = pbuf, 1
                        res = None
                        for lv in range(gi + 1):
                            dst = Pa if lv % 2 == 0 else Pb
                            lo = 2 * sh - 1
                            V_("tensor_tensor", r=[srcb], w=[dst], out=dst[:, :, lo:Lx], in0=srcb[:, :, lo:Lx], in1=srcb[:, :, lo - sh:Lx - sh], op=ALU.add)
                            srcb, sh, res = dst, sh * 2, dst
                        if prompt and b == 0:
                            V_("tensor_tensor", r=[res, pcorr], w=[res], out=res[:, 0, 15:31], in0=res[:, 0, 15:31], in1=pcorr[:, gi, :], op=ALU.mult)
                        V_("scalar_tensor_tensor", r=[res, pbuf], w=[pooledT], out=pooledT[:, ct % 2, :ntok].rearrange("p (s t) -> p s t", s=nseq),
                           in0=res[:, :, 15:Lx], scalar=1.0 / win, in1=pbuf[:, :, 15:Lx], op0=ALU.mult, op1=ALU.subtract)
                        if prompt:
                            V_("tensor_copy", r=[pbuf], w=[ptail], out=ptail[:, ct, :], in_=pbuf[:, 0, Tq:Tq + 15])
                        if ct % 2 == 1:
                            for ot in range(2):
                                pb2 = S.bank()
                                for kt in range(2):
                                    mm(pb2[:, :ntok], wpool[:, gi, kt, ot * 128:(ot + 1) * 128], pooledT[:, kt, :ntok], start=(kt == 0), stop=(kt == 1),
                                       r=[wpool, pooledT], w=[pb2])
                                V_("tensor_scalar", r=[pb2, pscale], w=[brT], out=brT[:, 12 + 2 * gi + ot, :ntok], in0=pb2[:, :ntok],
                                   scalar1=pscale[:, 2 * gi + ot:2 * gi + ot + 1], scalar2=None, op0=ALU.mult)

        def tails(l, prompt):
            t0 = 496 if prompt else 0
            with S.scope() as mk:
                sts = [mk([16, 512], F32, "tst%d" % i) for i in range(2)]
                for ci_ in range(8):
                    c0 = ci_ * 512 if ci_ < 6 else O_UC + (ci_ - 6) * 512
                    wb, (wv,) = wload([win_cols(l, c0, 512)], ("tail", l, ci_))
                    pb = S.bank()
                    for kc in range(KC):
                        mm(pb[:16, :], hT[:, kc, t0:t0 + 16], wv[:, kc, :], start=(kc == 0), stop=(kc == KC - 1), r=[hT, wb], w=[pb])
                    st = sts[ci_ % 2]
                    act(st[:], pb[:16, :], AF.Copy, r=[pb], w=[st])
                    if prompt:
                        if ci_ < 6:
                            dma("sp", O["convp"][l, :, ci_ * 512:(ci_ + 1) * 512], st[13:16, :], r=[st], w=[])
                        else:
                            dma("sp", O["poolp"][l, :, (ci_ - 6) * 512:(ci_ - 5) * 512], st[1:16, :], r=[st], w=[])
                    else:
                        for s in range(NS):
                            if ci_ < 6:
                                dma("sp", O["convs"][l, s, :, ci_ * 512:(ci_ + 1) * 512], st[4 * s + 1:4 * s + 4, :], r=[st], w=[])
                            else:
                                dma("sp", O["pools"][l, s, 11:15, (ci_ - 6) * 512:(ci_ - 5) * 512], st[4 * s:4 * s + 4, :], r=[st], w=[])

        def emit_block(l, b, prompt):
            ntok = 512 if prompt else NSQ
            nt = 4 if prompt else 1
            rows = 128 if prompt else NSQ
            last = prompt and (b == NB - 1)
            if l == 0:
                src = I["xp"] if prompt else I["xs"]
                with S.scope() as mk:
                    xins = [mk([128, D], F32, "xin%d" % i) for i in range(2)]
                    for tt in range(nt):
                        xin = xins[tt % 2]
                        r0 = b * 512 + tt * 128 if prompt else 0
                        dma("sp", xin[:rows, :], src[r0:r0 + rows, :], w=[xin])
                        for k4 in range(4):
                            pb = S.bank()
                            for j in range(4):
                                kc = k4 * 4 + j
                                tr(pb[:, j * 128:j * 128 + rows], xin[:rows, kc * 128:(kc + 1) * 128], ident[:rows, :rows],
                                   r=[xin, ident], w=[pb], sig=(j == 3))
                            act(xT[:, k4 * 4:(k4 + 1) * 4, tt * 128:tt * 128 + rows], pb[:].rearrange("p (j t) -> p j t", j=4)[:, :, :rows],
                                AF.Copy, r=[pb], w=[xT])
            else:
                if prompt:
                    dma("sp", xT[:], x1T[:, :, b * 512:(b + 1) * 512].rearrange("kc p t -> p kc t"), r=[d_x1T], w=[xT])
                else:
                    dma("sp", xT[:, :, :ntok], xs1T.rearrange("kc p t -> p kc t"), r=[d_xs1T], w=[xT])
            _chk(0.7)
            norm_to_hT(gpre, ntok)
            _chk(1)

            with S.scope() as mkA2:
                brT = mkA2([128, 20, ntok], BF16, "brT")
                if prompt:
                    attn_prompt(l, b, brT)
                else:
                    attn_sample(l, brT)
                _chk(2)
                gdn(l, b, prompt, brT)
                _chk(3)
                pool_branch(l, b, prompt, brT)
                _chk(4)
                if last or not prompt:
                    tails(l, prompt)
                _chk(5)
                mergedT = mkA2([128, KC, ntok], BF16, "mergedT")
                with S.scope() as mk:
                    sg = [mk([128, 512], F32, "sg%d" % i) for i in range(3)]
                    acc = mk([128, 512], F32, "acc")
                    t1 = mk([128, 512], F32, "t1")
                    for dg in range(16):
                        wb, gv = wload([win_cols(l, O_GA + 2048 * i + 128 * dg, 128) for i in range(3)], ("gate", l, dg))
                        wb2, bv = wload([rows_cols(I["w_br_a"][l], dg * 128, 128), rows_cols(I["w_br_b"][l], dg * 128, 128),
                                         rows_cols(I["w_br_c"][l], dg * 128, 128)], ("br", l, dg))
                        for j in range(1):
                            dt_ = dg
                            gps = [proj_fm(lambda kc, i=i: gv[i][:, kc, j * 128:(j + 1) * 128], ntok, wb) for i in range(3)]
                            bps = []
                            for i, (nk, k0) in enumerate(((8, 0), (4, 8), (8, 12))):
                                pb = S.bank()
                                for kk in range(nk):
                                    mm(pb[:, :ntok], bv[i][:, kk, j * 128:(j + 1) * 128], brT[:, k0 + kk, :ntok], start=(kk == 0), stop=(kk == nk - 1),
                                       r=[wb2, brT], w=[pb])
                                bps.append(pb)
                            for i in range(3):
                                act(sg[i][:, :ntok], gps[i][:, :ntok], AF.Sigmoid, r=[gps[i]], w=[sg[i]])
                            V_("tensor_tensor", r=[sg[0], bps[0]], w=[acc], out=acc[:, :ntok], in0=sg[0][:, :ntok], in1=bps[0][:, :ntok], op=ALU.mult)
                            V_("tensor_tensor", r=[sg[1], bps[1]], w=[t1], out=t1[:, :ntok], in0=sg[1][:, :ntok], in1=bps[1][:, :ntok], op=ALU.mult)
                            V_("tensor_tensor", r=[acc, t1], w=[acc], out=acc[:, :ntok], in0=acc[:, :ntok], in1=t1[:, :ntok], op=ALU.add)
                            V_("tensor_tensor", r=[sg[2], bps[2]], w=[t1], out=t1[:, :ntok], in0=sg[2][:, :ntok], in1=bps[2][:, :ntok], op=ALU.mult)
                            V_("tensor_tensor", r=[acc, t1], w=[mergedT], out=mergedT[:, dt_, :ntok], in0=acc[:, :ntok], in1=t1[:, :ntok], op=ALU.add)
                if l == 0 and b == 0 and prompt:
                    for i_ in range(20):
                        dump(i_, brT[:, i_, :], brT)
                    for i_ in range(4):
                        dump(20 + i_, mergedT[:, i_, :], mergedT)
                _chk(6)
                with S.scope() as mk:
                    ytmp = mk([128, KC, ntok], F32, "ytmp")
                    sqs = [mk([128, 512], F32, "sqo%d" % i) for i in range(2)]
                    ssb = S.bank_reserve()
                    for cg in range(4):
                        wb, (wv,) = wload([rows_cols(I["w_out"][l], cg * 512, 512)], ("out", l, cg))
                        for j in range(4):
                            dt_ = cg * 4 + j
                            pb = S.bank()
                            for kc in range(KC):
                                mm(pb[:, :ntok], wv[:, kc, j * 128:(j + 1) * 128], mergedT[:, kc, :ntok], start=(kc == 0), stop=(kc == KC - 1),
                                   r=[wb, mergedT], w=[pb])
                            V_("tensor_copy", r=[pb], w=[ytmp], out=ytmp[:, dt_, :ntok], in_=pb[:, :ntok])
                            sq = sqs[dt_ % 2]
                            act(sq[:, :ntok], pb[:, :ntok], AF.Square, r=[pb], w=[sq])
                            mm(ssb[:, :ntok], ones_f[:], sq[:, :ntok], start=(dt_ == 0), stop=(dt_ == KC - 1), r=[ones_f, sq], w=[ssb], sig=True)
                    rstd = mk([128, 512], F32, "rstdo")
                    V_("tensor_scalar", r=[ssb], w=[rstd], out=rstd[:, :ntok], in0=ssb[:, :ntok], scalar1=1.0 / D, scalar2=EPS, op0=ALU.mult, op1=ALU.add)
                    act(rstd[:, :ntok], rstd[:, :ntok], AF.Ln, r=[rstd], w=[rstd])
                    act(rstd[:, :ntok], rstd[:, :ntok], AF.Exp, r=[rstd], w=[rstd], scale=-0.5)
                    S.bank_release(ssb)
                    for dt_ in range(KC):
                        V_("scalar_tensor_tensor", r=[ytmp, gpost, rstd], w=[ytmp], out=ytmp[:, dt_, :ntok], in0=ytmp[:, dt_, :ntok],
                           scalar=gpost[:, dt_:dt_ + 1], in1=rstd[:, :ntok], op0=ALU.mult, op1=ALU.mult)
                        V_("tensor_tensor", r=[ytmp, xT], w=[xT], out=xT[:, dt_, :ntok], in0=xT[:, dt_, :ntok], in1=ytmp[:, dt_, :ntok], op=ALU.add)

            if l == 0 and b == 0 and prompt:
                for i_ in range(4):
                    dump(24 + i_, xT[:, i_, :], xT)
            _chk(7)
            norm_to_hT(gpre2, ntok)
            with S.scope() as mk:
                actT = mk([128, FT, ntok], BF16, "actT")
                ytmp = mk([128, KC, ntok], F32, "ytmp2")
                sgs = [mk([128, 512], F32, "sgf%d" % i) for i in range(2)]
                sqs = [mk([128, 512], F32, "sqf%d" % i) for i in range(2)]
                wgu = I["w_gu"][l].rearrange("(kc p) (u n) -> p kc u n", p=128, u=2)
                for fg in range(FT // 2):
                    wb, (wv,) = wload([wgu[:, :, :, fg * 256:(fg + 1) * 256]], ("gu", l, fg))
                    for j in range(2):
                        ft = fg * 2 + j
                        gpb = proj_fm(lambda kc: wv[:, kc, 0, j * 128:(j + 1) * 128], ntok, wb)
                        upb = proj_fm(lambda kc: wv[:, kc, 1, j * 128:(j + 1) * 128], ntok, wb)
                        sg_ = sgs[ft % 2]
                        act(sg_[:, :ntok], gpb[:, :ntok], AF.Silu, r=[gpb], w=[sg_])
                        V_("tensor_tensor", r=[sg_, upb], w=[actT], out=actT[:, ft, :ntok], in0=sg_[:, :ntok], in1=upb[:, :ntok], op=ALU.mult)
                ssb = S.bank_reserve()
                wdn = I["w_down"][l].rearrange("(kc p) n -> p kc n", p=128)
                for dg in range(8):
                    pbs = [S.bank_reserve() for _ in range(2)]
                    for kh in range(2):
                        wb, (wv,) = wload([wdn[:, kh * 22:(kh + 1) * 22, dg * 256:(dg + 1) * 256]], ("dn", l, dg, kh))
                        for j in range(2):
                            for kk in range(22):
                                fk = kh * 22 + kk
                                mm(pbs[j][:, :ntok], wv[:, kk, j * 128:(j + 1) * 128], actT[:, fk, :ntok], start=(fk == 0), stop=(fk == FT - 1),
                                   r=[wb, actT], w=[pbs[j]])
                    for j in range(2):
                        dt_ = dg * 2 + j
                        pb = pbs[j]
                        V_("tensor_copy", r=[pb], w=[ytmp], out=ytmp[:, dt_, :ntok], in_=pb[:, :ntok])
                        sq = sqs[dt_ % 2]
                        act(sq[:, :ntok], pb[:, :ntok], AF.Square, r=[pb], w=[sq])
                        mm(ssb[:, :ntok], ones_f[:], sq[:, :ntok], start=(dt_ == 0), stop=(dt_ == KC - 1), r=[ones_f, sq], w=[ssb], sig=True)
                        S.bank_release(pb)
                rstd = mk([128, 512], F32, "rstdf")
                V_("tensor_scalar", r=[ssb], w=[rstd], out=rstd[:, :ntok], in0=ssb[:, :ntok], scalar1=1.0 / D, scalar2=EPS, op0=ALU.mult, op1=ALU.add)
                act(rstd[:, :ntok], rstd[:, :ntok], AF.Ln, r=[rstd], w=[rstd])
                act(rstd[:, :ntok], rstd[:, :ntok], AF.Exp, r=[rstd], w=[rstd], scale=-0.5)
                S.bank_release(ssb)
                for dt_ in range(KC):
                    V_("scalar_tensor_tensor", r=[ytmp, gpost2, rstd], w=[ytmp], out=ytmp[:, dt_, :ntok], in0=ytmp[:, dt_, :ntok],
                       scalar=gpost2[:, dt_:dt_ + 1], in1=rstd[:, :ntok], op0=ALU.mult, op1=ALU.mult)
                    V_("tensor_tensor", r=[ytmp, xT], w=[xT], out=xT[:, dt_, :ntok], in0=xT[:, dt_, :ntok], in1=ytmp[:, dt_, :ntok], op=ALU.add)

            if l == 0 and b == 0 and prompt:
                for i_ in range(4):
                    dump(28 + i_, xT[:, i_, :], xT)
            _chk(8)
            if l < L - 1:
                if prompt:
                    dma("sp", x1T[:, :, b * 512:(b + 1) * 512].rearrange("kc p t -> p kc t"), xT[:], r=[xT], w=[d_x1T])
                else:
                    dma("sp", xs1T.rearrange("kc p t -> p kc t"), xT[:, :, :ntok], r=[xT], w=[d_xs1T])
            else:
                dst = O["yp"] if prompt else O["ys"]
                with S.scope() as mk:
                    ysts = [mk([128, D], F32, "yst%d" % i) for i in range(2)]
                    for tt in range(nt):
                        yst = ysts[tt % 2]
                        for k4 in range(4):
                            pb = S.bank()
                            for j in range(4):
                                kc = k4 * 4 + j
                                tr(pb[:rows, j * 128:(j + 1) * 128], xT[:, kc, tt * 128:tt * 128 + rows], ident[:], r=[xT, ident], w=[pb], sig=(j == 3))
                            act(yst[:rows, k4 * 512:(k4 + 1) * 512], pb[:rows, :], AF.Copy, r=[pb], w=[yst])
                        r0 = b * 512 + tt * 128 if prompt else 0
                        dma("sp", dst[r0:r0 + rows, :], yst[:rows, :], r=[yst], w=[])

        try:
            _chk(0)
            for l in range(L):
                load_layer_params(l)
                _chk(0.3)
                for tl in (ctail, ptail):
                    G_("memset", w=[tl], ap=tl[:], constant=0.0)
                G_("memset", w=[sstate] + sstate_t, ap=sstate[:], constant=0.0)
                _chk(0.5)
                for b in range(NB):
                    emit_block(l, b, True)
                emit_block(l, 0, False)
        except _Stop:
            pass
        S.finish()
        build.ninst = S.ninst
    return nc


def _consts():
    i = np.arange(128)
    ident = np.eye(128, dtype=np.float32)
    umat = (i[:, None] <= i[None, :]).astype(np.float32)
    mS = np.where(i[:, None] > i[None, :], 0.0, NEG).astype(np.float32)
    mI = np.where(i[None, :] >= i[:, None], 0.0, NEG).astype(np.float32)
    rep4 = lambda a: np.ascontiguousarray(np.broadcast_to(a[:, None, :], (128, 4, 128)))
    k = i[:, None]
    q = i[None, :]
    am = np.zeros((128, 7, 128), np.float32)
    am[:, 0] = (q >= k)
    am[:, 1] = (k >= q)
    am[:, 2] = (q >= k) & ((q - k) % 4 == 0)
    am[:, 3] = ((q - k) % 4 == 0)
    am[:, 4] = (k >= q) & ((q - k) % 4 == 0)
    am[:, 5] = (q >= k) & ((q - k) % 16 == 0)
    am[:, 6] = ((q - k) % 16 == 0)
    pc = np.ones((128, 4, 16), np.float32)
    for gi, win in enumerate((2, 4, 8, 16)):
        t = np.arange(16)
        pc[:, gi, :] = (win / np.minimum(win, t + 1))[None, :]
    sm = np.zeros((128, 12, 4), np.float32)
    t = np.arange(4)[None, :]
    sm[:, 0, :] = (i[:, None] >= t)
    for r in range(4):
        sm[:, 1 + r, :] = (t == r)
        sm[:, 5 + r, :] = (t == r)
    rr = np.arange(4)[:, None]
    sm[:4, 9, :] = (rr <= t)
    sm[:4, 10, :] = (rr == t)
    sm[:4, 11, :] = (rr == t)
    return dict(ident=ident, umat=umat, maskS4=rep4(mS), maskI4=rep4(mI), ident4=rep4(ident), amask=am, pcorr=pc, smask=sm)


_CACHE = {}


def run(inp, T, NS, ncores, prompt_of_core, sample_of_core):
    L = inp["w_in"].shape[0]
    key = (T, NS, L)
    if key not in _CACHE:
        _CACHE[key] = build(T, NS, L)
    nc = _CACHE[key]
    f = lambda a: np.ascontiguousarray(np.asarray(a, dtype=np.float32))
    shared = {k: f(inp[k]) for k in ("w_in", "w_pool", "w_br_a", "w_br_b", "w_br_c", "w_out", "w_gu", "w_down")}
    shared["convw"] = f(np.asarray(inp["conv_w"]).reshape(L, 4, 24, 128).transpose(0, 3, 2, 1))
    shared["alog"] = f(np.broadcast_to(np.asarray(inp["a_log"])[:, None, None, :], (L, 128, 4, 8)))
    shared["dtb"] = f(np.broadcast_to(np.asarray(inp["dt_bias"])[:, None, None, :], (L, 128, 4, 8)))
    shared["ggain"] = f(np.asarray(inp["gdn_gain"]).reshape(L, 128, 1))
    shared["pscale"] = f(np.asarray(inp["pool_scale"]).reshape(L, 8, 128).transpose(0, 2, 1))
    for nm, src in (("gpre", "g_pre_mix"), ("gpost", "g_post_mix"), ("gpre2", "g_pre_ffn"), ("gpost2", "g_post_ffn")):
        shared[nm] = f(np.asarray(inp[src]).reshape(L, 16, 128).transpose(0, 2, 1))
    shared.update(_consts())
    in_maps = []
    for c in range(ncores):
        pb = prompt_of_core[c]
        s0 = sample_of_core[c]
        m = dict(shared)
        m["xp"] = f(inp["x_prompt"][pb])
        m["xs"] = f(np.asarray(inp["x_sample"])[s0:s0 + NS].reshape(NS * 4, D))
        m["cw1"] = f(np.asarray(inp["cache_win1"])[:, s0:s0 + NS].reshape(L, NS, 128, 1024))
        m["cw2"] = f(np.asarray(inp["cache_win2"])[:, s0:s0 + NS].reshape(L, NS, 512, 1024))
        m["cw3"] = f(np.asarray(inp["cache_win3"])[:, s0:s0 + NS].reshape(L, NS, 2048, 1024))
        m["sgdn"] = f(np.asarray(inp["state_gdn"])[:, s0:s0 + NS])
        m["sconv"] = f(np.asarray(inp["state_conv"])[:, s0:s0 + NS].reshape(L, NS * 3, 3072))
        m["spool"] = f(np.asarray(inp["state_pool"])[:, s0:s0 + NS].reshape(L, NS * 15, 1024))
        in_maps.append(m)
    res = run_bass_kernel_spmd(nc, in_maps, core_ids=list(range(ncores)))
    return res.results


def kernel(**inputs):
    T, NS, NCORE = 2048, 4, 8
    L = 2
    res = run(inputs, T, NS, NCORE, [c % 4 for c in range(NCORE)], [4 * c for c in range(NCORE)])
    B = 4
    yp = np.stack([res[b]["yp"] for b in range(B)], 0)
    ys = np.concatenate([res[c]["ys"].reshape(NS, 4, D) for c in range(NCORE)], 0)

    def pst(name, shp):
        return np.stack([res[b][name].reshape(shp) for b in range(B)], 1)

    def sst(name, shp):
        return np.concatenate([res[c][name].reshape((L, NS) + shp) for c in range(NCORE)], 1)

    outs = (yp, ys,
            pst("w1p", (L, 128, 2, 4, 128)), pst("w2p", (L, 512, 2, 4, 128)), pst("w3p", (L, 2048, 2, 4, 128)),
            pst("gdnp", (L, 8, 128, 128)), pst("convp", (L, 3, 3072)), pst("poolp", (L, 15, 1024)),
            sst("w1s", (128, 2, 4, 128)), sst("w2s", (512, 2, 4, 128)), sst("w3s", (2048, 2, 4, 128)),
            sst("gdns", (8, 128, 128)), sst("convs", (3, 3072)), sst("pools", (15, 1024)))
    return tuple(np.ascontiguousarray(o, dtype=np.float32) for o in outs)
```

```python
import contextlib
import numpy as np
import concourse.bass as bass
import concourse.mybir as mybir
from concourse.bass_utils import run_bass_kernel_spmd

F32 = mybir.dt.float32
BF16 = mybir.dt.bfloat16
AF = mybir.ActivationFunctionType
ALU = mybir.AluOpType

D = 2048
KC = 16
NIN = 15888
DFF = 5632
FT = DFF // 128
EPS = 1e-6
NEG = -30000.0
O_QA, O_KA, O_VA, O_ZA, O_BA, O_AA = 0, 1024, 2048, 3072, 4096, 4104
O_QB, O_KB, O_VB, O_UC, O_GA, O_GB, O_GC = 4112, 5648, 7184, 8720, 9744, 11792, 13840
WINS = (128, 512, 2048)
DILS = (1, 4, 16)
NDS = 24
ISQ = 128 ** -0.5
DBG_STOP = None
DBG_DUMP = False


class _Stop(Exception):
    pass


def _chk(k):
    if DBG_STOP is not None and DBG_STOP == k:
        raise _Stop()


class Trk:
    def __init__(self, name="t"):
        self.name = name
        self.w = {}
        self.r = {}


class Tile(Trk):
    def __init__(self, name, t):
        super().__init__(name)
        self.t = t

    def __getitem__(self, idx):
        return self.t[idx]


class Sched:
    def __init__(self, nc, es):
        self.nc = nc
        self.es = es
        self.eng = {"pe": nc.tensor, "act": nc.scalar, "dve": nc.vector, "pool": nc.gpsimd, "sp": nc.sync}
        self.sem = {e: es.enter_context(nc.semaphore("s_" + e)) for e in ("pe", "act", "dve", "pool")}
        self.cnt = {e: 0 for e in self.sem}
        self.dsem = [es.enter_context(nc.semaphore("d%d" % i)) for i in range(NDS)]
        self.dcnt = [0] * NDS
        self.dnext = 0
        self.waited = {e: {} for e in self.eng}
        self.released = {}
        self.uid = 0
        self.banks = []
        self.ninst = 0

    def tile(self, stack, shape, dtype, name=None):
        self.uid += 1
        nm = "%s_%d" % (name or "t", self.uid)
        t = stack.enter_context(self.nc.sbuf_tensor(nm, list(shape), dtype))
        tl = Tile(nm, t)
        tl.r = dict(self.released)
        return tl

    def release(self, tiles):
        for tl in tiles:
            for k, v in list(tl.w.items()) + list(tl.r.items()):
                if self.released.get(k, 0) < v:
                    self.released[k] = v

    @contextlib.contextmanager
    def scope(self):
        st = contextlib.ExitStack()
        tiles = []

        def mk(shape, dtype, name=None):
            tl = self.tile(st, shape, dtype, name)
            tiles.append(tl)
            return tl

        try:
            yield mk
        finally:
            self.release(tiles)
            st.close()

    def init_psum(self):
        for i in range(8):
            t = self.es.enter_context(self.nc.psum_tensor("psb%d" % i, [128, 512], F32))
            self.banks.append(Tile("psb%d" % i, t))
            self.banks[-1].psum = True

    def bank(self):
        b = self.banks.pop(0)
        self.banks.append(b)
        return b

    def bank_reserve(self):
        return self.banks.pop(0)

    def bank_release(self, b):
        self.banks.append(b)

    def _semof(self, key):
        if isinstance(key, str):
            return self.sem[key]
        return self.dsem[key[1]]

    def _wait(self, e, key, val):
        if self.waited[e].get(key, 0) >= val:
            return
        self.waited[e][key] = val
        self.eng[e].wait_ge(self._semof(key), val)
        self.ninst += 1

    def _deps(self, e, r, w, append=False):
        deps = {}

        def add(ev, same_ok):
            if ev is None:
                return
            k, v = ev
            if k == e and not same_ok:
                return
            if deps.get(k, 0) < v:
                deps[k] = v

        for b in r:
            for ev in b.w.items():
                add(ev, e != "pe")
            if getattr(b, "psum", False):
                for ev in b.r.items():
                    add(ev, False)
        for b in w:
            if not append:
                for ev in b.w.items():
                    add(ev, False)
            for ev in b.r.items():
                add(ev, False)
        for k, v in deps.items():
            self._wait(e, k, v)

    def _record(self, ev, r, w, append=False):
        k, v = ev
        for b in r:
            if b.r.get(k, 0) < v:
                b.r[k] = v
        for b in w:
            if append:
                b.w[k] = v
            else:
                b.w = {k: v}
            b.r = {}

    def op(self, e, fn, r=(), w=(), sig=True):
        self._deps(e, r, w)
        ins = fn()
        self.ninst += 1
        if sig:
            self.cnt[e] += 1
            ins.then_inc(self.sem[e], 1)
            ev = (e, self.cnt[e])
        else:
            ev = (e, self.cnt[e] + 1)
        self._record(ev, r, w)
        return ins

    def dma(self, q, out, in_, r=(), w=(), append=False):
        self._deps(q, r, w, append)
        j = self.dnext
        self.dnext = (j + 1) % NDS
        if self.dcnt[j] > 0:
            self._wait(q, ("d", j), 16 * self.dcnt[j])
        self.dcnt[j] += 1
        self.eng[q].dma_start(out=out, in_=in_).then_inc(self.dsem[j], 16)
        self.ninst += 1
        self._record((("d", j), 16 * self.dcnt[j]), r, w, append)

    def finish(self):
        for j in range(NDS):
            if self.dcnt[j] > 0:
                self._wait("sp", ("d", j), 16 * self.dcnt[j])
        for e in ("pe", "act", "dve", "pool"):
            if self.cnt[e] > 0:
                self._wait("sp", e, self.cnt[e])

    def mm(self, out, lhsT, rhs, start=True, stop=True, r=(), w=(), sig=None):
        if sig is None:
            sig = stop
        return self.op("pe", lambda: self.nc.tensor.matmul(out, lhsT=lhsT, rhs=rhs, start=start, stop=stop), r, w, sig)

    def tr(self, out, in_, ident, r=(), w=(), sig=True):
        return self.op("pe", lambda: self.nc.tensor.transpose(out=out, in_=in_, identity=ident), r, w, sig)

    def act(self, out, in_, func, r=(), w=(), **kw):
        return self.op("act", lambda: self.nc.scalar.activation(out=out, in_=in_, func=func, **kw), r, w)

    def v(self, name, r=(), w=(), **kw):
        return self.op("dve", lambda: getattr(self.nc.vector, name)(**kw), r, w)

    def g(self, name, r=(), w=(), **kw):
        return self.op("pool", lambda: getattr(self.nc.gpsimd, name)(**kw), r, w)


def build(T, NS, L=2):
    NB = T // 512
    NSQ = NS * 4
    nc = bass.Bass("TRN2", target_bir_lowering=False)

    def din(name, shape, dt=F32):
        return nc.dram_tensor(name, list(shape), dt, kind="ExternalInput").ap()

    def dout(name, shape, dt=F32):
        return nc.dram_tensor(name, list(shape), dt, kind="ExternalOutput").ap()

    def dscr(name, shape, dt=F32):
        return nc.dram_tensor(name, list(shape), dt, kind="Internal").ap()

    I = {}
    I["xp"] = din("xp", [T, D])
    I["xs"] = din("xs", [NSQ, D])
    I["cw1"] = din("cw1", [L, NS, 128, 1024])
    I["cw2"] = din("cw2", [L, NS, 512, 1024])
    I["cw3"] = din("cw3", [L, NS, 2048, 1024])
    I["sgdn"] = din("sgdn", [L, NS, 8, 128, 128])
    I["sconv"] = din("sconv", [L, NS * 3, 3072])
    I["spool"] = din("spool", [L, NS * 15, 1024])
    I["w_in"] = din("w_in", [L, D, NIN])
    I["convw"] = din("convw", [L, 128, 24, 4])
    I["alog"] = din("alog", [L, 128, 4, 8])
    I["dtb"] = din("dtb", [L, 128, 4, 8])
    I["ggain"] = din("ggain", [L, 128, 1])
    I["w_pool"] = din("w_pool", [L, 4, 256, 256])
    I["pscale"] = din("pscale", [L, 128, 8])
    I["w_br_a"] = din("w_br_a", [L, 1024, D])
    I["w_br_b"] = din("w_br_b", [L, 512, D])
    I["w_br_c"] = din("w_br_c", [L, 1024, D])
    I["w_out"] = din("w_out", [L, D, D])
    I["w_gu"] = din("w_gu", [L, D, 2 * DFF])
    I["w_down"] = din("w_down", [L, DFF, D])
    for nm in ("gpre", "gpost", "gpre2", "gpost2"):
        I[nm] = din(nm, [L, 128, 16])
    I["ident"] = din("ident", [128, 128])
    I["umat"] = din("umat", [128, 128])
    I["maskS4"] = din("maskS4", [128, 4, 128])
    I["maskI4"] = din("maskI4", [128, 4, 128])
    I["ident4"] = din("ident4", [128, 4, 128])
    I["amask"] = din("amask", [128, 7, 128])
    I["pcorr"] = din("pcorr", [128, 4, 16])
    I["smask"] = din("smask", [128, 12, 4])

    O = {}
    O["yp"] = dout("yp", [T, D])
    O["ys"] = dout("ys", [NSQ, D])
    PW = [min(w, T) for w in WINS]
    O["w1p"] = dout("w1p", [L, PW[0], 1024])
    O["w2p"] = dout("w2p", [L, PW[1], 1024])
    O["w3p"] = dout("w3p", [L, PW[2], 1024])
    O["gdnp"] = dout("gdnp", [L, 8, 128, 128])
    O["convp"] = dout("convp", [L, 3, 3072])
    O["poolp"] = dout("poolp", [L, 15, 1024])
    O["w1s"] = dout("w1s", [L, NS, 128, 1024])
    O["w2s"] = dout("w2s", [L, NS, 512, 1024])
    O["w3s"] = dout("w3s", [L, NS, 2048, 1024])
    O["gdns"] = dout("gdns", [L, NS, 8, 128, 128])
    O["convs"] = dout("convs", [L, NS, 3, 3072])
    O["pools"] = dout("pools", [L, NS, 15, 1024])
    if DBG_DUMP:
        O["dbg"] = dout("dbg", [40, 128, 512])
    WP = [O["w1p"], O["w2p"], O["w3p"]]
    WS = [O["w1s"], O["w2s"], O["w3s"]]
    CW = [I["cw1"], I["cw2"], I["cw3"]]

    x1T = dscr("x1T", [KC, 128, T])
    xs1T = dscr("xs1T", [KC, 128, NSQ])
    kTs = dscr("kTs", [3, 4, 128, T], BF16)
    Vs = dscr("Vs", [3, 4, T, 128], BF16)
    d_x1T, d_xs1T, d_kTs, d_Vs = Trk("x1T"), Trk("xs1T"), Trk("kTs"), Trk("Vs")

    es = contextlib.ExitStack()
    with es:
        S = Sched(nc, es)
        S.init_psum()
        mm, act, V_, G_, dma, tr = S.mm, S.act, S.v, S.g, S.dma, S.tr

        def gt(shape, dt, name):
            return S.tile(es, shape, dt, name)

        ident = gt([128, 128], F32, "ident")
        umat = gt([128, 128], F32, "umat")
        ones_f = gt([128, 128], F32, "ones_f")
        ones_b = gt([128, 128], BF16, "ones_b")
        one_c = gt([128, 1], F32, "one_c")
        maskS4 = gt([128, 4, 128], F32, "maskS4")
        maskI4 = gt([128, 4, 128], F32, "maskI4")
        ident4 = gt([128, 4, 128], F32, "ident4")
        amask = gt([128, 7, 128], BF16, "amask")
        pcorr = gt([128, 4, 16], F32, "pcorr")
        smask = gt([128, 12, 4], F32, "smask")
        for tl, nm in ((ident, "ident"), (umat, "umat"), (maskS4, "maskS4"), (maskI4, "maskI4"),
                       (ident4, "ident4"), (pcorr, "pcorr"), (smask, "smask")):
            dma("sp", tl[:], I[nm], w=[tl])
        dma("pool", amask[:], I["amask"], w=[amask])
        G_("memset", w=[ones_f], ap=ones_f[:], constant=1.0)
        G_("memset", w=[ones_b], ap=ones_b[:], constant=1.0)
        G_("memset", w=[one_c], ap=one_c[:], constant=1.0)

        d_ws = Trk("ws")
        cc_list = []
        for l in range(L):
            for g in range(3):
                w = WINS[g]
                for s in range(NS):
                    nch_ = max(1, (w - 4) // 512)
                    rows = [(i * (w - 4)) // nch_ for i in range(nch_ + 1)]
                    for i in range(nch_):
                        cc_list.append((WS[g][l, s, rows[i]:rows[i + 1], :], CW[g][l, s, 4 + rows[i]:4 + rows[i + 1], :]))
            for s in range(NS):
                cc_list.append((O["pools"][l, s, 0:11, :], I["spool"][l, s * 15 + 4:s * 15 + 15, :]))

        def emit_cache_copies(n):
            for _ in range(n):
                if cc_list:
                    o_, i_ = cc_list.pop(0)
                    dma("act", o_, i_, w=[d_ws], append=True)

        convw = gt([128, 24, 4], F32, "convw")
        alog = gt([128, 4, 8], F32, "alog")
        dtb = gt([128, 4, 8], F32, "dtb")
        negA = gt([128, 4, 8], F32, "negA")
        ggain = gt([128, 1], F32, "ggain")
        pscale = gt([128, 8], F32, "pscale")
        gpre, gpost, gpre2, gpost2 = (gt([128, 16], F32, n) for n in ("gpre", "gpost", "gpre2", "gpost2"))
        wpool = gt([128, 4, 2, 256], BF16, "wpool")
        wba = gt([128, KC, 16], BF16, "wba")

        WB = [gt([128, 8192], BF16, "wb%d" % i) for i in range(2)]
        wstate = {"i": 0}

        wscr_chunks = [dscr("wscr%d" % i, [32, 128, 8192], BF16) for i in range(7)]

        def wscr_at(slot):
            return wscr_chunks[slot // 32][slot % 32]
        wslots = {}

        def wload(parts, key):
            wb = WB[wstate["i"] % 2]
            wstate["i"] += 1
            off = 0
            views = []
            hit = key in wslots
            for ap_ in parts:
                shp = list(ap_.shape)[1:]
                n = int(np.prod(shp))
                flat = wb.t[:, off:off + n]
                if len(shp) == 2:
                    vw = flat.rearrange("p (a b) -> p a b", a=shp[0])
                elif len(shp) == 3:
                    vw = flat.rearrange("p (a b c) -> p a b c", a=shp[0], b=shp[1])
                else:
                    raise ValueError(shp)
                if not hit:
                    if len(shp) == 3:
                        for bi in range(shp[1]):
                            dma("pool", vw[:, :, bi, :], ap_[:, :, bi, :], w=[wb], append=(len(views) > 0 or bi > 0))
                    else:
                        dma("pool", vw, ap_, w=[wb], append=(len(views) > 0))
                views.append(vw)
                off += n
            assert off <= 8192, off
            if hit:
                slot, stk = wslots[key]
                dma("pool", wb.t[:, :off], wscr_at(slot)[:, :off], r=[stk], w=[wb])
            elif key is not None:
                slot = len(wslots)
                assert slot < 7 * 32
                stk = Trk("ws%d" % slot)
                wslots[key] = (slot, stk)
                dma("sp", wscr_at(slot)[:, :off], wb.t[:, :off], r=[wb], w=[stk])
            return wb, views

        def win_cols(l, c0, n):
            return I["w_in"][l][:, c0:c0 + n].rearrange("(kc p) n -> p kc n", p=128)

        def rows_cols(w2d, c0, n):
            return w2d[:, c0:c0 + n].rearrange("(kc p) n -> p kc n", p=128)

        ctail = gt([128, 24, 3], F32, "ctail")
        ptail = gt([128, 8, 15], F32, "ptail")
        sstate = gt([128, 8, 128], F32, "sstate")
        sstate_t = [Trk("sstate%d" % i) for i in range(8)]
        xT = gt([128, KC, 512], F32, "xT")
        hT = gt([128, KC, 512], BF16, "hT")

        def dump(slot, ap_, trk, n=512):
            if not DBG_DUMP:
                return
            with S.scope() as mkd:
                tmp = mkd([128, 512], F32, "dbgt")
                V_("tensor_copy", r=[trk], w=[tmp], out=tmp[:, :n], in_=ap_)
                dma("sp", O["dbg"][slot, :, :n], tmp[:, :n], r=[tmp], w=[])

        def rstd_from(mk, srcs, trks, ntok, div):
            ssb = S.bank_reserve()
            sqs = [mk([128, 512], F32, "sq") for _ in range(2)]
            n = len(srcs)
            for i in range(n):
                sq = sqs[i % 2]
                act(sq[:, :ntok], srcs[i], AF.Square, r=[trks[i]], w=[sq])
                mm(ssb[:, :ntok], ones_f[:], sq[:, :ntok], start=(i == 0), stop=(i == n - 1), r=[ones_f, sq], w=[ssb], sig=True)
            rstd = mk([128, 512], F32, "rstd")
            V_("tensor_scalar", r=[ssb], w=[rstd], out=rstd[:, :ntok], in0=ssb[:, :ntok], scalar1=1.0 / div,
               scalar2=EPS, op0=ALU.mult, op1=ALU.add)
            act(rstd[:, :ntok], rstd[:, :ntok], AF.Ln, r=[rstd], w=[rstd])
            act(rstd[:, :ntok], rstd[:, :ntok], AF.Exp, r=[rstd], w=[rstd], scale=-0.5)
            S.bank_release(ssb)
            return rstd

        def norm_to_hT(gain, ntok):
            with S.scope() as mk:
                rstd = rstd_from(mk, [xT[:, kc, :ntok] for kc in range(KC)], [xT] * KC, ntok, float(D))
                for kc in range(KC):
                    V_("scalar_tensor_tensor", r=[xT, gain, rstd], w=[hT], out=hT[:, kc, :ntok], in0=xT[:, kc, :ntok],
                       scalar=gain[:, kc:kc + 1], in1=rstd[:, :ntok], op0=ALU.mult, op1=ALU.mult)

        def proj_fm(lhs_of_kc, ntok, r_w):
            pb = S.bank()
            for kc in range(KC):
                mm(pb[:, :ntok], lhs_of_kc(kc), hT[:, kc, :ntok], start=(kc == 0), stop=(kc == KC - 1), r=[r_w, hT], w=[pb])
            return pb

        def load_layer_params(l):
            for tl, nm in ((convw, "convw"), (alog, "alog"), (dtb, "dtb"), (ggain, "ggain"), (pscale, "pscale"),
                           (gpre, "gpre"), (gpost, "gpost"), (gpre2, "gpre2"), (gpost2, "gpost2")):
                dma("sp", tl[:], I[nm][l], w=[tl])
            dma("pool", wpool[:], I["w_pool"][l].rearrange("g (kt p) n -> p g kt n", p=128), w=[wpool])
            dma("pool", wba[:], win_cols(l, O_BA, 16), w=[wba])
            act(negA[:], alog[:], AF.Exp, r=[alog], w=[negA])
            V_("tensor_scalar", r=[negA], w=[negA], out=negA[:], in0=negA[:], scalar1=-1.0, scalar2=None, op0=ALU.mult)

        def gdn(l, b, prompt, brT):
            nseq = 1 if prompt else NS
            Tq = 512 if prompt else 4
            C = 128 if prompt else 4
            nch = 4 if prompt else NS
            ntok = nch * C
            nlev = 6 if prompt else 1
            last = prompt and (b == NB - 1)
            with S.scope() as mk:
                chist = sst = None
                if not prompt:
                    chist = mk([128, 24, NS, 3], F32, "chist")
                    with S.scope() as mk3:
                        craw = mk3([NS * 3, 3072], F32, "craw")
                        dma("sp", craw[:NS * 3, :], I["sconv"][l], w=[craw])
                        pb = S.bank()
                        for t_ in range(24):
                            tr(pb[:, t_ * NS * 3:(t_ + 1) * NS * 3], craw[:NS * 3, t_ * 128:(t_ + 1) * 128], ident[:NS * 3, :NS * 3],
                               r=[craw, ident], w=[pb], sig=(t_ == 23))
                        act(chist[:].rearrange("p t s j -> p (t s j)"), pb[:, :24 * NS * 3], AF.Copy, r=[pb], w=[chist])
                    sst = mk([128, NS, 8, 128], F32, "sst")
                    for s in range(NS):
                        dma("sp", sst[:, s, :, :], I["sgdn"][l, s].rearrange("h k v -> k h v"), w=[sst], append=(s > 0))
                gp = S.bank()
                for c in range(nch):
                    for kc in range(KC):
                        mm(gp[:C, c * 16:(c + 1) * 16], hT[:, kc, c * C:(c + 1) * C], wba[:, kc, :], start=(kc == 0),
                           stop=(kc == KC - 1), r=[hT, wba], w=[gp])
                gview = gp[:C, :nch * 16].rearrange("p (c n) -> p c n", n=16)
                sm = lambda nm: mk([128, 4, 8], F32, nm)
                beta, nbeta, spx, gg, gc, ngc, egc, bg = (sm(n) for n in ("beta", "nbeta", "spx", "gg", "gc", "ngc", "egc", "bg"))
                act(beta[:C, :nch, :], gview[:, :, 0:8], AF.Sigmoid, r=[gp], w=[beta])
                V_("tensor_tensor", r=[gp, dtb], w=[spx], out=spx[:C, :nch, :], in0=gview[:, :, 8:16], in1=dtb[:C, :nch, :], op=ALU.add)
                act(spx[:C, :nch, :], spx[:C, :nch, :], AF.Exp, r=[spx], w=[spx])
                act(spx[:C, :nch, :], spx[:C, :nch, :], AF.Ln, r=[spx, one_c], w=[spx], bias=one_c[:C, :])
                V_("tensor_tensor", r=[spx, negA], w=[gg], out=gg[:C, :nch, :], in0=spx[:C, :nch, :], in1=negA[:C, :nch, :], op=ALU.mult)
                gcb = S.bank()
                for c in range(nch):
                    mm(gcb[:C, c * 8:(c + 1) * 8], umat[:C, :C], gg[:C, c, :], r=[umat, gg], w=[gcb])
                act(gc[:C, :nch, :], gcb[:C, :nch * 8].rearrange("p (c n) -> p c n", n=8), AF.Copy, r=[gcb], w=[gc])
                V_("tensor_scalar", r=[gc], w=[ngc], out=ngc[:C, :nch, :], in0=gc[:C, :nch, :], scalar1=-1.0, scalar2=None, op0=ALU.mult)
                V_("tensor_scalar", r=[beta], w=[nbeta], out=nbeta[:C, :nch, :], in0=beta[:C, :nch, :], scalar1=-1.0, scalar2=None, op0=ALU.mult)
                act(egc[:C, :nch, :], gc[:C, :nch, :], AF.Exp, r=[gc], w=[egc])
                V_("tensor_tensor", r=[beta, egc], w=[bg], out=bg[:C, :nch, :], in0=beta[:C, :nch, :], in1=egc[:C, :nch, :], op=ALU.mult)

                wq4 = I["w_in"][l][:, 0:4096].rearrange("(kc p) (j h c) -> p kc j h c", p=128, j=4, h=8)

                def mkset(i):
                    return ([mk([128, 520], F32, "gs%d_%d" % (i, k)) for k in range(13)]
                            + [mk([128, 128], F32, "vn%d_%d" % (i, k)) for k in range(2)] + [mk([128, 4], F32, "egl%d" % i)])

                sets = [mkset(0), mkset(1)]

                def head(hd, st):
                    (s1, s2, s3, s4, s5, s6, s7, s8, s9, s10, s11, s12, s13) = st[:13]
                    vns = st[13:15]
                    egl = st[15]
                    f4 = lambda tl: tl.t[:, :512].rearrange("p (c n) -> p c n", c=4)
                    fl = lambda tl: tl.t[:, :ntok]
                    pre = s1.t[:, :nseq * (3 + Tq)].rearrange("p (s t) -> p s t", s=nseq)
                    cv = s2.t[:, :nseq * Tq].rearrange("p (s t) -> p s t", s=nseq)
                    wb, (wv,) = wload([wq4[:, :, :, hd, :]], ("gdn", l, hd))
                    for j, dst in ((0, s4), (1, s5), (2, s6)):
                        pb = proj_fm(lambda kc: wv[:, kc, j, :], ntok, wb)
                        ctile = j * 8 + hd
                        act(pre[:, :, 3:3 + Tq], pb[:, :ntok].rearrange("p (s t) -> p s t", s=nseq), AF.Copy, r=[pb], w=[s1])
                        if prompt:
                            V_("tensor_copy", r=[ctail], w=[s1], out=pre[:, 0, 0:3], in_=ctail[:, ctile, :])
                        else:
                            V_("tensor_copy", r=[chist], w=[s1], out=pre[:, :, 0:3], in_=chist[:, ctile, :, :])
                        V_("tensor_scalar", r=[s1, convw], w=[s2], out=cv, in0=pre[:, :, 0:Tq], scalar1=convw[:, ctile, 0:1],
                           scalar2=None, op0=ALU.mult)
                        for jj in range(1, 4):
                            V_("scalar_tensor_tensor", r=[s1, convw, s2], w=[s2], out=cv, in0=pre[:, :, jj:jj + Tq],
                               scalar=convw[:, ctile, jj:jj + 1], in1=cv, op0=ALU.mult, op1=ALU.add)
                        if prompt:
                            V_("tensor_copy", r=[s1], w=[ctail], out=ctail[:, ctile, :], in_=pre[:, 0, Tq:Tq + 3])
                        cvf = s2.t[:, :ntok]
                        if j == 2:
                            act(fl(s6), cvf, AF.Silu, r=[s2], w=[s6])
                        else:
                            act(fl(s3), cvf, AF.Silu, r=[s2], w=[s3])
                            with S.scope() as mk2:
                                rs = rstd_from(mk2, [fl(s3)], [s3], ntok, 1.0)
                                V_("scalar_tensor_tensor", r=[s3, rs], w=[dst], out=fl(dst), in0=fl(s3),
                                   scalar=(ISQ if j == 0 else 1.0), in1=rs[:, :ntok], op0=ALU.mult, op1=ALU.mult)
                        yield
                    pb = proj_fm(lambda kc: wv[:, kc, 3, :], ntok, wb)
                    act(fl(s7), pb[:, :ntok], AF.Silu, r=[pb], w=[s7])
                    Gb, Dm, Eg, DT = f4(s1), f4(s2), f4(s3), f4(s9)
                    for c in range(nch):
                        V_("tensor_scalar", r=[ones_f, gg], w=[s1], out=Gb[:C, c, :], in0=ones_f[:C, :], scalar1=gg[:C, c, hd:hd + 1],
                           scalar2=None, op0=ALU.mult)
                    rb = S.bank()
                    for c in range(nch):
                        mm(rb[:, c * 128:c * 128 + C], Gb[:C, c, :], umat[:C, :C], r=[s1, umat], w=[rb])
                    rb3 = rb[:].rearrange("p (c n) -> p c n", c=4)
                    V_("scalar_tensor_tensor", r=[rb, maskS4], w=[s2], out=Dm[:C, :nch, :C], in0=rb3[:C, :nch, :C], scalar=-1.0,
                       in1=maskS4[:C, :nch, :C], op0=ALU.mult, op1=ALU.add)
                    V_("tensor_tensor", r=[rb, maskI4], w=[s9], out=DT[:C, :nch, :C], in0=rb3[:C, :nch, :C], in1=maskI4[:C, :nch, :C], op=ALU.add)
                    act(Eg[:, :nch, :C], rb3[:, :nch, :C], AF.Exp, r=[rb], w=[s3])
                    for c in range(nch):
                        act(Dm[:C, c, :C], Dm[:C, c, :C], AF.Exp, r=[s2, gc], w=[s2], bias=gc[:C, c, hd:hd + 1])
                        act(DT[:C, c, :C], DT[:C, c, :C], AF.Exp, r=[s9, ngc], w=[s9], bias=ngc[:C, c, hd:hd + 1])
                    V_("tensor_copy", r=[s3], w=[egl], out=egl[:, :nch], in_=Eg[:, :nch, C - 1])
                    V_("tensor_tensor", r=[s4, s3], w=[s8], out=fl(s8).rearrange("p (c t) -> p c t", c=nch),
                       in0=fl(s4).rearrange("p (c t) -> p c t", c=nch), in1=Eg[:, :nch, :C], op=ALU.mult)
                    yield
                    W_a, AT = f4(s10), f4(s11)
                    kkb = S.bank()
                    qkb = S.bank()
                    for c in range(nch):
                        cs = slice(c * C, (c + 1) * C)
                        mm(kkb[:C, c * 128:c * 128 + C], s5.t[:, cs], s5.t[:, cs], r=[s5], w=[kkb])
                        mm(qkb[:C, c * 128:c * 128 + C], s5.t[:, cs], s4.t[:, cs], r=[s5, s4], w=[qkb])
                    kk3 = kkb[:].rearrange("p (c n) -> p c n", c=4)
                    qk3 = qkb[:].rearrange("p (c n) -> p c n", c=4)
                    for c in range(nch):
                        V_("scalar_tensor_tensor", r=[kkb, nbeta, s2], w=[s10], out=W_a[:C, c, :C], in0=kk3[:C, c, :C],
                           scalar=nbeta[:C, c, hd:hd + 1], in1=Dm[:C, c, :C], op0=ALU.mult, op1=ALU.mult)
                    V_("tensor_tensor", r=[qkb, s9], w=[s11], out=AT[:C, :nch, :C], in0=qk3[:C, :nch, :C], in1=DT[:C, :nch, :C], op=ALU.mult)
                    yield
                    N_a, RT = f4(s1), f4(s12)
                    tb = S.bank()
                    for c in range(nch):
                        tr(tb[:C, c * 128:c * 128 + C], W_a[:C, c, :C], ident[:C, :C], r=[s10, ident], w=[tb], sig=(c == nch - 1))
                    tb3 = tb[:].rearrange("p (c n) -> p c n", c=4)
                    act(N_a[:C, :nch, :C], tb3[:C, :nch, :C], AF.Copy, r=[tb], w=[s1])
                    V_("tensor_tensor", r=[tb, ident4], w=[s12], out=RT[:C, :nch, :C], in0=tb3[:C, :nch, :C], in1=ident4[:C, :nch, :C], op=ALU.add)
                    yield
                    Wc, Nc, Wn, Nn = s10, s1, s3, s2
                    for k in range(1, nlev + 1):
                        wb_ = S.bank()
                        for c in range(nch):
                            mm(wb_[:C, c * 128:c * 128 + C], f4(Nc)[:C, c, :C], f4(Wc)[:C, c, :C], r=[Nc, Wc], w=[wb_])
                        act(f4(Wn)[:C, :nch, :C], wb_[:].rearrange("p (c n) -> p c n", c=4)[:C, :nch, :C], AF.Copy, r=[wb_], w=[Wn])
                        if k < nlev:
                            nb_ = S.bank()
                            for c in range(nch):
                                mm(nb_[:C, c * 128:c * 128 + C], f4(Wc)[:C, c, :C], f4(Nc)[:C, c, :C], r=[Nc, Wc], w=[nb_])
                            V_("tensor_copy", r=[nb_], w=[Nn], out=f4(Nn)[:C, :nch, :C], in_=nb_[:].rearrange("p (c n) -> p c n", c=4)[:C, :nch, :C])
                        yield
                        pb_ = S.bank()
                        for c in range(nch):
                            mm(pb_[:C, c * 128:c * 128 + C], f4(Wn)[:C, c, :C], RT[:C, c, :C], r=[Wn, s12], w=[pb_])
                        V_("tensor_tensor", r=[pb_, s12], w=[s12], out=RT[:C, :nch, :C], in0=RT[:C, :nch, :C],
                           in1=pb_[:].rearrange("p (c n) -> p c n", c=4)[:C, :nch, :C], op=ALU.add)
                        Wc, Wn = Wn, Wc
                        Nc, Nn = Nn, Nc
                        yield
                    kbg, kd, vb = f4(s4), f4(s10), f4(s13)
                    ktb = S.bank()
                    vtb = S.bank()
                    for c in range(nch):
                        cs = slice(c * C, (c + 1) * C)
                        tr(ktb[:C, c * 128:(c + 1) * 128], s5.t[:, cs], ident[:], r=[s5, ident], w=[ktb], sig=(c == nch - 1))
                    for c in range(nch):
                        cs = slice(c * C, (c + 1) * C)
                        tr(vtb[:C, c * 128:(c + 1) * 128], s6.t[:, cs], ident[:], r=[s6, ident], w=[vtb], sig=(c == nch - 1))
                    for c in range(nch):
                        V_("tensor_scalar", r=[ktb, bg], w=[s4], out=kbg[:C, c, :], in0=ktb[:C, c * 128:(c + 1) * 128],
                           scalar1=bg[:C, c, hd:hd + 1], scalar2=None, op0=ALU.mult)
                        V_("tensor_scalar", r=[ktb, s9], w=[s10], out=kd[:C, c, :], in0=ktb[:C, c * 128:(c + 1) * 128],
                           scalar1=DT[:C, c, C - 1:C], scalar2=None, op0=ALU.mult)
                        V_("tensor_scalar", r=[vtb, beta], w=[s13], out=vb[:C, c, :], in0=vtb[:C, c * 128:(c + 1) * 128],
                           scalar1=beta[:C, c, hd:hd + 1], scalar2=None, op0=ALU.mult)
                    yield
                    u_, wT = f4(s1), f4(s2)
                    ub = S.bank()
                    wtb = S.bank()
                    for c in range(nch):
                        mm(ub[:C, c * 128:(c + 1) * 128], RT[:C, c, :C], vb[:C, c, :], r=[s12, s13], w=[ub])
                    for c in range(nch):
                        mm(wtb[:, c * 128:c * 128 + C], kbg[:C, c, :], RT[:C, c, :C], r=[s12, s4], w=[wtb])
                    act(u_[:C, :nch, :], ub[:].rearrange("p (c n) -> p c n", c=4)[:C, :nch, :], AF.Copy, r=[ub], w=[s1])
                    V_("tensor_copy", r=[wtb], w=[s2], out=wT[:, :nch, :C], in_=wtb[:].rearrange("p (c n) -> p c n", c=4)[:, :nch, :C])
                    yield
                    ob = S.bank_reserve()
                    for c in range(nch):
                        cs = slice(c * C, (c + 1) * C)
                        if prompt:
                            s_ap, s_tk = sstate[:, hd, :], sstate_t[hd]
                        else:
                            s_ap, s_tk = sst[:, c, hd, :], sst
                        vn = vns[c % 2]
                        wsb = S.bank()
                        mm(wsb[:C, :128], wT[:, c, :C], s_ap, r=[s2, s_tk], w=[wsb])
                        V_("tensor_tensor", r=[s1, wsb], w=[vn], out=vn[:C, :], in0=u_[:C, c, :], in1=wsb[:C, :128], op=ALU.subtract)
                        mm(ob[:, cs], s_ap, s8.t[:, cs], start=True, stop=False, r=[s_tk, s8], w=[ob])
                        mm(ob[:, cs], vn[:C, :], AT[:C, c, :C], start=False, stop=True, r=[vn, s11], w=[ob])
                        sb_ = S.bank()
                        mm(sb_[:, :128], kd[:C, c, :], vn[:C, :], r=[s10, vn], w=[sb_])
                        V_("scalar_tensor_tensor", r=[s_tk, egl, sb_], w=[s_tk], out=s_ap, in0=s_ap, scalar=egl[:, c:c + 1],
                           in1=sb_[:, :128], op0=ALU.mult, op1=ALU.add)
                        yield
                    act(fl(s3), ob[:, :ntok], AF.Copy, r=[ob], w=[s3])
                    S.bank_release(ob)
                    with S.scope() as mk2:
                        rs = rstd_from(mk2, [fl(s3)], [s3], ntok, 128.0)
                        V_("tensor_tensor", r=[s3, rs], w=[s3], out=fl(s3), in0=fl(s3), in1=rs[:, :ntok], op=ALU.mult)
                    V_("scalar_tensor_tensor", r=[s3, ggain, s7], w=[brT], out=brT[:, hd, :ntok], in0=fl(s3), scalar=ggain[:, 0:1],
                       in1=fl(s7), op0=ALU.mult, op1=ALU.mult)
                    if last:
                        dma("sp", O["gdnp"][l, hd], sstate[:, hd, :], r=[sstate_t[hd]], w=[])

                for pair in range(4):
                    alive = [head(2 * pair, sets[0]), head(2 * pair + 1, sets[1])]
                    while alive:
                        for g_ in list(alive):
                            try:
                                next(g_)
                            except StopIteration:
                                alive.remove(g_)
                if not prompt:
                    for s in range(NS):
                        dma("sp", O["gdns"][l, s].rearrange("h k v -> k h v"), sst[:, s, :, :], r=[sst], w=[])

        def attn_prompt(l, b, brT):
            last = (b == NB - 1)
            wq_ = I["w_in"][l][:, O_QB:O_QB + 1536].rearrange("(kc p) (g h c) -> p kc g h c", p=128, g=3, h=4)
            wk_ = I["w_in"][l][:, O_KB:O_KB + 1536].rearrange("(kc p) (g h c) -> p kc g h c", p=128, g=3, h=4)
            wv_ = I["w_in"][l][:, O_VB:O_VB + 1536].rearrange("(kc p) (g h c) -> p kc g h c", p=128, g=3, h=4)
            need = [(b + 1) * 512 > T - PW[g] for g in range(3)]
            t0 = [max(0, 4 * b - 1), max(0, 4 * b - 4), 0]
            nh = [4 * b - t0[g] for g in range(3)]
            hoff = [0, nh[0], nh[0] + nh[1]]
            nhT = sum(nh)
            with S.scope() as mk:
                qT = mk([128, 3, 512], BF16, "aq")
                kTc = mk([128, 3, 512], BF16, "ak")
                Vc = mk([128, 4, 3, 128], BF16, "av")
                kTh = mk([128, max(nhT, 1) * 128], BF16, "akh")
                Vh = mk([128, max(nhT, 1), 128], BF16, "avh")
                kst = [mk([128, 3, 128], F32, "kst%d" % i) for i in range(2)]
                Pts = [mk([128, 512], BF16, "Pt%d" % i) for i in range(3)]
                rec = mk([128, 512], F32, "rec")
                pcount = 0
                for hs in range(4):
                    wbq, (qv,) = wload([wq_[:, :, :, hs, :]], ("aq", l, hs))
                    for g in range(3):
                        pb = proj_fm(lambda kc: qv[:, kc, g, :], 512, wbq)
                        act(qT[:, g, :], pb[:], AF.Copy, r=[pb], w=[qT])
                    _chk(1.2)
                    wbk, (kv,) = wload([wk_[:, :, :, hs, :]], ("ak", l, hs))
                    for g in range(3):
                        pb = proj_fm(lambda kc: kv[:, kc, g, :], 512, wbk)
                        act(kTc[:, g, :], pb[:], AF.Copy, r=[pb], w=[kTc])
                        if not last:
                            dma("sp", kTs[g, hs, :, b * 512:(b + 1) * 512], kTc[:, g, :], r=[kTc], w=[d_kTs], append=True)
                    if any(need):
                        for tt in range(4):
                            pb = S.bank()
                            for kc in range(KC):
                                mm(pb[:, :384], hT[:, kc, tt * 128:(tt + 1) * 128], kv[:, kc].rearrange("p g c -> p (g c)"),
                                   start=(kc == 0), stop=(kc == KC - 1), r=[hT, wbk], w=[pb])
                            st = kst[tt % 2]
                            act(st[:].rearrange("p g c -> p (g c)"), pb[:, :384], AF.Copy, r=[pb], w=[st])
                            tok0 = b * 512 + tt * 128
                            for g in range(3):
                                r0 = tok0 - (T - PW[g])
                                if r0 >= 0:
                                    dma("sp", WP[g][l, r0:r0 + 128, hs * 128:(hs + 1) * 128], st[:, g, :], r=[st], w=[])
                    _chk(1.4)
                    wbv, (vv,) = wload([wv_[:, :, :, hs, :]], ("av", l, hs))
                    for tt in range(4):
                        pb = S.bank()
                        for kc in range(KC):
                            mm(pb[:, :384], hT[:, kc, tt * 128:(tt + 1) * 128], vv[:, kc].rearrange("p g c -> p (g c)"),
                               start=(kc == 0), stop=(kc == KC - 1), r=[hT, wbv], w=[pb])
                        V_("tensor_copy", r=[pb], w=[Vc], out=Vc[:, tt, :, :].rearrange("p g c -> p (g c)"), in_=pb[:, :384])
                        tok0 = b * 512 + tt * 128
                        if any(tok0 - (T - PW[g]) >= 0 for g in range(3)):
                            st = kst[tt % 2]
                            act(st[:].rearrange("p g c -> p (g c)"), pb[:, :384], AF.Copy, r=[pb], w=[st])
                            for g in range(3):
                                r0 = tok0 - (T - PW[g])
                                if r0 >= 0:
                                    dma("sp", WP[g][l, r0:r0 + 128, 512 + hs * 128:512 + (hs + 1) * 128], st[:, g, :], r=[st], w=[])
                    if not last:
                        for g in range(3):
                            dma("sp", Vs[g, hs, b * 512:(b + 1) * 512, :].rearrange("(n p) d -> p n d", p=128), Vc[:, :, g, :], r=[Vc], w=[d_Vs], append=True)
                    for g in range(3):
                        if nh[g] > 0:
                            dma("sp", kTh[:, hoff[g] * 128:(hoff[g] + nh[g]) * 128], kTs[g, hs, :, t0[g] * 128:4 * b * 128], r=[d_kTs], w=[kTh], append=(g > 0 and nh[0] + (nh[1] if g > 1 else 0) > 0))
                            dma("sp", Vh[:, hoff[g]:hoff[g] + nh[g], :], Vs[g, hs, t0[g] * 128:4 * b * 128, :].rearrange("(n p) d -> p n d", p=128),
                                r=[d_Vs], w=[Vh], append=(g > 0 and nh[0] + (nh[1] if g > 1 else 0) > 0))
                    _chk(1.6)
                    ob = S.bank_reserve()
                    db = S.bank_reserve()
                    allg = []
                    for qt in range(4):
                        qa = 4 * b + qt
                        pairs = []
                        for tt_ in (qa - 1, qa):
                            if tt_ >= 0:
                                pairs.append((0, tt_, 1 if tt_ == qa - 1 else 0))
                        for tt_ in range(qa - 4, qa + 1):
                            if tt_ >= 0:
                                dl = qa - tt_
                                pairs.append((1, tt_, 2 if dl == 0 else (4 if dl == 4 else 3)))
                        for tt_ in range(0, qa + 1):
                            pairs.append((2, tt_, 5 if tt_ == qa else 6))
                        npair = len(pairs)
                        for i0 in range(0, npair, 4):
                            allg.append((qt, pairs[i0:i0 + 4], i0, npair))

                    def stage_a(qt, grp):
                        sb = S.bank()
                        for i, (g, tt_, m) in enumerate(grp):
                            if tt_ >= 4 * b:
                                klhs, ktk = kTc[:, g, (tt_ - 4 * b) * 128:(tt_ - 4 * b + 1) * 128], kTc
                            else:
                                hh = hoff[g] + tt_ - t0[g]
                                klhs, ktk = kTh[:, hh * 128:(hh + 1) * 128], kTh
                            mm(sb[:, i * 128:(i + 1) * 128], klhs, qT[:, g, qt * 128:(qt + 1) * 128], r=[ktk, qT], w=[sb])
                        Pt = Pts[stage_a.n % 3]
                        stage_a.n += 1
                        n = len(grp) * 128
                        act(Pt[:, :n], sb[:, :n], AF.Exp, r=[sb], w=[Pt], scale=ISQ)
                        for i, (g, tt_, m) in enumerate(grp):
                            V_("tensor_tensor", r=[Pt, amask], w=[Pt], out=Pt[:, i * 128:(i + 1) * 128], in0=Pt[:, i * 128:(i + 1) * 128],
                               in1=amask[:, m, :], op=ALU.mult)
                        return Pt

                    stage_a.n = 0

                    def stage_b(qt, grp, i0, npair, Pt):
                        for i, (g, tt_, m) in enumerate(grp):
                            if tt_ >= 4 * b:
                                vl, vtk = Vc[:, tt_ - 4 * b, g, :], Vc
                            else:
                                vl, vtk = Vh[:, hoff[g] + tt_ - t0[g], :], Vh
                            first = (i0 + i == 0)
                            lastp = (i0 + i == npair - 1)
                            mm(ob[:, qt * 128:(qt + 1) * 128], vl, Pt[:, i * 128:(i + 1) * 128], start=first, stop=lastp, r=[vtk, Pt], w=[ob])
                            mm(db[:, qt * 128:(qt + 1) * 128], ones_b[:], Pt[:, i * 128:(i + 1) * 128], start=first, stop=lastp,
                               r=[ones_b, Pt], w=[db])

                    prev = None
                    for (qt, grp, i0, npair) in allg:
                        Pt = stage_a(qt, grp)
                        if prev is not None:
                            stage_b(*prev)
                        prev = (qt, grp, i0, npair, Pt)
                    stage_b(*prev)
                    _chk(1.8)
                    V_("reciprocal", r=[db], w=[rec], out=rec[:], in_=db[:])
                    V_("tensor_tensor", r=[ob, rec], w=[brT], out=brT[:, 8 + hs, :], in0=ob[:], in1=rec[:], op=ALU.mult)
                    S.bank_release(ob)
                    S.bank_release(db)

        def attn_sample(l, brT):
            wq_ = I["w_in"][l][:, O_QB:O_QB + 1536].rearrange("(kc p) (g h c) -> p kc g h c", p=128, g=3, h=4)
            wk_ = I["w_in"][l][:, O_KB:O_KB + 1536].rearrange("(kc p) (g h c) -> p kc g h c", p=128, g=3, h=4)
            wv_ = I["w_in"][l][:, O_VB:O_VB + 1536].rearrange("(kc p) (g h c) -> p kc g h c", p=128, g=3, h=4)
            with S.scope() as mk:
                qT = mk([128, 3, NSQ], F32, "sq_")
                kTn = mk([128, 3, NSQ], F32, "sk_")
                Kn = mk([128, NS, 3, 128], F32, "sKn")
                Vn = mk([128, NS, 3, 128], F32, "sVn")
                cks = [mk([128, 9, 2, 128], F32, "ck%d" % i) for i in range(2)]
                kTc = mk([128, 9, 128], F32, "skT")
                Pc = mk([128, 48], F32, "sPc")
                Pn = mk([128, 12], F32, "sPn")
                rec = mk([128, NSQ], F32, "srec")
                ci = 0
                for hs in range(4):
                    wbq, (qv,) = wload([wq_[:, :, :, hs, :]], ("aq", l, hs))
                    for g in range(3):
                        pb = proj_fm(lambda kc: qv[:, kc, g, :], NSQ, wbq)
                        act(qT[:, g, :], pb[:, :NSQ], AF.Copy, r=[pb], w=[qT])
                    wbk, (kv,) = wload([wk_[:, :, :, hs, :]], ("ak", l, hs))
                    for g in range(3):
                        pb = proj_fm(lambda kc: kv[:, kc, g, :], NSQ, wbk)
                        act(kTn[:, g, :], pb[:, :NSQ], AF.Copy, r=[pb], w=[kTn])
                    for s in range(NS):
                        pb = S.bank()
                        for kc in range(KC):
                            mm(pb[:4, :384], hT[:, kc, 4 * s:4 * s + 4], kv[:, kc].rearrange("p g c -> p (g c)"), start=(kc == 0),
                               stop=(kc == KC - 1), r=[hT, wbk], w=[pb])
                        act(Kn[:4, s, :, :].rearrange("p g c -> p (g c)"), pb[:4, :384], AF.Copy, r=[pb], w=[Kn])
                    wbv, (vv,) = wload([wv_[:, :, :, hs, :]], ("av", l, hs))
                    for s in range(NS):
                        pb = S.bank()
                        for kc in range(KC):
                            mm(pb[:4, :384], hT[:, kc, 4 * s:4 * s + 4], vv[:, kc].rearrange("p g c -> p (g c)"), start=(kc == 0),
                               stop=(kc == KC - 1), r=[hT, wbv], w=[pb])
                        act(Vn[:4, s, :, :].rearrange("p g c -> p (g c)"), pb[:4, :384], AF.Copy, r=[pb], w=[Vn])
                    for s in range(NS):
                        for g in range(3):
                            w = WINS[g]
                            dma("sp", WS[g][l, s, w - 4:w, hs * 128:(hs + 1) * 128], Kn[:4, s, g, :], r=[Kn], w=[])
                            dma("sp", WS[g][l, s, w - 4:w, 512 + hs * 128:512 + (hs + 1) * 128], Vn[:4, s, g, :], r=[Vn], w=[])
                    ob = S.bank_reserve()
                    db = S.bank_reserve()
                    for s in range(NS):
                        ck = cks[ci % 2]
                        ci += 1
                        c4 = lambda g: CW[g][l, s].rearrange("r (kv h d) -> r kv h d", kv=2, h=4)[:, :, hs, :]
                        dma("sp", ck[:, 0, :, :], c4(0), w=[ck])
                        for r_ in range(4):
                            dma("sp", ck[:, 1 + r_, :, :], c4(1).rearrange("(m q) kv d -> m q kv d", q=4)[:, r_, :, :], w=[ck], append=True)
                            dma("sp", ck[:, 5 + r_, :, :], c4(2).rearrange("(m q) kv d -> m q kv d", q=16)[:, r_, :, :], w=[ck], append=True)
                        for i0 in (0, 4, 8):
                            n = min(4, 9 - i0)
                            pb = S.bank()
                            for i in range(n):
                                tr(pb[:, i * 128:(i + 1) * 128], ck[:, i0 + i, 0, :], ident[:], r=[ck, ident], w=[pb], sig=(i == n - 1))
                            act(kTc[:, i0:i0 + n, :].rearrange("p a d -> p (a d)"), pb[:, :n * 128], AF.Copy, r=[pb], w=[kTc])
                        sb = S.bank()
                        for i in range(9):
                            g = 0 if i == 0 else (1 if i < 5 else 2)
                            mm(sb[:, i * 4:(i + 1) * 4], kTc[:, i, :], qT[:, g, 4 * s:4 * s + 4], r=[kTc, qT], w=[sb])
                        sb2 = S.bank()
                        for g in range(3):
                            mm(sb2[:4, g * 4:(g + 1) * 4], kTn[:, g, 4 * s:4 * s + 4], qT[:, g, 4 * s:4 * s + 4], r=[kTn, qT], w=[sb2])
                        act(Pc[:, :36], sb[:, :36], AF.Exp, r=[sb], w=[Pc], scale=ISQ)
                        act(Pn[:4, :12], sb2[:4, :12], AF.Exp, r=[sb2], w=[Pn], scale=ISQ)
                        V_("tensor_tensor", r=[Pc, smask], w=[Pc], out=Pc[:, :36], in0=Pc[:, :36], in1=smask[:, 0:9, :].rearrange("p a q -> p (a q)"), op=ALU.mult)
                        V_("tensor_tensor", r=[Pn, smask], w=[Pn], out=Pn[:4, :12], in0=Pn[:4, :12], in1=smask[:4, 9:12, :].rearrange("p a q -> p (a q)"), op=ALU.mult)
                        oc = slice(4 * s, 4 * s + 4)
                        for i in range(9):
                            mm(ob[:, oc], ck[:, i, 1, :], Pc[:, i * 4:(i + 1) * 4], start=(i == 0), stop=False, r=[ck, Pc], w=[ob])
                        for g in range(3):
                            mm(ob[:, oc], Vn[:4, s, g, :], Pn[:4, g * 4:(g + 1) * 4], start=False, stop=(g == 2), r=[Vn, Pn], w=[ob])
                        for i in range(9):
                            mm(db[:, oc], ones_f[:], Pc[:, i * 4:(i + 1) * 4], start=(i == 0), stop=False, r=[ones_f, Pc], w=[db])
                        for g in range(3):
                            mm(db[:, oc], ones_f[:4, :], Pn[:4, g * 4:(g + 1) * 4], start=False, stop=(g == 2), r=[ones_f, Pn], w=[db])
                    V_("reciprocal", r=[db], w=[rec], out=rec[:, :NSQ], in_=db[:, :NSQ])
                    V_("tensor_tensor", r=[ob, rec], w=[brT], out=brT[:, 8 + hs, :NSQ], in0=ob[:, :NSQ], in1=rec[:, :NSQ], op=ALU.mult)
                    S.bank_release(ob)
                    S.bank_release(db)

        def pool_branch(l, b, prompt, brT):
            nseq = 1 if prompt else NS
            Tq = 512 if prompt else 4
            ntok = nseq * Tq
            Lx = 15 + Tq
            with S.scope() as mk:
                phist = None
                if not prompt:
                    praw = mk([128, 1024], F32, "praw")
                    dma("sp", praw[:NS * 15, :], I["spool"][l], w=[praw])
                    phist = mk([128, 8, NS, 15], F32, "phist")
                    pb = S.bank()
                    for t_ in range(8):
                        tr(pb[:, t_ * NS * 15:(t_ + 1) * NS * 15], praw[:NS * 15, t_ * 128:(t_ + 1) * 128], ident[:NS * 15, :NS * 15],
                           r=[praw, ident], w=[pb], sig=(t_ == 7))
                    act(phist[:].rearrange("p t s j -> p (t s j)"), pb[:, :8 * NS * 15], AF.Copy, r=[pb], w=[phist])
                pbuf = mk([128, nseq, Lx], F32, "pbuf")
                Pa = mk([128, nseq, Lx], F32, "Pa")
                Pb = mk([128, nseq, Lx], F32, "Pb")
                pooledT = mk([128, 2, ntok], BF16, "pooledT")
                for half in range(2):
                    wb, (wv,) = wload([win_cols(l, O_UC + 512 * half, 512)], ("uc", l, half))
                    for j in range(4):
                        ct = half * 4 + j
                        gi = ct // 2
                        win = 2 << gi
                        pb = proj_fm(lambda kc: wv[:, kc, j * 128:(j + 1) * 128], ntok, wb)
                        act(pbuf[:, :, 15:Lx], pb[:, :ntok].rearrange("p (s t) -> p s t", s=nseq), AF.Copy, r=[pb], w=[pbuf])
                        if prompt:
                            V_("tensor_copy", r=[ptail], w=[pbuf], out=pbuf[:, 0, 0:15], in_=ptail[:, ct, :])
                        else:
                            V_("tensor_copy", r=[phist], w=[pbuf], out=pbuf[:, :, 0:15], in_=phist[:, ct, :, :])
                        srcb, sh = pbuf, 1
                        res = None
                        for lv in range(gi + 1):
                            dst = Pa if lv % 2 == 0 else Pb
                            lo = 2 * sh - 1
                            V_("tensor_tensor", r=[srcb], w=[dst], out=dst[:, :, lo:Lx], in0=srcb[:, :, lo:Lx], in1=srcb[:, :, lo - sh:Lx - sh], op=ALU.add)
                            srcb, sh, res = dst, sh * 2, dst
                        if prompt and b == 0:
                            V_("tensor_tensor", r=[res, pcorr], w=[res], out=res[:, 0, 15:31], in0=res[:, 0, 15:31], in1=pcorr[:, gi, :], op=ALU.mult)
                        V_("scalar_tensor_tensor", r=[res, pbuf], w=[pooledT], out=pooledT[:, ct % 2, :ntok].rearrange("p (s t) -> p s t", s=nseq),
                           in0=res[:, :, 15:Lx], scalar=1.0 / win, in1=pbuf[:, :, 15:Lx], op0=ALU.mult, op1=ALU.subtract)
                        if prompt:
                            V_("tensor_copy", r=[pbuf], w=[ptail], out=ptail[:, ct, :], in_=pbuf[:, 0, Tq:Tq + 15])
                        if ct % 2 == 1:
                            for ot in range(2):
                                pb2 = S.bank()
                                for kt in range(2):
                                    mm(pb2[:, :ntok], wpool[:, gi, kt, ot * 128:(ot + 1) * 128], pooledT[:, kt, :ntok], start=(kt == 0), stop=(kt == 1),
                                       r=[wpool, pooledT], w=[pb2])
                                V_("tensor_scalar", r=[pb2, pscale], w=[brT], out=brT[:, 12 + 2 * gi + ot, :ntok], in0=pb2[:, :ntok],
                                   scalar1=pscale[:, 2 * gi + ot:2 * gi + ot + 1], scalar2=None, op0=ALU.mult)

        def tails(l, prompt):
            t0 = 496 if prompt else 0
            with S.scope() as mk:
                sts = [mk([16, 512], F32, "tst%d" % i) for i in range(2)]
                for ci_ in range(8):
                    c0 = ci_ * 512 if ci_ < 6 else O_UC + (ci_ - 6) * 512
                    wb, (wv,) = wload([win_cols(l, c0, 512)], ("tail", l, ci_))
                    pb = S.bank()
                    for kc in range(KC):
                        mm(pb[:16, :], hT[:, kc, t0:t0 + 16], wv[:, kc, :], start=(kc == 0), stop=(kc == KC - 1), r=[hT, wb], w=[pb])
                    st = sts[ci_ % 2]
                    act(st[:], pb[:16, :], AF.Copy, r=[pb], w=[st])
                    if prompt:
                        if ci_ < 6:
                            dma("sp", O["convp"][l, :, ci_ * 512:(ci_ + 1) * 512], st[13:16, :], r=[st], w=[])
                        else:
                            dma("sp", O["poolp"][l, :, (ci_ - 6) * 512:(ci_ - 5) * 512], st[1:16, :], r=[st], w=[])
                    else:
                        for s in range(NS):
                            if ci_ < 6:
                                dma("sp", O["convs"][l, s, :, ci_ * 512:(ci_ + 1) * 512], st[4 * s + 1:4 * s + 4, :], r=[st], w=[])
                            else:
                                dma("sp", O["pools"][l, s, 11:15, (ci_ - 6) * 512:(ci_ - 5) * 512], st[4 * s:4 * s + 4, :], r=[st], w=[])

        def emit_block(l, b, prompt):
            ntok = 512 if prompt else NSQ
            nt = 4 if prompt else 1
            rows = 128 if prompt else NSQ
            last = prompt and (b == NB - 1)
            if l == 0:
                src = I["xp"] if prompt else I["xs"]
                with S.scope() as mk:
                    xins = [mk([128, D], F32, "xin%d" % i) for i in range(2)]
                    for tt in range(nt):
                        xin = xins[tt % 2]
                        r0 = b * 512 + tt * 128 if prompt else 0
                        dma("sp", xin[:rows, :], src[r0:r0 + rows, :], w=[xin])
                        for k4 in range(4):
                            pb = S.bank()
                            for j in range(4):
                                kc = k4 * 4 + j
                                tr(pb[:, j * 128:j * 128 + rows], xin[:rows, kc * 128:(kc + 1) * 128], ident[:rows, :rows],
                                   r=[xin, ident], w=[pb], sig=(j == 3))
                            act(xT[:, k4 * 4:(k4 + 1) * 4, tt * 128:tt * 128 + rows], pb[:].rearrange("p (j t) -> p j t", j=4)[:, :, :rows],
                                AF.Copy, r=[pb], w=[xT])
            else:
                if prompt:
                    dma("sp", xT[:], x1T[:, :, b * 512:(b + 1) * 512].rearrange("kc p t -> p kc t"), r=[d_x1T], w=[xT])
                else:
                    dma("sp", xT[:, :, :ntok], xs1T.rearrange("kc p t -> p kc t"), r=[d_xs1T], w=[xT])
            _chk(0.7)
            norm_to_hT(gpre, ntok)
            _chk(1)

            with S.scope() as mkA2:
                brT = mkA2([128, 20, ntok], BF16, "brT")
                if prompt:
                    attn_prompt(l, b, brT)
                else:
                    attn_sample(l, brT)
                _chk(2)
                gdn(l, b, prompt, brT)
                _chk(3)
                pool_branch(l, b, prompt, brT)
                _chk(4)
                if last or not prompt:
                    tails(l, prompt)
                _chk(5)
                mergedT = mkA2([128, KC, ntok], BF16, "mergedT")
                with S.scope() as mk:
                    sg = [mk([128, 512], F32, "sg%d" % i) for i in range(3)]
                    acc = mk([128, 512], F32, "acc")
                    t1 = mk([128, 512], F32, "t1")
                    for dg in range(16):
                        wb, gv = wload([win_cols(l, O_GA + 2048 * i + 128 * dg, 128) for i in range(3)], ("gate", l, dg))
                        wb2, bv = wload([rows_cols(I["w_br_a"][l], dg * 128, 128), rows_cols(I["w_br_b"][l], dg * 128, 128),
                                         rows_cols(I["w_br_c"][l], dg * 128, 128)], ("br", l, dg))
                        for j in range(1):
                            dt_ = dg
                            gps = [proj_fm(lambda kc, i=i: gv[i][:, kc, j * 128:(j + 1) * 128], ntok, wb) for i in range(3)]
                            bps = []
                            for i, (nk, k0) in enumerate(((8, 0), (4, 8), (8, 12))):
                                pb = S.bank()
                                for kk in range(nk):
                                    mm(pb[:, :ntok], bv[i][:, kk, j * 128:(j + 1) * 128], brT[:, k0 + kk, :ntok], start=(kk == 0), stop=(kk == nk - 1),
                                       r=[wb2, brT], w=[pb])
                                bps.append(pb)
                            for i in range(3):
                                act(sg[i][:, :ntok], gps[i][:, :ntok], AF.Sigmoid, r=[gps[i]], w=[sg[i]])
                            V_("tensor_tensor", r=[sg[0], bps[0]], w=[acc], out=acc[:, :ntok], in0=sg[0][:, :ntok], in1=bps[0][:, :ntok], op=ALU.mult)
                            V_("tensor_tensor", r=[sg[1], bps[1]], w=[t1], out=t1[:, :ntok], in0=sg[1][:, :ntok], in1=bps[1][:, :ntok], op=ALU.mult)
                            V_("tensor_tensor", r=[acc, t1], w=[acc], out=acc[:, :ntok], in0=acc[:, :ntok], in1=t1[:, :ntok], op=ALU.add)
                            V_("tensor_tensor", r=[sg[2], bps[2]], w=[t1], out=t1[:, :ntok], in0=sg[2][:, :ntok], in1=bps[2][:, :ntok], op=ALU.mult)
                            V_("tensor_tensor", r=[acc, t1], w=[mergedT], out=mergedT[:, dt_, :ntok], in0=acc[:, :ntok], in1=t1[:, :ntok], op=ALU.add)
                if l == 0 and b == 0 and prompt:
                    for i_ in range(20):
                        dump(i_, brT[:, i_, :], brT)
                    for i_ in range(4):
                        dump(20 + i_, mergedT[:, i_, :], mergedT)
                _chk(6)
                with S.scope() as mk:
                    ytmp = mk([128, KC, ntok], F32, "ytmp")
                    sqs = [mk([128, 512], F32, "sqo%d" % i) for i in range(2)]
                    ssb = S.bank_reserve()
                    for cg in range(4):
                        wb, (wv,) = wload([rows_cols(I["w_out"][l], cg * 512, 512)], ("out", l, cg))
                        for j in range(4):
                            dt_ = cg * 4 + j
                            pb = S.bank()
                            for kc in range(KC):
                                mm(pb[:, :ntok], wv[:, kc, j * 128:(j + 1) * 128], mergedT[:, kc, :ntok], start=(kc == 0), stop=(kc == KC - 1),
                                   r=[wb, mergedT], w=[pb])
                            V_("tensor_copy", r=[pb], w=[ytmp], out=ytmp[:, dt_, :ntok], in_=pb[:, :ntok])
                            sq = sqs[dt_ % 2]
                            act(sq[:, :ntok], pb[:, :ntok], AF.Square, r=[pb], w=[sq])
                            mm(ssb[:, :ntok], ones_f[:], sq[:, :ntok], start=(dt_ == 0), stop=(dt_ == KC - 1), r=[ones_f, sq], w=[ssb], sig=True)
                    rstd = mk([128, 512], F32, "rstdo")
                    V_("tensor_scalar", r=[ssb], w=[rstd], out=rstd[:, :ntok], in0=ssb[:, :ntok], scalar1=1.0 / D, scalar2=EPS, op0=ALU.mult, op1=ALU.add)
                    act(rstd[:, :ntok], rstd[:, :ntok], AF.Ln, r=[rstd], w=[rstd])
                    act(rstd[:, :ntok], rstd[:, :ntok], AF.Exp, r=[rstd], w=[rstd], scale=-0.5)
                    S.bank_release(ssb)
                    for dt_ in range(KC):
                        V_("scalar_tensor_tensor", r=[ytmp, gpost, rstd], w=[ytmp], out=ytmp[:, dt_, :ntok], in0=ytmp[:, dt_, :ntok],
                           scalar=gpost[:, dt_:dt_ + 1], in1=rstd[:, :ntok], op0=ALU.mult, op1=ALU.mult)
                        G_("tensor_tensor", r=[ytmp, xT], w=[xT], out=xT[:, dt_, :ntok], in0=xT[:, dt_, :ntok], in1=ytmp[:, dt_, :ntok], op=ALU.add)

            if l == 0 and b == 0 and prompt:
                for i_ in range(4):
                    dump(24 + i_, xT[:, i_, :], xT)
            _chk(7)
            norm_to_hT(gpre2, ntok)
            with S.scope() as mk:
                actT = mk([128, FT, ntok], BF16, "actT")
                ytmp = mk([128, KC, ntok], F32, "ytmp2")
                sgs = [mk([128, 512], F32, "sgf%d" % i) for i in range(2)]
                sqs = [mk([128, 512], F32, "sqf%d" % i) for i in range(2)]
                wgu = I["w_gu"][l].rearrange("(kc p) (u n) -> p kc u n", p=128, u=2)
                for fg in range(FT // 2):
                    wb, (wv,) = wload([wgu[:, :, :, fg * 256:(fg + 1) * 256]], ("gu", l, fg))
                    for j in range(2):
                        ft = fg * 2 + j
                        gpb = proj_fm(lambda kc: wv[:, kc, 0, j * 128:(j + 1) * 128], ntok, wb)
                        upb = proj_fm(lambda kc: wv[:, kc, 1, j * 128:(j + 1) * 128], ntok, wb)
                        sg_ = sgs[ft % 2]
                        act(sg_[:, :ntok], gpb[:, :ntok], AF.Silu, r=[gpb], w=[sg_])
                        V_("tensor_tensor", r=[sg_, upb], w=[actT], out=actT[:, ft, :ntok], in0=sg_[:, :ntok], in1=upb[:, :ntok], op=ALU.mult)
                ssb = S.bank_reserve()
                wdn = I["w_down"][l].rearrange("(kc p) n -> p kc n", p=128)
                for dg in range(8):
                    pbs = [S.bank_reserve() for _ in range(2)]
                    for kh in range(2):
                        wb, (wv,) = wload([wdn[:, kh * 22:(kh + 1) * 22, dg * 256:(dg + 1) * 256]], ("dn", l, dg, kh))
                        for j in range(2):
                            for kk in range(22):
                                fk = kh * 22 + kk
                                mm(pbs[j][:, :ntok], wv[:, kk, j * 128:(j + 1) * 128], actT[:, fk, :ntok], start=(fk == 0), stop=(fk == FT - 1),
                                   r=[wb, actT], w=[pbs[j]])
                    for j in range(2):
                        dt_ = dg * 2 + j
                        pb = pbs[j]
                        V_("tensor_copy", r=[pb], w=[ytmp], out=ytmp[:, dt_, :ntok], in_=pb[:, :ntok])
                        sq = sqs[dt_ % 2]
                        act(sq[:, :ntok], pb[:, :ntok], AF.Square, r=[pb], w=[sq])
                        mm(ssb[:, :ntok], ones_f[:], sq[:, :ntok], start=(dt_ == 0), stop=(dt_ == KC - 1), r=[ones_f, sq], w=[ssb], sig=True)
                        S.bank_release(pb)
                rstd = mk([128, 512], F32, "rstdf")
                V_("tensor_scalar", r=[ssb], w=[rstd], out=rstd[:, :ntok], in0=ssb[:, :ntok], scalar1=1.0 / D, scalar2=EPS, op0=ALU.mult, op1=ALU.add)
                act(rstd[:, :ntok], rstd[:, :ntok], AF.Ln, r=[rstd], w=[rstd])
                act(rstd[:, :ntok], rstd[:, :ntok], AF.Exp, r=[rstd], w=[rstd], scale=-0.5)
                S.bank_release(ssb)
                for dt_ in range(KC):
                    V_("scalar_tensor_tensor", r=[ytmp, gpost2, rstd], w=[ytmp], out=ytmp[:, dt_, :ntok], in0=ytmp[:, dt_, :ntok],
                       scalar=gpost2[:, dt_:dt_ + 1], in1=rstd[:, :ntok], op0=ALU.mult, op1=ALU.mult)
                    G_("tensor_tensor", r=[ytmp, xT], w=[xT], out=xT[:, dt_, :ntok], in0=xT[:, dt_, :ntok], in1=ytmp[:, dt_, :ntok], op=ALU.add)

            if l == 0 and b == 0 and prompt:
                for i_ in range(4):
                    dump(28 + i_, xT[:, i_, :], xT)
            _chk(8)
            if l < L - 1:
                if prompt:
                    dma("sp", x1T[:, :, b * 512:(b + 1) * 512].rearrange("kc p t -> p kc t"), xT[:], r=[xT], w=[d_x1T])
                else:
                    dma("sp", xs1T.rearrange("kc p t -> p kc t"), xT[:, :, :ntok], r=[xT], w=[d_xs1T])
            else:
                dst = O["yp"] if prompt else O["ys"]
                with S.scope() as mk:
                    ysts = [mk([128, D], F32, "yst%d" % i) for i in range(2)]
                    for tt in range(nt):
                        yst = ysts[tt % 2]
                        for k4 in range(4):
                            pb = S.bank()
                            for j in range(4):
                                kc = k4 * 4 + j
                                tr(pb[:rows, j * 128:(j + 1) * 128], xT[:, kc, tt * 128:tt * 128 + rows], ident[:], r=[xT, ident], w=[pb], sig=(j == 3))
                            act(yst[:rows, k4 * 512:(k4 + 1) * 512], pb[:rows, :], AF.Copy, r=[pb], w=[yst])
                        r0 = b * 512 + tt * 128 if prompt else 0
                        dma("sp", dst[r0:r0 + rows, :], yst[:rows, :], r=[yst], w=[])

        try:
            _chk(0)
            for l in range(L):
                load_layer_params(l)
                _chk(0.3)
                for tl in (ctail, ptail):
                    G_("memset", w=[tl], ap=tl[:], constant=0.0)
                G_("memset", w=[sstate] + sstate_t, ap=sstate[:], constant=0.0)
                _chk(0.5)
                for b in range(NB):
                    emit_block(l, b, True)
                    emit_cache_copies(8)
                emit_block(l, 0, False)
                emit_cache_copies(8)
        except _Stop:
            pass
        emit_cache_copies(10 ** 6)
        S.finish()
        build.ninst = S.ninst
    return nc


def _consts():
    i = np.arange(128)
    ident = np.eye(128, dtype=np.float32)
    umat = (i[:, None] <= i[None, :]).astype(np.float32)
    mS = np.where(i[:, None] > i[None, :], 0.0, NEG).astype(np.float32)
    mI = np.where(i[None, :] >= i[:, None], 0.0, NEG).astype(np.float32)
    rep4 = lambda a: np.ascontiguousarray(np.broadcast_to(a[:, None, :], (128, 4, 128)))
    k = i[:, None]
    q = i[None, :]
    am = np.zeros((128, 7, 128), np.float32)
    am[:, 0] = (q >= k)
    am[:, 1] = (k >= q)
    am[:, 2] = (q >= k) & ((q - k) % 4 == 0)
    am[:, 3] = ((q - k) % 4 == 0)
    am[:, 4] = (k >= q) & ((q - k) % 4 == 0)
    am[:, 5] = (q >= k) & ((q - k) % 16 == 0)
    am[:, 6] = ((q - k) % 16 == 0)
    pc = np.ones((128, 4, 16), np.float32)
    for gi, win in enumerate((2, 4, 8, 16)):
        t = np.arange(16)
        pc[:, gi, :] = (win / np.minimum(win, t + 1))[None, :]
    sm = np.zeros((128, 12, 4), np.float32)
    t = np.arange(4)[None, :]
    sm[:, 0, :] = (i[:, None] >= t)
    for r in range(4):
        sm[:, 1 + r, :] = (t == r)
        sm[:, 5 + r, :] = (t == r)
    rr = np.arange(4)[:, None]
    sm[:4, 9, :] = (rr <= t)
    sm[:4, 10, :] = (rr == t)
    sm[:4, 11, :] = (rr == t)
    return dict(ident=ident, umat=umat, maskS4=rep4(mS), maskI4=rep4(mI), ident4=rep4(ident), amask=am, pcorr=pc, smask=sm)


_CACHE = {}


def run(inp, T, NS, ncores, prompt_of_core, sample_of_core):
    L = inp["w_in"].shape[0]
    key = (T, NS, L)
    if key not in _CACHE:
        _CACHE[key] = build(T, NS, L)
    nc = _CACHE[key]
    f = lambda a: np.ascontiguousarray(np.asarray(a, dtype=np.float32))
    shared = {k: f(inp[k]) for k in ("w_in", "w_pool", "w_br_a", "w_br_b", "w_br_c", "w_out", "w_gu", "w_down")}
    shared["convw"] = f(np.asarray(inp["conv_w"]).reshape(L, 4, 24, 128).transpose(0, 3, 2, 1))
    shared["alog"] = f(np.broadcast_to(np.asarray(inp["a_log"])[:, None, None, :], (L, 128, 4, 8)))
    shared["dtb"] = f(np.broadcast_to(np.asarray(inp["dt_bias"])[:, None, None, :], (L, 128, 4, 8)))
    shared["ggain"] = f(np.asarray(inp["gdn_gain"]).reshape(L, 128, 1))
    shared["pscale"] = f(np.asarray(inp["pool_scale"]).reshape(L, 8, 128).transpose(0, 2, 1))
    for nm, src in (("gpre", "g_pre_mix"), ("gpost", "g_post_mix"), ("gpre2", "g_pre_ffn"), ("gpost2", "g_post_ffn")):
        shared[nm] = f(np.asarray(inp[src]).reshape(L, 16, 128).transpose(0, 2, 1))
    shared.update(_consts())
    in_maps = []
    for c in range(ncores):
        pb = prompt_of_core[c]
        s0 = sample_of_core[c]
        m = dict(shared)
        m["xp"] = f(inp["x_prompt"][pb])
        m["xs"] = f(np.asarray(inp["x_sample"])[s0:s0 + NS].reshape(NS * 4, D))
        m["cw1"] = f(np.asarray(inp["cache_win1"])[:, s0:s0 + NS].reshape(L, NS, 128, 1024))
        m["cw2"] = f(np.asarray(inp["cache_win2"])[:, s0:s0 + NS].reshape(L, NS, 512, 1024))
        m["cw3"] = f(np.asarray(inp["cache_win3"])[:, s0:s0 + NS].reshape(L, NS, 2048, 1024))
        m["sgdn"] = f(np.asarray(inp["state_gdn"])[:, s0:s0 + NS])
        m["sconv"] = f(np.asarray(inp["state_conv"])[:, s0:s0 + NS].reshape(L, NS * 3, 3072))
        m["spool"] = f(np.asarray(inp["state_pool"])[:, s0:s0 + NS].reshape(L, NS * 15, 1024))
        in_maps.append(m)
    res = run_bass_kernel_spmd(nc, in_maps, core_ids=list(range(ncores)))
    return res.results


def kernel(**inputs):
    T, NS, NCORE = 2048, 4, 8
    L = 2
    res = run(inputs, T, NS, NCORE, [c % 4 for c in range(NCORE)], [4 * c for c in range(NCORE)])
    B = 4
    yp = np.stack([res[b]["yp"] for b in range(B)], 0)
    ys = np.concatenate([res[c]["ys"].reshape(NS, 4, D) for c in range(NCORE)], 0)

    def pst(name, shp):
        return np.stack([res[b][name].reshape(shp) for b in range(B)], 1)

    def sst(name, shp):
        return np.concatenate([res[c][name].reshape((L, NS) + shp) for c in range(NCORE)], 1)

    outs = (yp, ys,
            pst("w1p", (L, 128, 2, 4, 128)), pst("w2p", (L, 512, 2, 4, 128)), pst("w3p", (L, 2048, 2, 4, 128)),
            pst("gdnp", (L, 8, 128, 128)), pst("convp", (L, 3, 3072)), pst("poolp", (L, 15, 1024)),
            sst("w1s", (128, 2, 4, 128)), sst("w2s", (512, 2, 4, 128)), sst("w3s", (2048, 2, 4, 128)),
            sst("gdns", (8, 128, 128)), sst("convs", (3, 3072)), sst("pools", (15, 1024)))
    return tuple(np.ascontiguousarray(o, dtype=np.float32) for o in outs)
```

```python
import contextlib
import numpy as np
import concourse.bass as bass
import concourse.mybir as mybir
from concourse.bass_utils import run_bass_kernel_spmd

F32 = mybir.dt.float32
BF16 = mybir.dt.bfloat16
AF = mybir.ActivationFunctionType
ALU = mybir.AluOpType

D = 2048
KC = 16
NIN = 15888
DFF = 5632
FT = DFF // 128
EPS = 1e-6
NEG = -30000.0
O_QA, O_KA, O_VA, O_ZA, O_BA, O_AA = 0, 1024, 2048, 3072, 4096, 4104
O_QB, O_KB, O_VB, O_UC, O_GA, O_GB, O_GC = 4112, 5648, 7184, 8720, 9744, 11792, 13840
WINS = (128, 512, 2048)
DILS = (1, 4, 16)
NDS = 24
ISQ = 128 ** -0.5
DBG_STOP = None
DBG_DUMP = False


class _Stop(Exception):
    pass


def _chk(k):
    if DBG_STOP is not None and DBG_STOP == k:
        raise _Stop()


class Trk:
    def __init__(self, name="t"):
        self.name = name
        self.w = {}
        self.r = {}


class Tile(Trk):
    def __init__(self, name, t):
        super().__init__(name)
        self.t = t

    def __getitem__(self, idx):
        return self.t[idx]


class Sched:
    def __init__(self, nc, es):
        self.nc = nc
        self.es = es
        self.eng = {"pe": nc.tensor, "act": nc.scalar, "dve": nc.vector, "pool": nc.gpsimd, "sp": nc.sync}
        self.sem = {e: es.enter_context(nc.semaphore("s_" + e)) for e in ("pe", "act", "dve", "pool")}
        self.cnt = {e: 0 for e in self.sem}
        self.dsem = [es.enter_context(nc.semaphore("d%d" % i)) for i in range(NDS)]
        self.dcnt = [0] * NDS
        self.dnext = 0
        self.waited = {e: {} for e in self.eng}
        self.released = {}
        self.uid = 0
        self.banks = []
        self.ninst = 0

    def tile(self, stack, shape, dtype, name=None):
        self.uid += 1
        nm = "%s_%d" % (name or "t", self.uid)
        t = stack.enter_context(self.nc.sbuf_tensor(nm, list(shape), dtype))
        tl = Tile(nm, t)
        tl.r = dict(self.released)
        return tl

    def release(self, tiles):
        for tl in tiles:
            for k, v in list(tl.w.items()) + list(tl.r.items()):
                if self.released.get(k, 0) < v:
                    self.released[k] = v

    @contextlib.contextmanager
    def scope(self):
        st = contextlib.ExitStack()
        tiles = []

        def mk(shape, dtype, name=None):
            tl = self.tile(st, shape, dtype, name)
            tiles.append(tl)
            return tl

        try:
            yield mk
        finally:
            self.release(tiles)
            st.close()

    def init_psum(self):
        for i in range(8):
            t = self.es.enter_context(self.nc.psum_tensor("psb%d" % i, [128, 512], F32))
            self.banks.append(Tile("psb%d" % i, t))
            self.banks[-1].psum = True

    def bank(self):
        b = self.banks.pop(0)
        self.banks.append(b)
        return b

    def bank_reserve(self):
        return self.banks.pop(0)

    def bank_release(self, b):
        self.banks.append(b)

    def _semof(self, key):
        if isinstance(key, str):
            return self.sem[key]
        return self.dsem[key[1]]

    def _wait(self, e, key, val):
        if self.waited[e].get(key, 0) >= val:
            return
        self.waited[e][key] = val
        self.eng[e].wait_ge(self._semof(key), val)
        self.ninst += 1

    def _deps(self, e, r, w, append=False):
        deps = {}

        def add(ev, same_ok):
            if ev is None:
                return
            k, v = ev
            if k == e and not same_ok:
                return
            if deps.get(k, 0) < v:
                deps[k] = v

        for b in r:
            for ev in b.w.items():
                add(ev, e != "pe")
            if getattr(b, "psum", False):
                for ev in b.r.items():
                    add(ev, False)
        for b in w:
            if not append:
                for ev in b.w.items():
                    add(ev, False)
            for ev in b.r.items():
                add(ev, False)
        for k, v in deps.items():
            self._wait(e, k, v)

    def _record(self, ev, r, w, append=False):
        k, v = ev
        for b in r:
            if b.r.get(k, 0) < v:
                b.r[k] = v
        for b in w:
            if append:
                b.w[k] = v
            else:
                b.w = {k: v}
            b.r = {}

    def op(self, e, fn, r=(), w=(), sig=True):
        self._deps(e, r, w)
        ins = fn()
        self.ninst += 1
        if sig:
            self.cnt[e] += 1
            ins.then_inc(self.sem[e], 1)
            ev = (e, self.cnt[e])
        else:
            ev = (e, self.cnt[e] + 1)
        self._record(ev, r, w)
        return ins

    def dma(self, q, out, in_, r=(), w=(), append=False):
        self._deps(q, r, w, append)
        j = self.dnext
        self.dnext = (j + 1) % NDS
        if self.dcnt[j] > 0:
            self._wait(q, ("d", j), 16 * self.dcnt[j])
        self.dcnt[j] += 1
        self.eng[q].dma_start(out=out, in_=in_).then_inc(self.dsem[j], 16)
        self.ninst += 1
        self._record((("d", j), 16 * self.dcnt[j]), r, w, append)

    def finish(self):
        for j in range(NDS):
            if self.dcnt[j] > 0:
                self._wait("sp", ("d", j), 16 * self.dcnt[j])
        for e in ("pe", "act", "dve", "pool"):
            if self.cnt[e] > 0:
                self._wait("sp", e, self.cnt[e])

    def mm(self, out, lhsT, rhs, start=True, stop=True, r=(), w=(), sig=None):
        if sig is None:
            sig = stop
        return self.op("pe", lambda: self.nc.tensor.matmul(out, lhsT=lhsT, rhs=rhs, start=start, stop=stop), r, w, sig)

    def tr(self, out, in_, ident, r=(), w=(), sig=True):
        return self.op("pe", lambda: self.nc.tensor.transpose(out=out, in_=in_, identity=ident), r, w, sig)

    def act(self, out, in_, func, r=(), w=(), **kw):
        return self.op("act", lambda: self.nc.scalar.activation(out=out, in_=in_, func=func, **kw), r, w)

    def v(self, name, r=(), w=(), **kw):
        return self.op("dve", lambda: getattr(self.nc.vector, name)(**kw), r, w)

    def g(self, name, r=(), w=(), **kw):
        return self.op("pool", lambda: getattr(self.nc.gpsimd, name)(**kw), r, w)


def build(T, NS, L=2):
    NB = T // 512
    NSQ = NS * 4
    nc = bass.Bass("TRN2", target_bir_lowering=False)

    def din(name, shape, dt=F32):
        return nc.dram_tensor(name, list(shape), dt, kind="ExternalInput").ap()

    def dout(name, shape, dt=F32):
        return nc.dram_tensor(name, list(shape), dt, kind="ExternalOutput").ap()

    def dscr(name, shape, dt=F32):
        return nc.dram_tensor(name, list(shape), dt, kind="Internal").ap()

    I = {}
    I["xp"] = din("xp", [T, D])
    I["xs"] = din("xs", [NSQ, D])
    I["cw1"] = din("cw1", [L, NS, 128, 1024])
    I["cw2"] = din("cw2", [L, NS, 512, 1024])
    I["cw3"] = din("cw3", [L, NS, 2048, 1024])
    I["sgdn"] = din("sgdn", [L, NS, 8, 128, 128])
    I["sconv"] = din("sconv", [L, NS * 3, 3072])
    I["spool"] = din("spool", [L, NS * 15, 1024])
    I["w_in"] = din("w_in", [L, D, NIN])
    I["convw"] = din("convw", [L, 128, 24, 4])
    I["alog"] = din("alog", [L, 128, 4, 8])
    I["dtb"] = din("dtb", [L, 128, 4, 8])
    I["ggain"] = din("ggain", [L, 128, 1])
    I["w_pool"] = din("w_pool", [L, 4, 256, 256])
    I["pscale"] = din("pscale", [L, 128, 8])
    I["w_br_a"] = din("w_br_a", [L, 1024, D])
    I["w_br_b"] = din("w_br_b", [L, 512, D])
    I["w_br_c"] = din("w_br_c", [L, 1024, D])
    I["w_out"] = din("w_out", [L, D, D])
    I["w_gu"] = din("w_gu", [L, D, 2 * DFF])
    I["w_down"] = din("w_down", [L, DFF, D])
    for nm in ("gpre", "gpost", "gpre2", "gpost2"):
        I[nm] = din(nm, [L, 128, 16])
    I["ident"] = din("ident", [128, 128])
    I["umat"] = din("umat", [128, 128])
    I["maskS4"] = din("maskS4", [128, 4, 128])
    I["maskI4"] = din("maskI4", [128, 4, 128])
    I["ident4"] = din("ident4", [128, 4, 128])
    I["amask"] = din("amask", [128, 7, 128])
    I["pcorr"] = din("pcorr", [128, 4, 16])
    I["smask"] = din("smask", [128, 12, 4])

    O = {}
    O["yp"] = dout("yp", [T, D])
    O["ys"] = dout("ys", [NSQ, D])
    PW = [min(w, T) for w in WINS]
    O["w1p"] = dout("w1p", [L, PW[0], 1024])
    O["w2p"] = dout("w2p", [L, PW[1], 1024])
    O["w3p"] = dout("w3p", [L, PW[2], 1024])
    O["gdnp"] = dout("gdnp", [L, 8, 128, 128])
    O["convp"] = dout("convp", [L, 3, 3072])
    O["poolp"] = dout("poolp", [L, 15, 1024])
    O["w1s"] = dout("w1s", [L, NS, 128, 1024])
    O["w2s"] = dout("w2s", [L, NS, 512, 1024])
    O["w3s"] = dout("w3s", [L, NS, 2048, 1024])
    O["gdns"] = dout("gdns", [L, NS, 8, 128, 128])
    O["convs"] = dout("convs", [L, NS, 3, 3072])
    O["pools"] = dout("pools", [L, NS, 15, 1024])
    if DBG_DUMP:
        O["dbg"] = dout("dbg", [40, 128, 512])
    WP = [O["w1p"], O["w2p"], O["w3p"]]
    WS = [O["w1s"], O["w2s"], O["w3s"]]
    CW = [I["cw1"], I["cw2"], I["cw3"]]

    x1T = dscr("x1T", [KC, 128, T])
    xs1T = dscr("xs1T", [KC, 128, NSQ])
    kTs = dscr("kTs", [3, 4, 128, T], BF16)
    Vs = dscr("Vs", [3, 4, T, 128], BF16)
    d_x1T, d_xs1T, d_kTs, d_Vs = Trk("x1T"), Trk("xs1T"), Trk("kTs"), Trk("Vs")

    es = contextlib.ExitStack()
    with es:
        S = Sched(nc, es)
        S.init_psum()
        mm, act, V_, G_, dma, tr = S.mm, S.act, S.v, S.g, S.dma, S.tr

        def gt(shape, dt, name):
            return S.tile(es, shape, dt, name)

        ident = gt([128, 128], F32, "ident")
        umat = gt([128, 128], F32, "umat")
        ones_f = gt([128, 128], F32, "ones_f")
        ones_b = gt([128, 128], BF16, "ones_b")
        one_c = gt([128, 1], F32, "one_c")
        maskS4 = gt([128, 4, 128], F32, "maskS4")
        maskI4 = gt([128, 4, 128], F32, "maskI4")
        ident4 = gt([128, 4, 128], F32, "ident4")
        amask = gt([128, 7, 128], BF16, "amask")
        pcorr = gt([128, 4, 16], F32, "pcorr")
        smask = gt([128, 12, 4], F32, "smask")
        for tl, nm in ((ident, "ident"), (umat, "umat"), (maskS4, "maskS4"), (maskI4, "maskI4"),
                       (ident4, "ident4"), (pcorr, "pcorr"), (smask, "smask")):
            dma("sp", tl[:], I[nm], w=[tl])
        dma("pool", amask[:], I["amask"], w=[amask])
        G_("memset", w=[ones_f], ap=ones_f[:], constant=1.0)
        G_("memset", w=[ones_b], ap=ones_b[:], constant=1.0)
        G_("memset", w=[one_c], ap=one_c[:], constant=1.0)

        d_ws = Trk("ws")
        cc_list = []
        for l in range(L):
            for g in range(3):
                w = WINS[g]
                for s in range(NS):
                    nch_ = max(1, (w - 4) // 512)
                    rows = [(i * (w - 4)) // nch_ for i in range(nch_ + 1)]
                    for i in range(nch_):
                        cc_list.append((WS[g][l, s, rows[i]:rows[i + 1], :], CW[g][l, s, 4 + rows[i]:4 + rows[i + 1], :]))
            for s in range(NS):
                cc_list.append((O["pools"][l, s, 0:11, :], I["spool"][l, s * 15 + 4:s * 15 + 15, :]))

        def emit_cache_copies(n):
            for _ in range(n):
                if cc_list:
                    o_, i_ = cc_list.pop(0)
                    dma("act", o_, i_, w=[d_ws], append=True)

        convw = gt([128, 24, 4], F32, "convw")
        alog = gt([128, 4, 8], F32, "alog")
        dtb = gt([128, 4, 8], F32, "dtb")
        negA = gt([128, 4, 8], F32, "negA")
        ggain = gt([128, 1], F32, "ggain")
        pscale = gt([128, 8], F32, "pscale")
        gpre, gpost, gpre2, gpost2 = (gt([128, 16], F32, n) for n in ("gpre", "gpost", "gpre2", "gpost2"))
        wpool = gt([128, 4, 2, 256], BF16, "wpool")
        wba = gt([128, KC, 16], BF16, "wba")

        WB = [gt([128, 8192], BF16, "wb%d" % i) for i in range(3)]
        wstate = {"i": 0}

        wscr_chunks = [dscr("wscr%d" % i, [32, 128, 8192], BF16) for i in range(7)]

        def wscr_at(slot):
            return wscr_chunks[slot // 32][slot % 32]
        wslots = {}

        def wload(parts, key):
            wb = WB[wstate["i"] % 3]
            wstate["i"] += 1
            off = 0
            views = []
            hit = key in wslots
            for ap_ in parts:
                shp = list(ap_.shape)[1:]
                n = int(np.prod(shp))
                flat = wb.t[:, off:off + n]
                if len(shp) == 2:
                    vw = flat.rearrange("p (a b) -> p a b", a=shp[0])
                elif len(shp) == 3:
                    vw = flat.rearrange("p (a b c) -> p a b c", a=shp[0], b=shp[1])
                else:
                    raise ValueError(shp)
                if not hit:
                    if len(shp) == 3:
                        for bi in range(shp[1]):
                            dma("pool", vw[:, :, bi, :], ap_[:, :, bi, :], w=[wb], append=(len(views) > 0 or bi > 0))
                    else:
                        dma("pool", vw, ap_, w=[wb], append=(len(views) > 0))
                views.append(vw)
                off += n
            assert off <= 8192, off
            if hit:
                slot, stk = wslots[key]
                dma("pool", wb.t[:, :off], wscr_at(slot)[:, :off], r=[stk], w=[wb])
            elif key is not None:
                slot = len(wslots)
                assert slot < 7 * 32
                stk = Trk("ws%d" % slot)
                wslots[key] = (slot, stk)
                dma("sp", wscr_at(slot)[:, :off], wb.t[:, :off], r=[wb], w=[stk])
            return wb, views

        def win_cols(l, c0, n):
            return I["w_in"][l][:, c0:c0 + n].rearrange("(kc p) n -> p kc n", p=128)

        def rows_cols(w2d, c0, n):
            return w2d[:, c0:c0 + n].rearrange("(kc p) n -> p kc n", p=128)

        ctail = gt([128, 24, 3], F32, "ctail")
        ptail = gt([128, 8, 15], F32, "ptail")
        sstate = gt([128, 8, 128], F32, "sstate")
        sstate_t = [Trk("sstate%d" % i) for i in range(8)]
        xT = gt([128, KC, 512], F32, "xT")
        hT = gt([128, KC, 512], BF16, "hT")

        def dump(slot, ap_, trk, n=512):
            if not DBG_DUMP:
                return
            with S.scope() as mkd:
                tmp = mkd([128, 512], F32, "dbgt")
                V_("tensor_copy", r=[trk], w=[tmp], out=tmp[:, :n], in_=ap_)
                dma("sp", O["dbg"][slot, :, :n], tmp[:, :n], r=[tmp], w=[])

        def rstd_from(mk, srcs, trks, ntok, div):
            ssb = S.bank_reserve()
            sqs = [mk([128, 512], F32, "sq") for _ in range(2)]
            n = len(srcs)
            for i in range(n):
                sq = sqs[i % 2]
                act(sq[:, :ntok], srcs[i], AF.Square, r=[trks[i]], w=[sq])
                mm(ssb[:, :ntok], ones_f[:], sq[:, :ntok], start=(i == 0), stop=(i == n - 1), r=[ones_f, sq], w=[ssb], sig=True)
            rstd = mk([128, 512], F32, "rstd")
            V_("tensor_scalar", r=[ssb], w=[rstd], out=rstd[:, :ntok], in0=ssb[:, :ntok], scalar1=1.0 / div,
               scalar2=EPS, op0=ALU.mult, op1=ALU.add)
            act(rstd[:, :ntok], rstd[:, :ntok], AF.Ln, r=[rstd], w=[rstd])
            act(rstd[:, :ntok], rstd[:, :ntok], AF.Exp, r=[rstd], w=[rstd], scale=-0.5)
            S.bank_release(ssb)
            return rstd

        def norm_to_hT(gain, ntok):
            with S.scope() as mk:
                rstd = rstd_from(mk, [xT[:, kc, :ntok] for kc in range(KC)], [xT] * KC, ntok, float(D))
                for kc in range(KC):
                    V_("scalar_tensor_tensor", r=[xT, gain, rstd], w=[hT], out=hT[:, kc, :ntok], in0=xT[:, kc, :ntok],
                       scalar=gain[:, kc:kc + 1], in1=rstd[:, :ntok], op0=ALU.mult, op1=ALU.mult)

        def proj_fm(lhs_of_kc, ntok, r_w):
            pb = S.bank()
            for kc in range(KC):
                mm(pb[:, :ntok], lhs_of_kc(kc), hT[:, kc, :ntok], start=(kc == 0), stop=(kc == KC - 1), r=[r_w, hT], w=[pb])
            return pb

        def load_layer_params(l):
            for tl, nm in ((convw, "convw"), (alog, "alog"), (dtb, "dtb"), (ggain, "ggain"), (pscale, "pscale"),
                           (gpre, "gpre"), (gpost, "gpost"), (gpre2, "gpre2"), (gpost2, "gpost2")):
                dma("sp", tl[:], I[nm][l], w=[tl])
            dma("pool", wpool[:], I["w_pool"][l].rearrange("g (kt p) n -> p g kt n", p=128), w=[wpool])
            dma("pool", wba[:], win_cols(l, O_BA, 16), w=[wba])
            act(negA[:], alog[:], AF.Exp, r=[alog], w=[negA])
            V_("tensor_scalar", r=[negA], w=[negA], out=negA[:], in0=negA[:], scalar1=-1.0, scalar2=None, op0=ALU.mult)

        def gdn(l, b, prompt, brT):
            nseq = 1 if prompt else NS
            Tq = 512 if prompt else 4
            C = 128 if prompt else 4
            nch = 4 if prompt else NS
            ntok = nch * C
            nlev = 6 if prompt else 1
            last = prompt and (b == NB - 1)
            with S.scope() as mk:
                chist = sst = None
                if not prompt:
                    chist = mk([128, 24, NS, 3], F32, "chist")
                    with S.scope() as mk3:
                        craw = mk3([NS * 3, 3072], F32, "craw")
                        dma("sp", craw[:NS * 3, :], I["sconv"][l], w=[craw])
                        pb = S.bank()
                        for t_ in range(24):
                            tr(pb[:, t_ * NS * 3:(t_ + 1) * NS * 3], craw[:NS * 3, t_ * 128:(t_ + 1) * 128], ident[:NS * 3, :NS * 3],
                               r=[craw, ident], w=[pb], sig=(t_ == 23))
                        act(chist[:].rearrange("p t s j -> p (t s j)"), pb[:, :24 * NS * 3], AF.Copy, r=[pb], w=[chist])
                    sst = mk([128, NS, 8, 128], F32, "sst")
                    for s in range(NS):
                        dma("sp", sst[:, s, :, :], I["sgdn"][l, s].rearrange("h k v -> k h v"), w=[sst], append=(s > 0))
                gp = S.bank()
                for c in range(nch):
                    for kc in range(KC):
                        mm(gp[:C, c * 16:(c + 1) * 16], hT[:, kc, c * C:(c + 1) * C], wba[:, kc, :], start=(kc == 0),
                           stop=(kc == KC - 1), r=[hT, wba], w=[gp])
                gview = gp[:C, :nch * 16].rearrange("p (c n) -> p c n", n=16)
                sm = lambda nm: mk([128, 4, 8], F32, nm)
                beta, nbeta, spx, gg, gc, ngc, egc, bg = (sm(n) for n in ("beta", "nbeta", "spx", "gg", "gc", "ngc", "egc", "bg"))
                act(beta[:C, :nch, :], gview[:, :, 0:8], AF.Sigmoid, r=[gp], w=[beta])
                V_("tensor_tensor", r=[gp, dtb], w=[spx], out=spx[:C, :nch, :], in0=gview[:, :, 8:16], in1=dtb[:C, :nch, :], op=ALU.add)
                act(spx[:C, :nch, :], spx[:C, :nch, :], AF.Exp, r=[spx], w=[spx])
                act(spx[:C, :nch, :], spx[:C, :nch, :], AF.Ln, r=[spx, one_c], w=[spx], bias=one_c[:C, :])
                V_("tensor_tensor", r=[spx, negA], w=[gg], out=gg[:C, :nch, :], in0=spx[:C, :nch, :], in1=negA[:C, :nch, :], op=ALU.mult)
                gcb = S.bank()
                for c in range(nch):
                    mm(gcb[:C, c * 8:(c + 1) * 8], umat[:C, :C], gg[:C, c, :], r=[umat, gg], w=[gcb])
                act(gc[:C, :nch, :], gcb[:C, :nch * 8].rearrange("p (c n) -> p c n", n=8), AF.Copy, r=[gcb], w=[gc])
                V_("tensor_scalar", r=[gc], w=[ngc], out=ngc[:C, :nch, :], in0=gc[:C, :nch, :], scalar1=-1.0, scalar2=None, op0=ALU.mult)
                V_("tensor_scalar", r=[beta], w=[nbeta], out=nbeta[:C, :nch, :], in0=beta[:C, :nch, :], scalar1=-1.0, scalar2=None, op0=ALU.mult)
                act(egc[:C, :nch, :], gc[:C, :nch, :], AF.Exp, r=[gc], w=[egc])
                V_("tensor_tensor", r=[beta, egc], w=[bg], out=bg[:C, :nch, :], in0=beta[:C, :nch, :], in1=egc[:C, :nch, :], op=ALU.mult)

                wq4 = I["w_in"][l][:, 0:4096].rearrange("(kc p) (j h c) -> p kc j h c", p=128, j=4, h=8)

                def mkset(i):
                    return ([mk([128, 520], F32, "gs%d_%d" % (i, k)) for k in range(13)]
                            + [mk([128, 128], F32, "vn%d_%d" % (i, k)) for k in range(2)] + [mk([128, 4], F32, "egl%d" % i)])

                sets = [mkset(0), mkset(1)]

                def head(hd, st):
                    (s1, s2, s3, s4, s5, s6, s7, s8, s9, s10, s11, s12, s13) = st[:13]
                    vns = st[13:15]
                    egl = st[15]
                    f4 = lambda tl: tl.t[:, :512].rearrange("p (c n) -> p c n", c=4)
                    fl = lambda tl: tl.t[:, :ntok]
                    pre = s1.t[:, :nseq * (3 + Tq)].rearrange("p (s t) -> p s t", s=nseq)
                    cv = s2.t[:, :nseq * Tq].rearrange("p (s t) -> p s t", s=nseq)
                    wb, (wv,) = wload([wq4[:, :, :, hd, :]], ("gdn", l, hd))
                    for j, dst in ((0, s4), (1, s5), (2, s6)):
                        pb = proj_fm(lambda kc: wv[:, kc, j, :], ntok, wb)
                        ctile = j * 8 + hd
                        act(pre[:, :, 3:3 + Tq], pb[:, :ntok].rearrange("p (s t) -> p s t", s=nseq), AF.Copy, r=[pb], w=[s1])
                        if prompt:
                            V_("tensor_copy", r=[ctail], w=[s1], out=pre[:, 0, 0:3], in_=ctail[:, ctile, :])
                        else:
                            V_("tensor_copy", r=[chist], w=[s1], out=pre[:, :, 0:3], in_=chist[:, ctile, :, :])
                        V_("tensor_scalar", r=[s1, convw], w=[s2], out=cv, in0=pre[:, :, 0:Tq], scalar1=convw[:, ctile, 0:1],
                           scalar2=None, op0=ALU.mult)
                        for jj in range(1, 4):
                            V_("scalar_tensor_tensor", r=[s1, convw, s2], w=[s2], out=cv, in0=pre[:, :, jj:jj + Tq],
                               scalar=convw[:, ctile, jj:jj + 1], in1=cv, op0=ALU.mult, op1=ALU.add)
                        if prompt:
                            V_("tensor_copy", r=[s1], w=[ctail], out=ctail[:, ctile, :], in_=pre[:, 0, Tq:Tq + 3])
                        cvf = s2.t[:, :ntok]
                        if j == 2:
                            act(fl(s6), cvf, AF.Silu, r=[s2], w=[s6])
                        else:
                            act(fl(s3), cvf, AF.Silu, r=[s2], w=[s3])
                            with S.scope() as mk2:
                                rs = rstd_from(mk2, [fl(s3)], [s3], ntok, 1.0)
                                V_("scalar_tensor_tensor", r=[s3, rs], w=[dst], out=fl(dst), in0=fl(s3),
                                   scalar=(ISQ if j == 0 else 1.0), in1=rs[:, :ntok], op0=ALU.mult, op1=ALU.mult)
                        yield
                    pb = proj_fm(lambda kc: wv[:, kc, 3, :], ntok, wb)
                    act(fl(s7), pb[:, :ntok], AF.Silu, r=[pb], w=[s7])
                    Gb, Dm, Eg, DT = f4(s1), f4(s2), f4(s3), f4(s9)
                    for c in range(nch):
                        V_("tensor_scalar", r=[ones_f, gg], w=[s1], out=Gb[:C, c, :], in0=ones_f[:C, :], scalar1=gg[:C, c, hd:hd + 1],
                           scalar2=None, op0=ALU.mult)
                    rb = S.bank()
                    for c in range(nch):
                        mm(rb[:, c * 128:c * 128 + C], Gb[:C, c, :], umat[:C, :C], r=[s1, umat], w=[rb])
                    rb3 = rb[:].rearrange("p (c n) -> p c n", c=4)
                    V_("scalar_tensor_tensor", r=[rb, maskS4], w=[s2], out=Dm[:C, :nch, :C], in0=rb3[:C, :nch, :C], scalar=-1.0,
                       in1=maskS4[:C, :nch, :C], op0=ALU.mult, op1=ALU.add)
                    V_("tensor_tensor", r=[rb, maskI4], w=[s9], out=DT[:C, :nch, :C], in0=rb3[:C, :nch, :C], in1=maskI4[:C, :nch, :C], op=ALU.add)
                    act(Eg[:, :nch, :C], rb3[:, :nch, :C], AF.Exp, r=[rb], w=[s3])
                    for c in range(nch):
                        act(Dm[:C, c, :C], Dm[:C, c, :C], AF.Exp, r=[s2, gc], w=[s2], bias=gc[:C, c, hd:hd + 1])
                        act(DT[:C, c, :C], DT[:C, c, :C], AF.Exp, r=[s9, ngc], w=[s9], bias=ngc[:C, c, hd:hd + 1])
                    V_("tensor_copy", r=[s3], w=[egl], out=egl[:, :nch], in_=Eg[:, :nch, C - 1])
                    V_("tensor_tensor", r=[s4, s3], w=[s8], out=fl(s8).rearrange("p (c t) -> p c t", c=nch),
                       in0=fl(s4).rearrange("p (c t) -> p c t", c=nch), in1=Eg[:, :nch, :C], op=ALU.mult)
                    yield
                    W_a, AT = f4(s10), f4(s11)
                    kkb = S.bank()
                    qkb = S.bank()
                    for c in range(nch):
                        cs = slice(c * C, (c + 1) * C)
                        mm(kkb[:C, c * 128:c * 128 + C], s5.t[:, cs], s5.t[:, cs], r=[s5], w=[kkb])
                        mm(qkb[:C, c * 128:c * 128 + C], s5.t[:, cs], s4.t[:, cs], r=[s5, s4], w=[qkb])
                    kk3 = kkb[:].rearrange("p (c n) -> p c n", c=4)
                    qk3 = qkb[:].rearrange("p (c n) -> p c n", c=4)
                    for c in range(nch):
                        V_("scalar_tensor_tensor", r=[kkb, nbeta, s2], w=[s10], out=W_a[:C, c, :C], in0=kk3[:C, c, :C],
                           scalar=nbeta[:C, c, hd:hd + 1], in1=Dm[:C, c, :C], op0=ALU.mult, op1=ALU.mult)
                    V_("tensor_tensor", r=[qkb, s9], w=[s11], out=AT[:C, :nch, :C], in0=qk3[:C, :nch, :C], in1=DT[:C, :nch, :C], op=ALU.mult)
                    yield
                    N_a, RT = f4(s1), f4(s12)
                    tb = S.bank()
                    for c in range(nch):
                        tr(tb[:C, c * 128:c * 128 + C], W_a[:C, c, :C], ident[:C, :C], r=[s10, ident], w=[tb], sig=(c == nch - 1))
                    tb3 = tb[:].rearrange("p (c n) -> p c n", c=4)
                    act(N_a[:C, :nch, :C], tb3[:C, :nch, :C], AF.Copy, r=[tb], w=[s1])
                    V_("tensor_tensor", r=[tb, ident4], w=[s12], out=RT[:C, :nch, :C], in0=tb3[:C, :nch, :C], in1=ident4[:C, :nch, :C], op=ALU.add)
                    yield
                    Wc, Nc, Wn, Nn = s10, s1, s3, s2
                    for k in range(1, nlev + 1):
                        wb_ = S.bank()
                        for c in range(nch):
                            mm(wb_[:C, c * 128:c * 128 + C], f4(Nc)[:C, c, :C], f4(Wc)[:C, c, :C], r=[Nc, Wc], w=[wb_])
                        act(f4(Wn)[:C, :nch, :C], wb_[:].rearrange("p (c n) -> p c n", c=4)[:C, :nch, :C], AF.Copy, r=[wb_], w=[Wn])
                        if k < nlev:
                            nb_ = S.bank()
                            for c in range(nch):
                                mm(nb_[:C, c * 128:c * 128 + C], f4(Wc)[:C, c, :C], f4(Nc)[:C, c, :C], r=[Nc, Wc], w=[nb_])
                            V_("tensor_copy", r=[nb_], w=[Nn], out=f4(Nn)[:C, :nch, :C], in_=nb_[:].rearrange("p (c n) -> p c n", c=4)[:C, :nch, :C])
                        yield
                        pb_ = S.bank()
                        for c in range(nch):
                            mm(pb_[:C, c * 128:c * 128 + C], f4(Wn)[:C, c, :C], RT[:C, c, :C], r=[Wn, s12], w=[pb_])
                        V_("tensor_tensor", r=[pb_, s12], w=[s12], out=RT[:C, :nch, :C], in0=RT[:C, :nch, :C],
                           in1=pb_[:].rearrange("p (c n) -> p c n", c=4)[:C, :nch, :C], op=ALU.add)
                        Wc, Wn = Wn, Wc
                        Nc, Nn = Nn, Nc
                        yield
                    kbg, kd, vb = f4(s4), f4(s10), f4(s13)
                    ktb = S.bank()
                    vtb = S.bank()
                    for c in range(nch):
                        cs = slice(c * C, (c + 1) * C)
                        tr(ktb[:C, c * 128:(c + 1) * 128], s5.t[:, cs], ident[:], r=[s5, ident], w=[ktb], sig=(c == nch - 1))
                    for c in range(nch):
                        cs = slice(c * C, (c + 1) * C)
                        tr(vtb[:C, c * 128:(c + 1) * 128], s6.t[:, cs], ident[:], r=[s6, ident], w=[vtb], sig=(c == nch - 1))
                    for c in range(nch):
                        V_("tensor_scalar", r=[ktb, bg], w=[s4], out=kbg[:C, c, :], in0=ktb[:C, c * 128:(c + 1) * 128],
                           scalar1=bg[:C, c, hd:hd + 1], scalar2=None, op0=ALU.mult)
                        V_("tensor_scalar", r=[ktb, s9], w=[s10], out=kd[:C, c, :], in0=ktb[:C, c * 128:(c + 1) * 128],
                           scalar1=DT[:C, c, C - 1:C], scalar2=None, op0=ALU.mult)
                        V_("tensor_scalar", r=[vtb, beta], w=[s13], out=vb[:C, c, :], in0=vtb[:C, c * 128:(c + 1) * 128],
                           scalar1=beta[:C, c, hd:hd + 1], scalar2=None, op0=ALU.mult)
                    yield
                    u_, wT = f4(s1), f4(s2)
                    ub = S.bank()
                    wtb = S.bank()
                    for c in range(nch):
                        mm(ub[:C, c * 128:(c + 1) * 128], RT[:C, c, :C], vb[:C, c, :], r=[s12, s13], w=[ub])
                    for c in range(nch):
                        mm(wtb[:, c * 128:c * 128 + C], kbg[:C, c, :], RT[:C, c, :C], r=[s12, s4], w=[wtb])
                    act(u_[:C, :nch, :], ub[:].rearrange("p (c n) -> p c n", c=4)[:C, :nch, :], AF.Copy, r=[ub], w=[s1])
                    V_("tensor_copy", r=[wtb], w=[s2], out=wT[:, :nch, :C], in_=wtb[:].rearrange("p (c n) -> p c n", c=4)[:, :nch, :C])
                    yield
                    ob = S.bank_reserve()
                    for c in range(nch):
                        cs = slice(c * C, (c + 1) * C)
                        if prompt:
                            s_ap, s_tk = sstate[:, hd, :], sstate_t[hd]
                        else:
                            s_ap, s_tk = sst[:, c, hd, :], sst
                        vn = vns[c % 2]
                        wsb = S.bank()
                        mm(wsb[:C, :128], wT[:, c, :C], s_ap, r=[s2, s_tk], w=[wsb])
                        V_("tensor_tensor", r=[s1, wsb], w=[vn], out=vn[:C, :], in0=u_[:C, c, :], in1=wsb[:C, :128], op=ALU.subtract)
                        mm(ob[:, cs], s_ap, s8.t[:, cs], start=True, stop=False, r=[s_tk, s8], w=[ob])
                        mm(ob[:, cs], vn[:C, :], AT[:C, c, :C], start=False, stop=True, r=[vn, s11], w=[ob])
                        sb_ = S.bank()
                        mm(sb_[:, :128], kd[:C, c, :], vn[:C, :], r=[s10, vn], w=[sb_])
                        V_("scalar_tensor_tensor", r=[s_tk, egl, sb_], w=[s_tk], out=s_ap, in0=s_ap, scalar=egl[:, c:c + 1],
                           in1=sb_[:, :128], op0=ALU.mult, op1=ALU.add)
                        yield
                    act(fl(s3), ob[:, :ntok], AF.Copy, r=[ob], w=[s3])
                    S.bank_release(ob)
                    with S.scope() as mk2:
                        rs = rstd_from(mk2, [fl(s3)], [s3], ntok, 128.0)
                        V_("tensor_tensor", r=[s3, rs], w=[s3], out=fl(s3), in0=fl(s3), in1=rs[:, :ntok], op=ALU.mult)
                    V_("scalar_tensor_tensor", r=[s3, ggain, s7], w=[brT], out=brT[:, hd, :ntok], in0=fl(s3), scalar=ggain[:, 0:1],
                       in1=fl(s7), op0=ALU.mult, op1=ALU.mult)
                    if last:
                        dma("sp", O["gdnp"][l, hd], sstate[:, hd, :], r=[sstate_t[hd]], w=[])

                for pair in range(4):
                    alive = [head(2 * pair, sets[0]), head(2 * pair + 1, sets[1])]
                    while alive:
                        for g_ in list(alive):
                            try:
                                next(g_)
                            except StopIteration:
                                alive.remove(g_)
                if not prompt:
                    for s in range(NS):
                        dma("sp", O["gdns"][l, s].rearrange("h k v -> k h v"), sst[:, s, :, :], r=[sst], w=[])

        def attn_prompt(l, b, brT):
            last = (b == NB - 1)
            wq_ = I["w_in"][l][:, O_QB:O_QB + 1536].rearrange("(kc p) (g h c) -> p kc g h c", p=128, g=3, h=4)
            wk_ = I["w_in"][l][:, O_KB:O_KB + 1536].rearrange("(kc p) (g h c) -> p kc g h c", p=128, g=3, h=4)
            wv_ = I["w_in"][l][:, O_VB:O_VB + 1536].rearrange("(kc p) (g h c) -> p kc g h c", p=128, g=3, h=4)
            need = [(b + 1) * 512 > T - PW[g] for g in range(3)]
            t0 = [max(0, 4 * b - 1), max(0, 4 * b - 4), 0]
            nh = [4 * b - t0[g] for g in range(3)]
            hoff = [0, nh[0], nh[0] + nh[1]]
            nhT = sum(nh)
            with S.scope() as mk:
                qT = mk([128, 3, 512], BF16, "aq")
                kTc = mk([128, 3, 512], BF16, "ak")
                Vc = mk([128, 4, 3, 128], BF16, "av")
                kTh = mk([128, max(nhT, 1) * 128], BF16, "akh")
                Vh = mk([128, max(nhT, 1), 128], BF16, "avh")
                kst = [mk([128, 3, 128], F32, "kst%d" % i) for i in range(2)]
                Pts = [mk([128, 512], BF16, "Pt%d" % i) for i in range(3)]
                rec = mk([128, 512], F32, "rec")
                pcount = 0
                for hs in range(4):
                    wbq, (qv,) = wload([wq_[:, :, :, hs, :]], ("aq", l, hs))
                    for g in range(3):
                        pb = proj_fm(lambda kc: qv[:, kc, g, :], 512, wbq)
                        act(qT[:, g, :], pb[:], AF.Copy, r=[pb], w=[qT])
                    _chk(1.2)
                    wbk, (kv,) = wload([wk_[:, :, :, hs, :]], ("ak", l, hs))
                    for g in range(3):
                        pb = proj_fm(lambda kc: kv[:, kc, g, :], 512, wbk)
                        act(kTc[:, g, :], pb[:], AF.Copy, r=[pb], w=[kTc])
                        if not last:
                            dma("sp", kTs[g, hs, :, b * 512:(b + 1) * 512], kTc[:, g, :], r=[kTc], w=[d_kTs], append=True)
                    if any(need):
                        for tt in range(4):
                            pb = S.bank()
                            for kc in range(KC):
                                mm(pb[:, :384], hT[:, kc, tt * 128:(tt + 1) * 128], kv[:, kc].rearrange("p g c -> p (g c)"),
                                   start=(kc == 0), stop=(kc == KC - 1), r=[hT, wbk], w=[pb])
                            st = kst[tt % 2]
                            act(st[:].rearrange("p g c -> p (g c)"), pb[:, :384], AF.Copy, r=[pb], w=[st])
                            tok0 = b * 512 + tt * 128
                            for g in range(3):
                                r0 = tok0 - (T - PW[g])
                                if r0 >= 0:
                                    dma("sp", WP[g][l, r0:r0 + 128, hs * 128:(hs + 1) * 128], st[:, g, :], r=[st], w=[])
                    _chk(1.4)
                    wbv, (vv,) = wload([wv_[:, :, :, hs, :]], ("av", l, hs))
                    for tt in range(4):
                        pb = S.bank()
                        for kc in range(KC):
                            mm(pb[:, :384], hT[:, kc, tt * 128:(tt + 1) * 128], vv[:, kc].rearrange("p g c -> p (g c)"),
                               start=(kc == 0), stop=(kc == KC - 1), r=[hT, wbv], w=[pb])
                        V_("tensor_copy", r=[pb], w=[Vc], out=Vc[:, tt, :, :].rearrange("p g c -> p (g c)"), in_=pb[:, :384])
                        tok0 = b * 512 + tt * 128
                        if any(tok0 - (T - PW[g]) >= 0 for g in range(3)):
                            st = kst[tt % 2]
                            act(st[:].rearrange("p g c -> p (g c)"), pb[:, :384], AF.Copy, r=[pb], w=[st])
                            for g in range(3):
                                r0 = tok0 - (T - PW[g])
                                if r0 >= 0:
                                    dma("sp", WP[g][l, r0:r0 + 128, 512 + hs * 128:512 + (hs + 1) * 128], st[:, g, :], r=[st], w=[])
                    if not last:
                        for g in range(3):
                            dma("sp", Vs[g, hs, b * 512:(b + 1) * 512, :].rearrange("(n p) d -> p n d", p=128), Vc[:, :, g, :], r=[Vc], w=[d_Vs], append=True)
                    for g in range(3):
                        if nh[g] > 0:
                            dma("sp", kTh[:, hoff[g] * 128:(hoff[g] + nh[g]) * 128], kTs[g, hs, :, t0[g] * 128:4 * b * 128], r=[d_kTs], w=[kTh], append=(g > 0 and nh[0] + (nh[1] if g > 1 else 0) > 0))
                            dma("sp", Vh[:, hoff[g]:hoff[g] + nh[g], :], Vs[g, hs, t0[g] * 128:4 * b * 128, :].rearrange("(n p) d -> p n d", p=128),
                                r=[d_Vs], w=[Vh], append=(g > 0 and nh[0] + (nh[1] if g > 1 else 0) > 0))
                    _chk(1.6)
                    ob = S.bank_reserve()
                    db = S.bank_reserve()
                    allg = []
                    for qt in range(4):
                        qa = 4 * b + qt
                        pairs = []
                        for tt_ in (qa - 1, qa):
                            if tt_ >= 0:
                                pairs.append((0, tt_, 1 if tt_ == qa - 1 else 0))
                        for tt_ in range(qa - 4, qa + 1):
                            if tt_ >= 0:
                                dl = qa - tt_
                                pairs.append((1, tt_, 2 if dl == 0 else (4 if dl == 4 else 3)))
                        for tt_ in range(0, qa + 1):
                            pairs.append((2, tt_, 5 if tt_ == qa else 6))
                        npair = len(pairs)
                        for i0 in range(0, npair, 4):
                            allg.append((qt, pairs[i0:i0 + 4], i0, npair))

                    def stage_a(qt, grp):
                        sb = S.bank()
                        for i, (g, tt_, m) in enumerate(grp):
                            if tt_ >= 4 * b:
                                klhs, ktk = kTc[:, g, (tt_ - 4 * b) * 128:(tt_ - 4 * b + 1) * 128], kTc
                            else:
                                hh = hoff[g] + tt_ - t0[g]
                                klhs, ktk = kTh[:, hh * 128:(hh + 1) * 128], kTh
                            mm(sb[:, i * 128:(i + 1) * 128], klhs, qT[:, g, qt * 128:(qt + 1) * 128], r=[ktk, qT], w=[sb])
                        Pt = Pts[stage_a.n % 3]
                        stage_a.n += 1
                        n = len(grp) * 128
                        act(Pt[:, :n], sb[:, :n], AF.Exp, r=[sb], w=[Pt], scale=ISQ)
                        for i, (g, tt_, m) in enumerate(grp):
                            V_("tensor_tensor", r=[Pt, amask], w=[Pt], out=Pt[:, i * 128:(i + 1) * 128], in0=Pt[:, i * 128:(i + 1) * 128],
                               in1=amask[:, m, :], op=ALU.mult)
                        return Pt

                    stage_a.n = 0

                    def stage_b(qt, grp, i0, npair, Pt):
                        for i, (g, tt_, m) in enumerate(grp):
                            if tt_ >= 4 * b:
                                vl, vtk = Vc[:, tt_ - 4 * b, g, :], Vc
                            else:
                                vl, vtk = Vh[:, hoff[g] + tt_ - t0[g], :], Vh
                            first = (i0 + i == 0)
                            lastp = (i0 + i == npair - 1)
                            mm(ob[:, qt * 128:(qt + 1) * 128], vl, Pt[:, i * 128:(i + 1) * 128], start=first, stop=lastp, r=[vtk, Pt], w=[ob])
                            mm(db[:, qt * 128:(qt + 1) * 128], ones_b[:], Pt[:, i * 128:(i + 1) * 128], start=first, stop=lastp,
                               r=[ones_b, Pt], w=[db])

                    prev = None
                    for (qt, grp, i0, npair) in allg:
                        Pt = stage_a(qt, grp)
                        if prev is not None:
                            stage_b(*prev)
                        prev = (qt, grp, i0, npair, Pt)
                    stage_b(*prev)
                    _chk(1.8)
                    V_("reciprocal", r=[db], w=[rec], out=rec[:], in_=db[:])
                    V_("tensor_tensor", r=[ob, rec], w=[brT], out=brT[:, 8 + hs, :], in0=ob[:], in1=rec[:], op=ALU.mult)
                    S.bank_release(ob)
                    S.bank_release(db)

        def attn_sample(l, brT):
            wq_ = I["w_in"][l][:, O_QB:O_QB + 1536].rearrange("(kc p) (g h c) -> p kc g h c", p=128, g=3, h=4)
            wk_ = I["w_in"][l][:, O_KB:O_KB + 1536].rearrange("(kc p) (g h c) -> p kc g h c", p=128, g=3, h=4)
            wv_ = I["w_in"][l][:, O_VB:O_VB + 1536].rearrange("(kc p) (g h c) -> p kc g h c", p=128, g=3, h=4)
            with S.scope() as mk:
                qT = mk([128, 3, NSQ], F32, "sq_")
                kTn = mk([128, 3, NSQ], F32, "sk_")
                Kn = mk([128, NS, 3, 128], F32, "sKn")
                Vn = mk([128, NS, 3, 128], F32, "sVn")
                cks = [mk([128, 9, 2, 128], F32, "ck%d" % i) for i in range(2)]
                kTc = mk([128, 9, 128], F32, "skT")
                Pc = mk([128, 48], F32, "sPc")
                Pn = mk([128, 12], F32, "sPn")
                rec = mk([128, NSQ], F32, "srec")
                ci = 0
                for hs in range(4):
                    wbq, (qv,) = wload([wq_[:, :, :, hs, :]], ("aq", l, hs))
                    for g in range(3):
                        pb = proj_fm(lambda kc: qv[:, kc, g, :], NSQ, wbq)
                        act(qT[:, g, :], pb[:, :NSQ], AF.Copy, r=[pb], w=[qT])
                    wbk, (kv,) = wload([wk_[:, :, :, hs, :]], ("ak", l, hs))
                    for g in range(3):
                        pb = proj_fm(lambda kc: kv[:, kc, g, :], NSQ, wbk)
                        act(kTn[:, g, :], pb[:, :NSQ], AF.Copy, r=[pb], w=[kTn])
                    for s in range(NS):
                        pb = S.bank()
                        for kc in range(KC):
                            mm(pb[:4, :384], hT[:, kc, 4 * s:4 * s + 4], kv[:, kc].rearrange("p g c -> p (g c)"), start=(kc == 0),
                               stop=(kc == KC - 1), r=[hT, wbk], w=[pb])
                        act(Kn[:4, s, :, :].rearrange("p g c -> p (g c)"), pb[:4, :384], AF.Copy, r=[pb], w=[Kn])
                    wbv, (vv,) = wload([wv_[:, :, :, hs, :]], ("av", l, hs))
                    for s in range(NS):
                        pb = S.bank()
                        for kc in range(KC):
                            mm(pb[:4, :384], hT[:, kc, 4 * s:4 * s + 4], vv[:, kc].rearrange("p g c -> p (g c)"), start=(kc == 0),
                               stop=(kc == KC - 1), r=[hT, wbv], w=[pb])
                        act(Vn[:4, s, :, :].rearrange("p g c -> p (g c)"), pb[:4, :384], AF.Copy, r=[pb], w=[Vn])
                    for s in range(NS):
                        for g in range(3):
                            w = WINS[g]
                            dma("sp", WS[g][l, s, w - 4:w, hs * 128:(hs + 1) * 128], Kn[:4, s, g, :], r=[Kn], w=[])
                            dma("sp", WS[g][l, s, w - 4:w, 512 + hs * 128:512 + (hs + 1) * 128], Vn[:4, s, g, :], r=[Vn], w=[])
                    ob = S.bank_reserve()
                    db = S.bank_reserve()
                    for s in range(NS):
                        ck = cks[ci % 2]
                        ci += 1
                        c4 = lambda g: CW[g][l, s].rearrange("r (kv h d) -> r kv h d", kv=2, h=4)[:, :, hs, :]
                        dma("sp", ck[:, 0, :, :], c4(0), w=[ck])
                        for r_ in range(4):
                            dma("sp", ck[:, 1 + r_, :, :], c4(1).rearrange("(m q) kv d -> m q kv d", q=4)[:, r_, :, :], w=[ck], append=True)
                            dma("sp", ck[:, 5 + r_, :, :], c4(2).rearrange("(m q) kv d -> m q kv d", q=16)[:, r_, :, :], w=[ck], append=True)
                        for i0 in (0, 4, 8):
                            n = min(4, 9 - i0)
                            pb = S.bank()
                            for i in range(n):
                                tr(pb[:, i * 128:(i + 1) * 128], ck[:, i0 + i, 0, :], ident[:], r=[ck, ident], w=[pb], sig=(i == n - 1))
                            act(kTc[:, i0:i0 + n, :].rearrange("p a d -> p (a d)"), pb[:, :n * 128], AF.Copy, r=[pb], w=[kTc])
                        sb = S.bank()
                        for i in range(9):
                            g = 0 if i == 0 else (1 if i < 5 else 2)
                            mm(sb[:, i * 4:(i + 1) * 4], kTc[:, i, :], qT[:, g, 4 * s:4 * s + 4], r=[kTc, qT], w=[sb])
                        sb2 = S.bank()
                        for g in range(3):
                            mm(sb2[:4, g * 4:(g + 1) * 4], kTn[:, g, 4 * s:4 * s + 4], qT[:, g, 4 * s:4 * s + 4], r=[kTn, qT], w=[sb2])
                        act(Pc[:, :36], sb[:, :36], AF.Exp, r=[sb], w=[Pc], scale=ISQ)
                        act(Pn[:4, :12], sb2[:4, :12], AF.Exp, r=[sb2], w=[Pn], scale=ISQ)
                        V_("tensor_tensor", r=[Pc, smask], w=[Pc], out=Pc[:, :36], in0=Pc[:, :36], in1=smask[:, 0:9, :].rearrange("p a q -> p (a q)"), op=ALU.mult)
                        V_("tensor_tensor", r=[Pn, smask], w=[Pn], out=Pn[:4, :12], in0=Pn[:4, :12], in1=smask[:4, 9:12, :].rearrange("p a q -> p (a q)"), op=ALU.mult)
                        oc = slice(4 * s, 4 * s + 4)
                        for i in range(9):
                            mm(ob[:, oc], ck[:, i, 1, :], Pc[:, i * 4:(i + 1) * 4], start=(i == 0), stop=False, r=[ck, Pc], w=[ob])
                        for g in range(3):
                            mm(ob[:, oc], Vn[:4, s, g, :], Pn[:4, g * 4:(g + 1) * 4], start=False, stop=(g == 2), r=[Vn, Pn], w=[ob])
                        for i in range(9):
                            mm(db[:, oc], ones_f[:], Pc[:, i * 4:(i + 1) * 4], start=(i == 0), stop=False, r=[ones_f, Pc], w=[db])
                        for g in range(3):
                            mm(db[:, oc], ones_f[:4, :], Pn[:4, g * 4:(g + 1) * 4], start=False, stop=(g == 2), r=[ones_f, Pn], w=[db])
                    V_("reciprocal", r=[db], w=[rec], out=rec[:, :NSQ], in_=db[:, :NSQ])
                    V_("tensor_tensor", r=[ob, rec], w=[brT], out=brT[:, 8 + hs, :NSQ], in0=ob[:, :NSQ], in1=rec[:, :NSQ], op=ALU.mult)
                    S.bank_release(ob)
                    S.bank_release(db)

        def pool_branch(l, b, prompt, brT):
            nseq = 1 if prompt else NS
            Tq = 512 if prompt else 4
            ntok = nseq * Tq
            Lx = 15 + Tq
            with S.scope() as mk:
                phist = None
                if not prompt:
                    praw = mk([128, 1024], F32, "praw")
                    dma("sp", praw[:NS * 15, :], I["spool"][l], w=[praw])
                    phist = mk([128, 8, NS, 15], F32, "phist")
                    pb = S.bank()
                    for t_ in range(8):
                        tr(pb[:, t_ * NS * 15:(t_ + 1) * NS * 15], praw[:NS * 15, t_ * 128:(t_ + 1) * 128], ident[:NS * 15, :NS * 15],
                           r=[praw, ident], w=[pb], sig=(t_ == 7))
                    act(phist[:].rearrange("p t s j -> p (t s j)"), pb[:, :8 * NS * 15], AF.Copy, r=[pb], w=[phist])
                pbuf = mk([128, nseq, Lx], F32, "pbuf")
                Pa = mk([128, nseq, Lx], F32, "Pa")
                Pb = mk([128, nseq, Lx], F32, "Pb")
                pooledT = mk([128, 2, ntok], BF16, "pooledT")
                for half in range(2):
                    wb, (wv,) = wload([win_cols(l, O_UC + 512 * half, 512)], ("uc", l, half))
                    for j in range(4):
                        ct = half * 4 + j
                        gi = ct // 2
                        win = 2 << gi
                        pb = proj_fm(lambda kc: wv[:, kc, j * 128:(j + 1) * 128], ntok, wb)
                        act(pbuf[:, :, 15:Lx], pb[:, :ntok].rearrange("p (s t) -> p s t", s=nseq), AF.Copy, r=[pb], w=[pbuf])
                        if prompt:
                            V_("tensor_copy", r=[ptail], w=[pbuf], out=pbuf[:, 0, 0:15], in_=ptail[:, ct, :])
                        else:
                            V_("tensor_copy", r=[phist], w=[pbuf], out=pbuf[:, :, 0:15], in_=phist[:, ct, :, :])
                        srcb, sh = pbuf, 1
                        res = None
                        for lv in range(gi + 1):
                            dst = Pa if lv % 2 == 0 else Pb
                            lo = 2 * sh - 1
                            V_("tensor_tensor", r=[srcb], w=[dst], out=dst[:, :, lo:Lx], in0=srcb[:, :, lo:Lx], in1=srcb[:, :, lo - sh:Lx - sh], op=ALU.add)
                            srcb, sh, res = dst, sh * 2, dst
                        if prompt and b == 0:
                            V_("tensor_tensor", r=[res, pcorr], w=[res], out=res[:, 0, 15:31], in0=res[:, 0, 15:31], in1=pcorr[:, gi, :], op=ALU.mult)
                        V_("scalar_tensor_tensor", r=[res, pbuf], w=[pooledT], out=pooledT[:, ct % 2, :ntok].rearrange("p (s t) -> p s t", s=nseq),
                           in0=res[:, :, 15:Lx], scalar=1.0 / win, in1=pbuf[:, :, 15:Lx], op0=ALU.mult, op1=ALU.subtract)
                        if prompt:
                            V_("tensor_copy", r=[pbuf], w=[ptail], out=ptail[:, ct, :], in_=pbuf[:, 0, Tq:Tq + 15])
                        if ct % 2 == 1:
                            for ot in range(2):
                                pb2 = S.bank()
                                for kt in range(2):
                                    mm(pb2[:, :ntok], wpool[:, gi, kt, ot * 128:(ot + 1) * 128], pooledT[:, kt, :ntok], start=(kt == 0), stop=(kt == 1),
                                       r=[wpool, pooledT], w=[pb2])
                                V_("tensor_scalar", r=[pb2, pscale], w=[brT], out=brT[:, 12 + 2 * gi + ot, :ntok], in0=pb2[:, :ntok],
                                   scalar1=pscale[:, 2 * gi + ot:2 * gi + ot + 1], scalar2=None, op0=ALU.mult)

        def tails(l, prompt):
            t0 = 496 if prompt else 0
            with S.scope() as mk:
                sts = [mk([16, 512], F32, "tst%d" % i) for i in range(2)]
                for ci_ in range(8):
                    c0 = ci_ * 512 if ci_ < 6 else O_UC + (ci_ - 6) * 512
                    wb, (wv,) = wload([win_cols(l, c0, 512)], ("tail", l, ci_))
                    pb = S.bank()
                    for kc in range(KC):
                        mm(pb[:16, :], hT[:, kc, t0:t0 + 16], wv[:, kc, :], start=(kc == 0), stop=(kc == KC - 1), r=[hT, wb], w=[pb])
                    st = sts[ci_ % 2]
                    act(st[:], pb[:16, :], AF.Copy, r=[pb], w=[st])
                    if prompt:
                        if ci_ < 6:
                            dma("sp", O["convp"][l, :, ci_ * 512:(ci_ + 1) * 512], st[13:16, :], r=[st], w=[])
                        else:
                            dma("sp", O["poolp"][l, :, (ci_ - 6) * 512:(ci_ - 5) * 512], st[1:16, :], r=[st], w=[])
                    else:
                        for s in range(NS):
                            if ci_ < 6:
                                dma("sp", O["convs"][l, s, :, ci_ * 512:(ci_ + 1) * 512], st[4 * s + 1:4 * s + 4, :], r=[st], w=[])
                            else:
                                dma("sp", O["pools"][l, s, 11:15, (ci_ - 6) * 512:(ci_ - 5) * 512], st[4 * s:4 * s + 4, :], r=[st], w=[])

        def emit_block(l, b, prompt):
            ntok = 512 if prompt else NSQ
            nt = 4 if prompt else 1
            rows = 128 if prompt else NSQ
            last = prompt and (b == NB - 1)
            if l == 0:
                src = I["xp"] if prompt else I["xs"]
                with S.scope() as mk:
                    xins = [mk([128, D], F32, "xin%d" % i) for i in range(2)]
                    for tt in range(nt):
                        xin = xins[tt % 2]
                        r0 = b * 512 + tt * 128 if prompt else 0
                        dma("sp", xin[:rows, :], src[r0:r0 + rows, :], w=[xin])
                        for k4 in range(4):
                            pb = S.bank()
                            for j in range(4):
                                kc = k4 * 4 + j
                                tr(pb[:, j * 128:j * 128 + rows], xin[:rows, kc * 128:(kc + 1) * 128], ident[:rows, :rows],
                                   r=[xin, ident], w=[pb], sig=(j == 3))
                            act(xT[:, k4 * 4:(k4 + 1) * 4, tt * 128:tt * 128 + rows], pb[:].rearrange("p (j t) -> p j t", j=4)[:, :, :rows],
                                AF.Copy, r=[pb], w=[xT])
            else:
                if prompt:
                    dma("sp", xT[:], x1T[:, :, b * 512:(b + 1) * 512].rearrange("kc p t -> p kc t"), r=[d_x1T], w=[xT])
                else:
                    dma("sp", xT[:, :, :ntok], xs1T.rearrange("kc p t -> p kc t"), r=[d_xs1T], w=[xT])
            _chk(0.7)
            norm_to_hT(gpre, ntok)
            _chk(1)

            with S.scope() as mkA2:
                brT = mkA2([128, 20, ntok], BF16, "brT")
                if prompt:
                    attn_prompt(l, b, brT)
                else:
                    attn_sample(l, brT)
                _chk(2)
                gdn(l, b, prompt, brT)
                _chk(3)
                pool_branch(l, b, prompt, brT)
                _chk(4)
                if last or not prompt:
                    tails(l, prompt)
                _chk(5)
                mergedT = mkA2([128, KC, ntok], BF16, "mergedT")
                with S.scope() as mk:
                    sg = [mk([128, 512], F32, "sg%d" % i) for i in range(3)]
                    acc = mk([128, 512], F32, "acc")
                    t1 = mk([128, 512], F32, "t1")
                    for dg in range(16):
                        wb, gv = wload([win_cols(l, O_GA + 2048 * i + 128 * dg, 128) for i in range(3)], ("gate", l, dg))
                        wb2, bv = wload([rows_cols(I["w_br_a"][l], dg * 128, 128), rows_cols(I["w_br_b"][l], dg * 128, 128),
                                         rows_cols(I["w_br_c"][l], dg * 128, 128)], ("br", l, dg))
                        for j in range(1):
                            dt_ = dg
                            gps = [proj_fm(lambda kc, i=i: gv[i][:, kc, j * 128:(j + 1) * 128], ntok, wb) for i in range(3)]
                            bps = []
                            for i, (nk, k0) in enumerate(((8, 0), (4, 8), (8, 12))):
                                pb = S.bank()
                                for kk in range(nk):
                                    mm(pb[:, :ntok], bv[i][:, kk, j * 128:(j + 1) * 128], brT[:, k0 + kk, :ntok], start=(kk == 0), stop=(kk == nk - 1),
                                       r=[wb2, brT], w=[pb])
                                bps.append(pb)
                            for i in range(3):
                                act(sg[i][:, :ntok], gps[i][:, :ntok], AF.Sigmoid, r=[gps[i]], w=[sg[i]])
                            V_("tensor_tensor", r=[sg[0], bps[0]], w=[acc], out=acc[:, :ntok], in0=sg[0][:, :ntok], in1=bps[0][:, :ntok], op=ALU.mult)
                            V_("tensor_tensor", r=[sg[1], bps[1]], w=[t1], out=t1[:, :ntok], in0=sg[1][:, :ntok], in1=bps[1][:, :ntok], op=ALU.mult)
                            V_("tensor_tensor", r=[acc, t1], w=[acc], out=acc[:, :ntok], in0=acc[:, :ntok], in1=t1[:, :ntok], op=ALU.add)
                            V_("tensor_tensor", r=[sg[2], bps[2]], w=[t1], out=t1[:, :ntok], in0=sg[2][:, :ntok], in1=bps[2][:, :ntok], op=ALU.mult)
                            V_("tensor_tensor", r=[acc, t1], w=[mergedT], out=mergedT[:, dt_, :ntok], in0=acc[:, :ntok], in1=t1[:, :ntok], op=ALU.add)
                if l == 0 and b == 0 and prompt:
                    for i_ in range(20):
                        dump(i_, brT[:, i_, :], brT)
                    for i_ in range(4):
                        dump(20 + i_, mergedT[:, i_, :], mergedT)
                _chk(6)
                with S.scope() as mk:
                    ytmp = mk([128, KC, ntok], F32, "ytmp")
                    sqs = [mk([128, 512], F32, "sqo%d" % i) for i in range(2)]
                    ssb = S.bank_reserve()
                    for cg in range(4):
                        wb, (wv,) = wload([rows_cols(I["w_out"][l], cg * 512, 512)], ("out", l, cg))
                        for j in range(4):
                            dt_ = cg * 4 + j
                            pb = S.bank()
                            for kc in range(KC):
                                mm(pb[:, :ntok], wv[:, kc, j * 128:(j + 1) * 128], mergedT[:, kc, :ntok], start=(kc == 0), stop=(kc == KC - 1),
                                   r=[wb, mergedT], w=[pb])
                            V_("tensor_copy", r=[pb], w=[ytmp], out=ytmp[:, dt_, :ntok], in_=pb[:, :ntok])
                            sq = sqs[dt_ % 2]
                            act(sq[:, :ntok], pb[:, :ntok], AF.Square, r=[pb], w=[sq])
                            mm(ssb[:, :ntok], ones_f[:], sq[:, :ntok], start=(dt_ == 0), stop=(dt_ == KC - 1), r=[ones_f, sq], w=[ssb], sig=True)
                    rstd = mk([128, 512], F32, "rstdo")
                    V_("tensor_scalar", r=[ssb], w=[rstd], out=rstd[:, :ntok], in0=ssb[:, :ntok], scalar1=1.0 / D, scalar2=EPS, op0=ALU.mult, op1=ALU.add)
                    act(rstd[:, :ntok], rstd[:, :ntok], AF.Ln, r=[rstd], w=[rstd])
                    act(rstd[:, :ntok], rstd[:, :ntok], AF.Exp, r=[rstd], w=[rstd], scale=-0.5)
                    S.bank_release(ssb)
                    for dt_ in range(KC):
                        V_("scalar_tensor_tensor", r=[ytmp, gpost, rstd], w=[ytmp], out=ytmp[:, dt_, :ntok], in0=ytmp[:, dt_, :ntok],
                           scalar=gpost[:, dt_:dt_ + 1], in1=rstd[:, :ntok], op0=ALU.mult, op1=ALU.mult)
                        V_("tensor_tensor", r=[ytmp, xT], w=[xT], out=xT[:, dt_, :ntok], in0=xT[:, dt_, :ntok], in1=ytmp[:, dt_, :ntok], op=ALU.add)

            if l == 0 and b == 0 and prompt:
                for i_ in range(4):
                    dump(24 + i_, xT[:, i_, :], xT)
            _chk(7)
            norm_to_hT(gpre2, ntok)
            with S.scope() as mk:
                actT = mk([128, FT, ntok], BF16, "actT")
                ytmp = mk([128, KC, ntok], F32, "ytmp2")
                sgs = [mk([128, 512], F32, "sgf%d" % i) for i in range(2)]
                sqs = [mk([128, 512], F32, "sqf%d" % i) for i in range(2)]
                wgu = I["w_gu"][l].rearrange("(kc p) (u n) -> p kc u n", p=128, u=2)
                for fg in range(FT // 2):
                    wb, (wv,) = wload([wgu[:, :, :, fg * 256:(fg + 1) * 256]], ("gu", l, fg))
                    for j in range(2):
                        ft = fg * 2 + j
                        gpb = proj_fm(lambda kc: wv[:, kc, 0, j * 128:(j + 1) * 128], ntok, wb)
                        upb = proj_fm(lambda kc: wv[:, kc, 1, j * 128:(j + 1) * 128], ntok, wb)
                        sg_ = sgs[ft % 2]
                        act(sg_[:, :ntok], gpb[:, :ntok], AF.Silu, r=[gpb], w=[sg_])
                        V_("tensor_tensor", r=[sg_, upb], w=[actT], out=actT[:, ft, :ntok], in0=sg_[:, :ntok], in1=upb[:, :ntok], op=ALU.mult)
                ssb = S.bank_reserve()
                wdn = I["w_down"][l].rearrange("(kc p) n -> p kc n", p=128)
                for dg in range(8):
                    pbs = [S.bank_reserve() for _ in range(2)]
                    for kh in range(2):
                        wb, (wv,) = wload([wdn[:, kh * 22:(kh + 1) * 22, dg * 256:(dg + 1) * 256]], ("dn", l, dg, kh))
                        for j in range(2):
                            for kk in range(22):
                                fk = kh * 22 + kk
                                mm(pbs[j][:, :ntok], wv[:, kk, j * 128:(j + 1) * 128], actT[:, fk, :ntok], start=(fk == 0), stop=(fk == FT - 1),
                                   r=[wb, actT], w=[pbs[j]])
                    for j in range(2):
                        dt_ = dg * 2 + j
                        pb = pbs[j]
                        V_("tensor_copy", r=[pb], w=[ytmp], out=ytmp[:, dt_, :ntok], in_=pb[:, :ntok])
                        sq = sqs[dt_ % 2]
                        act(sq[:, :ntok], pb[:, :ntok], AF.Square, r=[pb], w=[sq])
                        mm(ssb[:, :ntok], ones_f[:], sq[:, :ntok], start=(dt_ == 0), stop=(dt_ == KC - 1), r=[ones_f, sq], w=[ssb], sig=True)
                        S.bank_release(pb)
                rstd = mk([128, 512], F32, "rstdf")
                V_("tensor_scalar", r=[ssb], w=[rstd], out=rstd[:, :ntok], in0=ssb[:, :ntok], scalar1=1.0 / D, scalar2=EPS, op0=ALU.mult, op1=ALU.add)
                act(rstd[:, :ntok], rstd[:, :ntok], AF.Ln, r=[rstd], w=[rstd])
                act(rstd[:, :ntok], rstd[:, :ntok], AF.Exp, r=[rstd], w=[rstd], scale=-0.5)
                S.bank_release(ssb)
                for dt_ in range(KC):
                    V_("scalar_tensor_tensor", r=[ytmp, gpost2, rstd], w=[ytmp], out=ytmp[:, dt_, :ntok], in0=ytmp[:, dt_, :ntok],
                       scalar=gpost2[:, dt_:dt_ + 1], in1=rstd[:, :ntok], op0=ALU.mult, op1=ALU.mult)
                    V_("tensor_tensor", r=[ytmp, xT], w=[xT], out=xT[:, dt_, :ntok], in0=xT[:, dt_, :ntok], in1=ytmp[:, dt_, :ntok], op=ALU.add)

            if l == 0 and b == 0 and prompt:
                for i_ in range(4):
                    dump(28 + i_, xT[:, i_, :], xT)
            _chk(8)
            if l < L - 1:
                if prompt:
                    dma("sp", x1T[:, :, b * 512:(b + 1) * 512].rearrange("kc p t -> p kc t"), xT[:], r=[xT], w=[d_x1T])
                else:
                    dma("sp", xs1T.rearrange("kc p t -> p kc t"), xT[:, :, :ntok], r=[xT], w=[d_xs1T])
            else:
                dst = O["yp"] if prompt else O["ys"]
                with S.scope() as mk:
                    ysts = [mk([128, D], F32, "yst%d" % i) for i in range(2)]
                    for tt in range(nt):
                        yst = ysts[tt % 2]
                        for k4 in range(4):
                            pb = S.bank()
                            for j in range(4):
                                kc = k4 * 4 + j
                                tr(pb[:rows, j * 128:(j + 1) * 128], xT[:, kc, tt * 128:tt * 128 + rows], ident[:], r=[xT, ident], w=[pb], sig=(j == 3))
                            act(yst[:rows, k4 * 512:(k4 + 1) * 512], pb[:rows, :], AF.Copy, r=[pb], w=[yst])
                        r0 = b * 512 + tt * 128 if prompt else 0
                        dma("sp", dst[r0:r0 + rows, :], yst[:rows, :], r=[yst], w=[])

        try:
            _chk(0)
            for l in range(L):
                load_layer_params(l)
                _chk(0.3)
                for tl in (ctail, ptail):
                    G_("memset", w=[tl], ap=tl[:], constant=0.0)
                G_("memset", w=[sstate] + sstate_t, ap=sstate[:], constant=0.0)
                _chk(0.5)
                for b in range(NB):
                    emit_block(l, b, True)
                    emit_cache_copies(8)
                emit_block(l, 0, False)
                emit_cache_copies(8)
        except _Stop:
            pass
        emit_cache_copies(10 ** 6)
        S.finish()
        build.ninst = S.ninst
    return nc


def _consts():
    i = np.arange(128)
    ident = np.eye(128, dtype=np.float32)
    umat = (i[:, None] <= i[None, :]).astype(np.float32)
    mS = np.where(i[:, None] > i[None, :], 0.0, NEG).astype(np.float32)
    mI = np.where(i[None, :] >= i[:, None], 0.0, NEG).astype(np.float32)
    rep4 = lambda a: np.ascontiguousarray(np.broadcast_to(a[:, None, :], (128, 4, 128)))
    k = i[:, None]
    q = i[None, :]
    am = np.zeros((128, 7, 128), np.float32)
    am[:, 0] = (q >= k)
    am[:, 1] = (k >= q)
    am[:, 2] = (q >= k) & ((q - k) % 4 == 0)
    am[:, 3] = ((q - k) % 4 == 0)
    am[:, 4] = (k >= q) & ((q - k) % 4 == 0)
    am[:, 5] = (q >= k) & ((q - k) % 16 == 0)
    am[:, 6] = ((q - k) % 16 == 0)
    pc = np.ones((128, 4, 16), np.float32)
    for gi, win in enumerate((2, 4, 8, 16)):
        t = np.arange(16)
        pc[:, gi, :] = (win / np.minimum(win, t + 1))[None, :]
    sm = np.zeros((128, 12, 4), np.float32)
    t = np.arange(4)[None, :]
    sm[:, 0, :] = (i[:, None] >= t)
    for r in range(4):
        sm[:, 1 + r, :] = (t == r)
        sm[:, 5 + r, :] = (t == r)
    rr = np.arange(4)[:, None]
    sm[:4, 9, :] = (rr <= t)
    sm[:4, 10, :] = (rr == t)
    sm[:4, 11, :] = (rr == t)
    return dict(ident=ident, umat=umat, maskS4=rep4(mS), maskI4=rep4(mI), ident4=rep4(ident), amask=am, pcorr=pc, smask=sm)


_CACHE = {}


def run(inp, T, NS, ncores, prompt_of_core, sample_of_core):
    L = inp["w_in"].shape[0]
    key = (T, NS, L)
    if key not in _CACHE:
        _CACHE[key] = build(T, NS, L)
    nc = _CACHE[key]
    f = lambda a: np.ascontiguousarray(np.asarray(a, dtype=np.float32))
    shared = {k: f(inp[k]) for k in ("w_in", "w_pool", "w_br_a", "w_br_b", "w_br_c", "w_out", "w_gu", "w_down")}
    shared["convw"] = f(np.asarray(inp["conv_w"]).reshape(L, 4, 24, 128).transpose(0, 3, 2, 1))
    shared["alog"] = f(np.broadcast_to(np.asarray(inp["a_log"])[:, None, None, :], (L, 128, 4, 8)))
    shared["dtb"] = f(np.broadcast_to(np.asarray(inp["dt_bias"])[:, None, None, :], (L, 128, 4, 8)))
    shared["ggain"] = f(np.asarray(inp["gdn_gain"]).reshape(L, 128, 1))
    shared["pscale"] = f(np.asarray(inp["pool_scale"]).reshape(L, 8, 128).transpose(0, 2, 1))
    for nm, src in (("gpre", "g_pre_mix"), ("gpost", "g_post_mix"), ("gpre2", "g_pre_ffn"), ("gpost2", "g_post_ffn")):
        shared[nm] = f(np.asarray(inp[src]).reshape(L, 16, 128).transpose(0, 2, 1))
    shared.update(_consts())
    in_maps = []
    for c in range(ncores):
        pb = prompt_of_core[c]
        s0 = sample_of_core[c]
        m = dict(shared)
        m["xp"] = f(inp["x_prompt"][pb])
        m["xs"] = f(np.asarray(inp["x_sample"])[s0:s0 + NS].reshape(NS * 4, D))
        m["cw1"] = f(np.asarray(inp["cache_win1"])[:, s0:s0 + NS].reshape(L, NS, 128, 1024))
        m["cw2"] = f(np.asarray(inp["cache_win2"])[:, s0:s0 + NS].reshape(L, NS, 512, 1024))
        m["cw3"] = f(np.asarray(inp["cache_win3"])[:, s0:s0 + NS].reshape(L, NS, 2048, 1024))
        m["sgdn"] = f(np.asarray(inp["state_gdn"])[:, s0:s0 + NS])
        m["sconv"] = f(np.asarray(inp["state_conv"])[:, s0:s0 + NS].reshape(L, NS * 3, 3072))
        m["spool"] = f(np.asarray(inp["state_pool"])[:, s0:s0 + NS].reshape(L, NS * 15, 1024))
        in_maps.append(m)
    res = run_bass_kernel_spmd(nc, in_maps, core_ids=list(range(ncores)))
    return res.results


def kernel(**inputs):
    T, NS, NCORE = 2048, 4, 8
    L = 2
    res = run(inputs, T, NS, NCORE, [c % 4 for c in range(NCORE)], [4 * c for c in range(NCORE)])
    B = 4
    yp = np.stack([res[b]["yp"] for b in range(B)], 0)
    ys = np.concatenate([res[c]["ys"].reshape(NS, 4, D) for c in range(NCORE)], 0)

    def pst(name, shp):
        return np.stack([res[b][name].reshape(shp) for b in range(B)], 1)

    def sst(name, shp):
        return np.concatenate([res[c][name].reshape((L, NS) + shp) for c in range(NCORE)], 1)

    outs = (yp, ys,
            pst("w1p", (L, 128, 2, 4, 128)), pst("w2p", (L, 512, 2, 4, 128)), pst("w3p", (L, 2048, 2, 4, 128)),
            pst("gdnp", (L, 8, 128, 128)), pst("convp", (L, 3, 3072)), pst("poolp", (L, 15, 1024)),
            sst("w1s", (128, 2, 4, 128)), sst("w2s", (512, 2, 4, 128)), sst("w3s", (2048, 2, 4, 128)),
            sst("gdns", (8, 128, 128)), sst("convs", (3, 3072)), sst("pools", (15, 1024)))
    return tuple(np.ascontiguousarray(o, dtype=np.float32) for o in outs)
```
